# Optimizing a Trainium2 kernel written in Bass

```python
import jax, jax.numpy as jnp
from jax import lax
import numpy as np

D_MODEL = 1024
BATCH = 32
SEQ = 256
DEPTH = 1
DEC_BATCH = 8
DEC_SEQ = 4096
PAST_LEN = 256

GRID_W = 64
MIX_WIDTH = D_MODEL
LRU_WIDTH = MIX_WIDTH // 2
LRU_BLOCKS = 8
LRU_BLOCK_W = LRU_WIDTH // LRU_BLOCKS
LRU_CONV_WIDTH = 4
LRU_CONV_PAD_LEFT = 2
LRU_C = 8.0
RET_WIDTH = MIX_WIDTH - LRU_WIDTH
RET_HEADS = 4
RET_HEAD_DIM = RET_WIDTH // RET_HEADS
RET_CHUNK = 128
IN_WIDTH = 2 * LRU_WIDTH + 4 * RET_WIDTH
FFN_HIDDEN = ((8 * D_MODEL // 3 + 127) // 128) * 128
FFN_CONV_WIDTH = 3
N_MOD = 6
EPS = 1e-6

kernel_name = 'hymba_rglru_retention_convffn_dit_step'


def _rmsnorm(x, w):
    x32 = x.astype(jnp.float32)
    y = x32 * lax.rsqrt(jnp.mean(x32 * x32, axis=-1, keepdims=True) + EPS)
    return (y * w.astype(jnp.float32)).astype(x.dtype)


def _group_norm_heads(o):
    mu = jnp.mean(o, axis=-1, keepdims=True)
    var = jnp.mean(jnp.square(o - mu), axis=-1, keepdims=True)
    return (o - mu) * lax.rsqrt(var + EPS)


def _modulation(cvec, w, b):
    return (jax.nn.silu(cvec) @ w + b)[:, None, :]


def _dwconv1d(x, w, b, pad_left):
    width = w.shape[0]
    t = x.shape[1]
    xp = jnp.pad(x, ((0, 0), (pad_left, width - 1 - pad_left), (0, 0)))
    return sum(xp[:, j:j + t] * w[j] for j in range(width)) + b


def _dwconv2d_grid(x, w, b):
    bsz, t, ch = x.shape
    rows = t // GRID_W
    g = x.reshape(bsz, rows, GRID_W, ch)
    y = lax.conv_general_dilated(g, w[:, :, None, :], (1, 1), 'SAME',
                                 dimension_numbers=('NHWC', 'HWIO', 'NHWC'),
                                 feature_group_count=ch)
    return y.reshape(bsz, t, ch) + b


def _linear_combine(e1, e2):
    a1, b1 = e1
    a2, b2 = e2
    return a1 * a2, a2 * b1 + b2


def _rglru_dir(x, w_a, b_a, w_x, b_x, lam, h0, reverse):
    bsz, t, wdt = x.shape
    xb = x.reshape(bsz, t, LRU_BLOCKS, LRU_BLOCK_W)
    r = jax.nn.sigmoid(jnp.einsum('btnc,ncd->btnd', xb, w_a).reshape(bsz, t, wdt) + b_a)
    i = jax.nn.sigmoid(jnp.einsum('btnc,ncd->btnd', xb, w_x).reshape(bsz, t, wdt) + b_x)
    log_a = -LRU_C * r * jax.nn.softplus(-lam.astype(jnp.float32))
    a = jnp.exp(log_a)
    u = jnp.sqrt(-jnp.expm1(2.0 * log_a)) * (i * x)
    edge = -1 if reverse else 0
    u = u.at[:, edge].add(a[:, edge] * h0)
    _, h = lax.associative_scan(_linear_combine, (a, u), axis=1, reverse=reverse)
    return h, h[:, edge]


def _retention_dir(q, k, v, log_gamma, s0):
    bsz, t, nh, dh = q.shape
    n = t // RET_CHUNK
    q = q.reshape(bsz, n, RET_CHUNK, nh, dh)
    k = k.reshape(bsz, n, RET_CHUNK, nh, dh)
    v = v.reshape(bsz, n, RET_CHUNK, nh, dh)
    pos = jnp.arange(RET_CHUNK, dtype=jnp.float32)
    rel = pos[:, None] - pos[None, :]
    lg = log_gamma[:, None, None]
    intra_decay = jnp.where(rel >= 0, jnp.exp(lg * jnp.maximum(rel, 0.0)), 0.0)
    scores = jnp.einsum('bnihd,bnjhd->bnhij', q, k) * intra_decay
    intra = jnp.einsum('bnhij,bnjhe->bnihe', scores, v)
    tail = jnp.exp(log_gamma[:, None] * (RET_CHUNK - 1 - pos))
    head = jnp.exp(log_gamma[:, None] * (pos + 1.0))
    chunk_kv = jnp.einsum('bnjhd,bnjhe,hj->nbhde', k, v, tail)
    g_chunk = jnp.exp(log_gamma * RET_CHUNK)[:, None, None]

    def step(s, kv):
        return g_chunk * s + kv, s

    s_final, s_before = lax.scan(step, s0, chunk_kv)
    cross = jnp.einsum('bnihd,nbhde,hi->bnihe', q, s_before, head)
    return (intra + cross).reshape(bsz, t, nh, dh), s_final


def _retention_bidir(q, k, v, lg_fw, lg_bw, s_fw, s_bw):
    o_fw, sf = _retention_dir(q, k, v, lg_fw, s_fw)
    o_bw, sb = _retention_dir(q[:, ::-1], k[:, ::-1], v[:, ::-1], lg_bw, s_bw)
    return o_fw + o_bw[:, ::-1], sf, sb


def _layer(x, mod, p, st, latent):
    bsz, t, _ = x.shape
    shift1, scale1, gate1, shift2, scale2, gate2 = jnp.split(mod, N_MOD, axis=-1)
    h = _rmsnorm(x, p['norm1_w']) * (1.0 + scale1) + shift1
    proj = h @ p['w_in']
    L, R = LRU_WIDTH, RET_WIDTH
    lru_x, lru_g, q, k, v, ret_g = jnp.split(
        proj, [L, 2 * L, 2 * L + R, 2 * L + 2 * R, 2 * L + 3 * R], axis=-1)
    xc = _dwconv1d(lru_x, p['lru_conv_w'], p['lru_conv_b'], LRU_CONV_PAD_LEFT).astype(jnp.float32)
    h_fw, s_lru_fw = _rglru_dir(xc, p['lru_wa_fw'], p['lru_ba_fw'], p['lru_wx_fw'], p['lru_bx_fw'],
                                p['lru_lambda_fw'], st[0], False)
    h_bw, s_lru_bw = _rglru_dir(xc, p['lru_wa_bw'], p['lru_ba_bw'], p['lru_wx_bw'], p['lru_bx_bw'],
                                p['lru_lambda_bw'], st[1], True)
    y_lru = (h_fw + h_bw).astype(x.dtype) * jax.nn.gelu(lru_g)
    qh = q.reshape(bsz, t, RET_HEADS, RET_HEAD_DIM).astype(jnp.float32)
    kh = k.reshape(bsz, t, RET_HEADS, RET_HEAD_DIM).astype(jnp.float32) * (RET_HEAD_DIM ** -0.5)
    vh = v.reshape(bsz, t, RET_HEADS, RET_HEAD_DIM).astype(jnp.float32)
    lg_fw = jax.nn.log_sigmoid(p['ret_decay_fw'].astype(jnp.float32))
    lg_bw = jax.nn.log_sigmoid(p['ret_decay_bw'].astype(jnp.float32))
    o, s_ret_fw, s_ret_bw = _retention_bidir(qh, kh, vh, lg_fw, lg_bw, st[2], st[3])
    o = _group_norm_heads(o).reshape(bsz, t, RET_WIDTH) * p['ret_gn_w'].astype(jnp.float32)
    y_ret = o.astype(x.dtype) * jax.nn.silu(ret_g)
    x = x + gate1 * (jnp.concatenate([y_lru, y_ret], axis=-1) @ p['w_out'])
    h = _rmsnorm(x, p['norm2_w']) * (1.0 + scale2) + shift2
    g = h @ p['ffn_w_gate']
    if latent:
        g = _dwconv2d_grid(g, p['ffn_conv_w'], p['ffn_conv_b'])
    else:
        g = _dwconv1d(g, p['ffn_conv_w'][1], p['ffn_conv_b'], 1)
    x = x + gate2 * ((jax.nn.gelu(g) * (h @ p['ffn_w_up'])) @ p['ffn_w_down'])
    finals = (s_lru_fw.astype(x.dtype), s_lru_bw.astype(x.dtype),
              s_ret_fw.astype(x.dtype), s_ret_bw.astype(x.dtype))
    return x, finals


def setup_inputs(seed: int = 0) -> dict:
    key = jax.random.key(seed)
    ks = iter(jax.random.split(key, 48))

    def nrm(shape, scale):
        return scale * jax.random.normal(next(ks), shape, jnp.float32)

    def lru_lambda():
        a = jax.random.uniform(next(ks), (DEPTH, LRU_WIDTH), jnp.float32, 0.9, 0.999)
        pr = a ** (1.0 / LRU_C)
        return jnp.log(pr) - jnp.log1p(-pr)

    ret_base = jnp.log(2.0 ** (5.0 + jnp.arange(RET_HEADS, dtype=jnp.float32)) - 1.0)
    return {
        'x_prompt': nrm((BATCH, SEQ, D_MODEL), 1.0),
        'x_sample': nrm((DEC_BATCH, DEC_SEQ, D_MODEL), 1.0),
        'state_lru_fw': nrm((DEC_BATCH, DEPTH, LRU_WIDTH), 0.5),
        'state_lru_bw': nrm((DEC_BATCH, DEPTH, LRU_WIDTH), 0.5),
        'state_ret_fw': nrm((DEC_BATCH, DEPTH, RET_HEADS, RET_HEAD_DIM, RET_HEAD_DIM), 0.3),
        'state_ret_bw': nrm((DEC_BATCH, DEPTH, RET_HEADS, RET_HEAD_DIM, RET_HEAD_DIM), 0.3),
        'c': nrm((DEC_BATCH, D_MODEL), 1.0),
        'c_ctx': nrm((D_MODEL,), 1.0),
        'norm1_w': 1.0 + nrm((DEPTH, D_MODEL), 0.02),
        'w_mod': nrm((DEPTH, D_MODEL, N_MOD * D_MODEL), 0.5 * D_MODEL ** -0.5),
        'b_mod': nrm((DEPTH, N_MOD * D_MODEL), 0.02),
        'w_in': nrm((DEPTH, D_MODEL, IN_WIDTH), D_MODEL ** -0.5),
        'lru_conv_w': nrm((DEPTH, LRU_CONV_WIDTH, LRU_WIDTH), LRU_CONV_WIDTH ** -0.5),
        'lru_conv_b': nrm((DEPTH, LRU_WIDTH), 0.02),
        'lru_wa_fw': nrm((DEPTH, LRU_BLOCKS, LRU_BLOCK_W, LRU_BLOCK_W), LRU_BLOCK_W ** -0.5),
        'lru_ba_fw': nrm((DEPTH, LRU_WIDTH), 0.02),
        'lru_wx_fw': nrm((DEPTH, LRU_BLOCKS, LRU_BLOCK_W, LRU_BLOCK_W), LRU_BLOCK_W ** -0.5),
        'lru_bx_fw': nrm((DEPTH, LRU_WIDTH), 0.02),
        'lru_lambda_fw': lru_lambda(),
        'lru_wa_bw': nrm((DEPTH, LRU_BLOCKS, LRU_BLOCK_W, LRU_BLOCK_W), LRU_BLOCK_W ** -0.5),
        'lru_ba_bw': nrm((DEPTH, LRU_WIDTH), 0.02),
        'lru_wx_bw': nrm((DEPTH, LRU_BLOCKS, LRU_BLOCK_W, LRU_BLOCK_W), LRU_BLOCK_W ** -0.5),
        'lru_bx_bw': nrm((DEPTH, LRU_WIDTH), 0.02),
        'lru_lambda_bw': lru_lambda(),
        'ret_decay_fw': ret_base[None, :] + nrm((DEPTH, RET_HEADS), 0.1),
        'ret_decay_bw': ret_base[None, :] + nrm((DEPTH, RET_HEADS), 0.1),
        'ret_gn_w': 1.0 + nrm((DEPTH, RET_WIDTH), 0.02),
        'w_out': nrm((DEPTH, MIX_WIDTH, D_MODEL), MIX_WIDTH ** -0.5),
        'norm2_w': 1.0 + nrm((DEPTH, D_MODEL), 0.02),
        'ffn_w_gate': nrm((DEPTH, D_MODEL, FFN_HIDDEN), D_MODEL ** -0.5),
        'ffn_w_up': nrm((DEPTH, D_MODEL, FFN_HIDDEN), D_MODEL ** -0.5),
        'ffn_conv_w': nrm((DEPTH, FFN_CONV_WIDTH, FFN_CONV_WIDTH, FFN_HIDDEN), 1.0 / FFN_CONV_WIDTH),
        'ffn_conv_b': nrm((DEPTH, FFN_HIDDEN), 0.02),
        'ffn_w_down': nrm((DEPTH, FFN_HIDDEN, D_MODEL), FFN_HIDDEN ** -0.5),
        'final_norm_w': 1.0 + nrm((D_MODEL,), 0.02),
    }


def reference(x_prompt, x_sample, state_lru_fw, state_lru_bw, state_ret_fw, state_ret_bw, c, c_ctx,
              norm1_w, w_mod, b_mod, w_in, lru_conv_w, lru_conv_b,
              lru_wa_fw, lru_ba_fw, lru_wx_fw, lru_bx_fw, lru_lambda_fw,
              lru_wa_bw, lru_ba_bw, lru_wx_bw, lru_bx_bw, lru_lambda_bw,
              ret_decay_fw, ret_decay_bw, ret_gn_w, w_out, norm2_w,
              ffn_w_gate, ffn_w_up, ffn_conv_w, ffn_conv_b, ffn_w_down, final_norm_w):
    x_p = x_prompt
    x_s = x_sample
    bp = x_prompt.shape[0]
    lru_fw_list, lru_bw_list, ret_fw_list, ret_bw_list = [], [], [], []
    for l in range(DEPTH):
        p = dict(norm1_w=norm1_w[l], w_in=w_in[l], lru_conv_w=lru_conv_w[l], lru_conv_b=lru_conv_b[l],
                 lru_wa_fw=lru_wa_fw[l], lru_ba_fw=lru_ba_fw[l], lru_wx_fw=lru_wx_fw[l],
                 lru_bx_fw=lru_bx_fw[l], lru_lambda_fw=lru_lambda_fw[l],
                 lru_wa_bw=lru_wa_bw[l], lru_ba_bw=lru_ba_bw[l], lru_wx_bw=lru_wx_bw[l],
                 lru_bx_bw=lru_bx_bw[l], lru_lambda_bw=lru_lambda_bw[l],
                 ret_decay_fw=ret_decay_fw[l], ret_decay_bw=ret_decay_bw[l], ret_gn_w=ret_gn_w[l],
                 w_out=w_out[l], norm2_w=norm2_w[l], ffn_w_gate=ffn_w_gate[l], ffn_w_up=ffn_w_up[l],
                 ffn_conv_w=ffn_conv_w[l], ffn_conv_b=ffn_conv_b[l], ffn_w_down=ffn_w_down[l])
        mod_ctx = _modulation(c_ctx[None, :], w_mod[l], b_mod[l])
        mod_lat = _modulation(c, w_mod[l], b_mod[l])
        init_ctx = (jnp.zeros((bp, LRU_WIDTH), jnp.float32),
                    jnp.zeros((bp, LRU_WIDTH), jnp.float32),
                    jnp.zeros((bp, RET_HEADS, RET_HEAD_DIM, RET_HEAD_DIM), jnp.float32),
                    jnp.zeros((bp, RET_HEADS, RET_HEAD_DIM, RET_HEAD_DIM), jnp.float32))
        x_p, finals = _layer(x_p, mod_ctx, p, init_ctx, False)
        lru_fw_list.append(finals[0])
        lru_bw_list.append(finals[1])
        ret_fw_list.append(finals[2])
        ret_bw_list.append(finals[3])
        init_lat = (state_lru_fw[:, l].astype(jnp.float32), state_lru_bw[:, l].astype(jnp.float32),
                    state_ret_fw[:, l].astype(jnp.float32), state_ret_bw[:, l].astype(jnp.float32))
        x_s, _ = _layer(x_s, mod_lat, p, init_lat, True)
    y_prompt = _rmsnorm(x_p, final_norm_w)
    y_sample = _rmsnorm(x_s, final_norm_w)
    new_lru_fw = jnp.stack(lru_fw_list, axis=1)
    new_lru_bw = jnp.stack(lru_bw_list, axis=1)
    new_ret_fw = jnp.stack(ret_fw_list, axis=1)
    new_ret_bw = jnp.stack(ret_bw_list, axis=1)
    return (y_prompt, y_sample, new_lru_fw, new_lru_bw, new_ret_fw, new_ret_bw)
```

```python
import numpy as np
from contextlib import ExitStack
import concourse.bass as bass
import concourse.mybir as mybir
from concourse.bass_utils import run_bass_kernel_spmd

F32 = mybir.dt.float32
BF16 = mybir.dt.bfloat16
AF = mybir.ActivationFunctionType
ALU = mybir.AluOpType
D = 1024
LW = 512
FH = 2816
NCC = FH // 128
EPS = 1e-6
GW = 64
DEFAULT_FILL = 0
DEFAULT_FILLN = 512
DEFAULT_POOLCC = ""


_UNIQ = [0]


class Buf:
    def __init__(self, name):
        _UNIQ[0] += 1
        self.name = "%s_%d" % (name, _UNIQ[0])
        self.wtok = None
        self.rtoks = {}
        self.dsem = None
        self.dcnt = 0
        self.excl = False


class Tl:
    def __init__(self, t, name):
        self.t = t
        self.b = Buf(name)

    def __getitem__(self, k):
        return self.t[k]


def _b(x):
    return x.b if isinstance(x, Tl) else x


def I(name, *args, **kwargs):
    return lambda e: getattr(e, name)(*args, **kwargs)


class Sched:
    def __init__(self, nc, es):
        self.nc = nc
        self.es = es
        self.engs = ["pe", "act", "dve", "pool", "sp"]
        self.ops = {e: [] for e in self.engs}
        self.sem = {}
        self.cnt = {e: 0 for e in self.engs}
        self.seen = {e: {} for e in self.engs}
        self.dbufs = []
        self.fill_n = 0
        self.fill_fn = None
        for e in self.engs:
            self.sem[e] = es.enter_context(nc.semaphore("prog_" + e))

    def _need(self, eng, tok, waits):
        if tok is None:
            return
        key, val = tok
        if self.seen[eng].get(key, 0) >= val:
            return
        self.seen[eng][key] = val
        waits.append(tok)

    def _deps(self, eng, reads, writes):
        waits = []
        for b in reads:
            self._need(eng, b.wtok, waits)
        for b in writes:
            self._need(eng, b.wtok, waits)
            for t in b.rtoks.values():
                self._need(eng, t, waits)
        return waits

    def op(self, eng, fns, reads=(), writes=()):
        reads = [_b(x) for x in reads]
        writes = [_b(x) for x in writes]
        writes = writes + [b for b in reads if b.excl and b not in writes]
        reads = [b for b in reads if not b.excl]
        if callable(fns):
            fns = [fns]
        waits = self._deps(eng, reads, writes)
        self.cnt[eng] += 1
        tok = (eng, self.cnt[eng])
        for b in reads:
            b.rtoks[eng] = tok
        for b in writes:
            b.wtok = tok
            b.rtoks = {}
        pre = [self.fill_fn] * self.fill_n if (eng == "pe" and self.fill_n > 0 and self.fill_fn is not None and fns) else []
        self.ops[eng].append(("op", waits, fns, pre))
        return tok

    def dma(self, eng, fn, reads=(), writes=()):
        reads = [_b(x) for x in reads]
        writes = [_b(x) for x in writes]
        waits = self._deps(eng, reads, writes)
        owner = writes[0]
        if owner.dsem is None:
            owner.dsem = self.es.enter_context(self.nc.semaphore("d_" + owner.name))
            self.sem["d_" + owner.name] = owner.dsem
            self.dbufs.append(owner)
        owner.dcnt += 16
        tok = ("d_" + owner.name, owner.dcnt)
        for b in reads:
            b.rtoks["d_" + owner.name] = tok
        for b in writes:
            b.wtok = tok
            b.rtoks = {}
        self.ops[eng].append(("dma", waits, [fn], owner))
        return tok

    def wait_for(self, eng, bufs):
        waits = []
        for b in bufs:
            self._need(eng, _b(b).wtok, waits)
        self.ops[eng].append(("op", waits, [], None))

    def barrier(self):
        toks = [(e, self.cnt[e]) for e in self.engs if self.cnt[e] > 0]
        toks += [("d_" + b.name, b.dcnt) for b in self.dbufs]
        for e in self.engs:
            waits = []
            for t in toks:
                self._need(e, t, waits)
            self.ops[e].append(("op", waits, [], None))

    def emit(self):
        nc = self.nc
        import os
        if os.environ.get("KDBG"):
            print("emit: ops per engine", {e: sum(len(o[2]) for o in self.ops[e]) for e in self.engs},
                  "waits", {e: sum(len(o[1]) for o in self.ops[e]) for e in self.engs}, "cnt", dict(self.cnt), "nsem", len(self.sem), flush=True)
        with nc.allow_non_contiguous_dma(reason="small parameter / state layouts"), nc.Block() as block:
            def replay(e, name):
                for kind, waits, fns, owner in self.ops[name]:
                    if kind == "op" and owner:
                        for f in owner:
                            f(e)
                    for key, val in waits:
                        e.wait_ge(self.sem[key], val)
                    ins = None
                    for f in fns:
                        ins = f(e)
                    if ins is not None:
                        if kind == "dma":
                            ins.then_inc(owner.dsem, 16)
                        else:
                            ins.then_inc(self.sem[name], 1)

            @block.tensor
            def _(e):
                replay(e, "pe")

            @block.scalar
            def _(e):
                replay(e, "act")

            @block.vector
            def _(e):
                replay(e, "dve")

            @block.gpsimd
            def _(e):
                replay(e, "pool")

            @block.sync
            def _(e):
                replay(e, "sp")
        self.ops = {e: [] for e in self.engs}


def build_nc(TS=4096, NP=4, TP=256, dbg=False):
    import os
    KSTOP = os.environ.get("KSTOP", "")
    FLVL = int(os.environ.get("FLVL", "9"))
    KDBG = bool(os.environ.get("KDBG"))
    GELU = AF.Gelu if os.environ.get("GELU") == "erf" else AF.Gelu_apprx_tanh
    NOQ = os.environ.get("NOQ") == "1"
    POOL_CCS = set(int(x) for x in os.environ.get("POOLCC", DEFAULT_POOLCC).split(",") if x)
    FSUB = os.environ.get("FSUB", "gqkt")
    nc = bass.Bass("TRN2", target_bir_lowering=False)

    def inp(name, shape):
        return nc.dram_tensor(name, list(shape), F32, kind="ExternalInput").ap()

    def outp(name, shape):
        return nc.dram_tensor(name, list(shape), F32, kind="ExternalOutput").ap()

    xs = inp("xs", [TS, D])
    xp = inp("xp", [NP * TP, D])
    st_lf = inp("st_lf", [LW])
    st_lb = inp("st_lb", [LW])
    st_rf = inp("st_rf", [4, 128, 128])
    st_rb = inp("st_rb", [4, 128, 128])
    c_in = inp("c", [D])
    cx_in = inp("c_ctx", [D])
    norm1_w = inp("norm1_w", [D])
    w_mod = inp("w_mod", [D, 6 * D])
    b_mod = inp("b_mod", [6 * D])
    w_in = inp("w_in", [D, 3072])
    lru_conv_w = inp("lru_conv_w", [4, LW])
    lru_conv_b = inp("lru_conv_b", [LW])
    lru_w = [inp(n, [8, 64, 64]) for n in ("lru_wa_fw", "lru_wx_fw", "lru_wa_bw", "lru_wx_bw")]
    lru_bias = [inp(n, [LW]) for n in ("lru_ba_fw", "lru_bx_fw", "lru_ba_bw", "lru_bx_bw")]
    lam_f = inp("lru_lambda_fw", [LW])
    lam_b = inp("lru_lambda_bw", [LW])
    dec_f = inp("ret_decay_fw", [4])
    dec_b = inp("ret_decay_bw", [4])
    ret_gn_w = inp("ret_gn_w", [LW])
    w_out = inp("w_out", [D, D])
    norm2_w = inp("norm2_w", [D])
    w_gate = inp("ffn_w_gate", [D, FH])
    w_up = inp("ffn_w_up", [D, FH])
    ffn_conv_w = inp("ffn_conv_w", [9, FH])
    ffn_conv_b = inp("ffn_conv_b", [FH])
    w_down = inp("ffn_w_down", [FH, D])
    fnw = inp("final_norm_w", [D])

    ys = outp("ys", [TS, D])
    yp = outp("yp", [NP * TP, D])
    nlf = outp("nlf", [NP, LW])
    nlb = outp("nlb", [NP, LW])
    nrf = outp("nrf", [NP, 4, 128, 128])
    nrb = outp("nrb", [NP, 4, 128, 128])

    es = ExitStack()
    S = Sched(nc, es)
    outs_toks = []
    OB = {}

    def obuf(ap, name):
        if name not in OB:
            OB[name] = Tl(ap, name)
        return OB[name]

    def sb(name, shape, dtype=F32):
        return Tl(es.enter_context(nc.sbuf_tensor(name, list(shape), dtype)), name)

    esBF = ExitStack()
    esS = ExitStack()

    def sbS(name, shape, dtype=F32):
        return Tl(esS.enter_context(nc.sbuf_tensor(name, list(shape), dtype)), name)

    def sbBF(name, shape, dtype=F32):
        return Tl(esBF.enter_context(nc.sbuf_tensor(name, list(shape), dtype)), name)

    def ps(name, shape, dtype=F32):
        t = Tl(es.enter_context(nc.psum_tensor(name, list(shape), dtype)), name)
        t.b.excl = True
        return t

    def dr(name, shape, dtype=F32):
        return Tl(nc.dram_tensor(name, list(shape), dtype, kind="Internal").ap(), name)

    segs = [dict(name="s", kind="s", T=TS, nt=512, xin=xs, yout=ys, m=0, idx=0)]
    for i in range(NP):
        segs.append(dict(name="p%d" % i, kind="p", T=TP, nt=TP, xin=xp[i * TP:(i + 1) * TP, :],
                         yout=yp[i * TP:(i + 1) * TP, :], m=1, idx=i))
    for sg in segs:
        T = sg["T"]
        n = sg["name"]
        sg["lx"] = dr("lx_" + n, [4, 128, T + 3])
        sg["h1"] = dr("h1_" + n, [8, 128, T], BF16)
        sg["h1"].b = sg["lx"].b
        sg["hb"] = dr("hb_" + n, [4, 128, T])
        sg["Sb"] = dr("Sb_" + n, [T // 128, 128, 512], BF16)
        sg["x1"] = dr("x1_" + n, [T, D])
        sg["h2"] = dr("h2_" + n, [8, 128, T + 128], BF16)
        sg["yb"] = [Tl(sg["yout"], "y_%s_%d" % (n, k)) for k in range(2)]
        sg["blocks"] = [(t0, sg["nt"]) for t0 in range(0, T, sg["nt"])]
        sg["ntb"] = min(256, sg["nt"])
        sg["blocksBF"] = [(t0, sg["ntb"]) for t0 in range(0, T, sg["ntb"])]
        sg["ntB"] = sg["nt"]
        sg["blocksB"] = [(t0, sg["ntB"]) for t0 in range(0, T, sg["ntB"])]

    g2d = dr("g2d", [2, D])
    PB = [ps("pb%d" % i, [128, 512]) for i in range(6)]
    PT = [ps("ptA", [128, 1024], BF16), ps("ptB", [128, 1024], BF16)]
    pp_rr = [0, 6 if int(os.environ.get("FILL", str(DEFAULT_FILL))) == 0 else 5]
    PJ = PB[5]

    def PP():
        k = pp_rr[0] % pp_rr[1]
        pp_rr[0] += 1
        return PB[k]

    PS_, PO_, PKV_ = PB[3], PB[4], PB[5]

    ident = sb("ident", [128, 128], BF16)
    zerof = sb("zerof", [128, 512])
    zerob = sb("zerob", [128, 512], BF16)
    epsc = sb("epsc", [128, 1])
    mhalf = sb("mhalf", [128, 8])
    qtr = sb("qtr", [128, 1])
    fcw = sb("fcw", [128, 9 * NCC])
    fcb = sb("fcb", [128, NCC])
    AB = sb("AB", [128, 4, 2, 8])
    cw = sbBF("cw", [128, 16])
    cb = sbBF("cb", [128, 4])
    lbias = sbBF("lbias", [128, 16])
    cf = sbBF("cf", [128, 8])
    gnw = sbBF("gnw", [128, 4])
    g1bc = [sbBF("g1bc%d" % m, [128, D]) for m in range(2)]
    Wbd = sbBF("Wbd", [128, 16, 128], BF16)
    g128 = sbBF("g128", [128, 8])
    Dm = sbBF("Dm", [128, 4, 128])
    HF = sbBF("HF", [128, 4, 128])
    HB = sbBF("HB", [128, 4, 128])
    TFm = sbBF("TFm", [128, 4, 128])
    TBm = sbBF("TBm", [128, 4, 128])
    h0 = sbBF("h0", [128, 8])
    g2bc = [sbS("g2bc%d" % m, [128, D]) for m in range(2)]
    identf = sbS("identf", [128, 128])
    ones_bf = sbS("ones_bf", [128, 128], BF16)
    onesf = sbS("onesf", [128, 128])
    n1w = sbS("n1w", [128, 8])
    n2w = sbS("n2w", [128, 8])
    bm = sbS("bm", [128, 48])
    lam = sbS("lam", [128, 8])
    cT = sbS("cT", [128, 16])
    scf = sbS("scf", [128, 16])
    scT = sbS("scT", [128, 8, 2], BF16)
    bcT = sbS("bcT", [128, 16, 128], BF16)
    modF = sbS("modF", [128, 32, 2])
    bg = sbS("bg", [128, 2, D])
    dec = sbS("dec", [128, 8])
    lg = sbS("lg", [128, 8])
    rel = sbS("rel", [128, 128])
    tmpA = sbS("tmpA", [128, 128])
    tmpB = sbS("tmpB", [128, 128])
    mge = sbS("mge", [128, 128])
    mle = sbS("mle", [128, 128])
    pj = sbS("pj", [128, 2])
    tl8 = sbS("tl8", [128, 8])

    def ldcol(dst, dst_ap, src1d):
        S.dma("sp", I("dma_start", out=dst_ap, in_=src1d.rearrange("(n p) -> p n", p=128)), writes=[dst])

    S.op("pool", I("memset", identf[:], 0.0), writes=[identf])
    S.op("pool", I("affine_select", out=identf[:], in_=identf[:], pattern=[[-1, 128]], compare_op=ALU.not_equal,
                                           fill=1.0, base=0, channel_multiplier=1), reads=[identf], writes=[identf])
    S.op("dve", I("tensor_copy", out=ident[:], in_=identf[:]), reads=[identf], writes=[ident])
    S.op("pool", I("memset", onesf[:], 1.0), writes=[onesf])
    S.op("pool", I("memset", zerof[:], 0.0), writes=[zerof])
    S.op("pool", I("memset", zerob[:], 0.0), writes=[zerob])
    S.op("pool", I("memset", epsc[:], EPS), writes=[epsc])
    S.op("pool", I("memset", mhalf[:], -0.5), writes=[mhalf])
    S.op("pool", I("memset", qtr[:], 0.25), writes=[qtr])

    ldcol(n1w, n1w[:], norm1_w)
    ldcol(n2w, n2w[:], norm2_w)
    ldcol(bm, bm[:], b_mod)
    for j in range(4):
        ldcol(cw, cw[:, j * 4:(j + 1) * 4], lru_conv_w[j])
    ldcol(cb, cb[:], lru_conv_b)
    for k in range(4):
        ldcol(lbias, lbias[:, k * 4:(k + 1) * 4], lru_bias[k])
    ldcol(lam, lam[:, 0:4], lam_f)
    ldcol(lam, lam[:, 4:8], lam_b)
    for tp in range(9):
        ldcol(fcw, fcw[:, tp * NCC:(tp + 1) * NCC], ffn_conv_w[tp])
    ldcol(fcb, fcb[:], ffn_conv_b)
    ldcol(gnw, gnw[:], ret_gn_w)
    ldcol(cT, cT[:, 0:8], c_in)
    ldcol(cT, cT[:, 8:16], cx_in)
    ldcol(h0, h0[:, 0:4], st_lf)
    ldcol(h0, h0[:, 4:8], st_lb)
    S.dma("sp", I("dma_start", out=bg[:, 0, :], in_=b_mod[2 * D:3 * D].partition_broadcast(128)), writes=[bg])
    S.dma("sp", I("dma_start", out=bg[:, 1, :], in_=b_mod[5 * D:6 * D].partition_broadcast(128)), writes=[bg])
    S.dma("sp", I("dma_start", out=dec[:, 0:4], in_=dec_f.partition_broadcast(128)), writes=[dec])
    S.dma("sp", I("dma_start", out=dec[:, 4:8], in_=dec_b.partition_broadcast(128)), writes=[dec])

    S.op("pool", I("memset", Wbd[:], 0.0), writes=[Wbd])
    for mi in range(4):
        for half in range(2):
            src = lru_w[mi].rearrange("(n two) c d -> two c n d", two=2)[half]
            S.dma("pool", I("dma_start",
                out=Wbd[half * 64:(half + 1) * 64, mi * 4:(mi + 1) * 4, half * 64:(half + 1) * 64], in_=src), writes=[Wbd])

    S.op("act", I("activation", out=cf[:], in_=lam[:], func=AF.Exp, scale=-1.0), reads=[lam], writes=[cf])
    S.op("act", I("activation", out=cf[:], in_=cf[:], func=AF.Ln, bias=1.0), reads=[cf], writes=[cf])
    S.op("dve", I("tensor_scalar", out=cf[:], in0=cf[:], scalar1=-4.0, scalar2=None, op0=ALU.mult), reads=[cf], writes=[cf])
    S.op("dve", I("tensor_scalar", out=lbias[:], in0=lbias[:], scalar1=0.5, scalar2=None, op0=ALU.mult), reads=[lbias], writes=[lbias])
    S.op("dve", I("tensor_scalar", out=gnw[:], in0=gnw[:], scalar1=0.5, scalar2=None, op0=ALU.mult), reads=[gnw], writes=[gnw])
    S.op("act", I("activation", out=lg[:], in_=dec[:], func=AF.Exp, scale=-1.0), reads=[dec], writes=[lg])
    S.op("act", I("activation", out=lg[:], in_=lg[:], func=AF.Ln, bias=1.0), reads=[lg], writes=[lg])
    S.op("dve", I("tensor_scalar", out=lg[:], in0=lg[:], scalar1=-1.0, scalar2=None, op0=ALU.mult), reads=[lg], writes=[lg])
    S.op("act", I("activation", out=g128[:], in_=lg[:], func=AF.Exp, scale=128.0), reads=[lg], writes=[g128])
    S.op("pool", I("iota", rel[:], pattern=[[1, 128]], base=0, channel_multiplier=-1,
                                  allow_small_or_imprecise_dtypes=True), writes=[rel])
    S.op("dve", I("tensor_scalar", out=mge[:], in0=rel[:], scalar1=0.0, scalar2=None, op0=ALU.is_ge), reads=[rel], writes=[mge])
    S.op("dve", I("tensor_scalar", out=mle[:], in0=rel[:], scalar1=0.0, scalar2=None, op0=ALU.is_le), reads=[rel], writes=[mle])
    for h in range(4):
        S.op("dve", I("tensor_scalar", out=tmpA[:], in0=rel[:], scalar1=0.0, scalar2=None, op0=ALU.max), reads=[rel], writes=[tmpA])
        S.op("act", I("activation", out=tmpA[:], in_=tmpA[:], func=AF.Exp, scale=lg[:, h:h + 1]), reads=[tmpA, lg], writes=[tmpA])
        S.op("dve", I("tensor_tensor", out=tmpA[:], in0=tmpA[:], in1=mge[:], op=ALU.mult), reads=[tmpA, mge], writes=[tmpA])
        S.op("dve", I("tensor_scalar", out=tmpB[:], in0=rel[:], scalar1=-1.0, scalar2=0.0, op0=ALU.mult, op1=ALU.max), reads=[rel], writes=[tmpB])
        S.op("act", I("activation", out=tmpB[:], in_=tmpB[:], func=AF.Exp, scale=lg[:, 4 + h:5 + h]), reads=[tmpB, lg], writes=[tmpB])
        S.op("dve", I("tensor_tensor", out=tmpB[:], in0=tmpB[:], in1=mle[:], op=ALU.mult), reads=[tmpB, mle], writes=[tmpB])
        S.op("dve", I("tensor_tensor", out=Dm[:, h, :], in0=tmpA[:], in1=tmpB[:], op=ALU.add), reads=[tmpA, tmpB], writes=[Dm])
    S.op("pool", I("iota", tmpA[:], pattern=[[1, 128]], base=1, channel_multiplier=0,
                                  allow_small_or_imprecise_dtypes=True), reads=[Dm], writes=[tmpA])
    S.op("pool", I("iota", tmpB[:], pattern=[[-1, 128]], base=128, channel_multiplier=0,
                                  allow_small_or_imprecise_dtypes=True), reads=[Dm], writes=[tmpB])
    for h in range(4):
        S.op("act", I("activation", out=HF[:, h, :], in_=tmpA[:], func=AF.Exp, scale=lg[:, h:h + 1]), reads=[tmpA, lg], writes=[HF])
        S.op("act", I("activation", out=HB[:, h, :], in_=tmpB[:], func=AF.Exp, scale=lg[:, 4 + h:5 + h]), reads=[tmpB, lg], writes=[HB])
    S.op("pool", I("iota", pj[:, 0:1], pattern=[[0, 1]], base=127, channel_multiplier=-1,
                                  allow_small_or_imprecise_dtypes=True), writes=[pj])
    S.op("pool", I("iota", pj[:, 1:2], pattern=[[0, 1]], base=0, channel_multiplier=1,
                                  allow_small_or_imprecise_dtypes=True), reads=[pj], writes=[pj])
    S.op("dve", I("tensor_scalar", out=tl8[:, 0:4], in0=lg[:, 0:4], scalar1=pj[:, 0:1], scalar2=None, op0=ALU.mult), reads=[lg, pj], writes=[tl8])
    S.op("dve", I("tensor_scalar", out=tl8[:, 4:8], in0=lg[:, 4:8], scalar1=pj[:, 1:2], scalar2=None, op0=ALU.mult), reads=[lg, pj, tl8], writes=[tl8])
    S.op("act", I("activation", out=tl8[:], in_=tl8[:], func=AF.Exp), reads=[tl8], writes=[tl8])
    for h in range(4):
        S.op("dve", I("tensor_scalar", out=TFm[:, h, :], in0=onesf[:], scalar1=tl8[:, h:h + 1], scalar2=None, op0=ALU.mult), reads=[onesf, tl8], writes=[TFm])
        S.op("dve", I("tensor_scalar", out=TBm[:, h, :], in0=onesf[:], scalar1=tl8[:, 4 + h:5 + h], scalar2=None, op0=ALU.mult), reads=[onesf, tl8], writes=[TBm])

    S.wait_for("pe", [ident, zerob])
    S.fill_n = int(os.environ.get("FILL", str(DEFAULT_FILL)))
    FILLN = int(os.environ.get("FILLN", str(DEFAULT_FILLN)))
    S.fill_fn = I("matmul", PJ[:, 0:FILLN], lhsT=ident[:], rhs=zerob[:, 0:FILLN], start=True, stop=True)
    S.op("act", I("activation", out=scf[:], in_=cT[:], func=AF.Silu), reads=[cT], writes=[scf])
    for m in range(2):
        S.op("dve", I("tensor_copy", out=scT[:, :, m], in_=scf[:, m * 8:(m + 1) * 8]), reads=[scf], writes=[scT])
    for m in range(2):
        for kc in range(8):
            S.op("dve", I("tensor_scalar", out=bcT[:, m * 8 + kc, :], in0=onesf[:], scalar1=scf[:, m * 8 + kc:m * 8 + kc + 1],
                                                              scalar2=None, op0=ALU.mult), reads=[onesf, scf], writes=[bcT])
    if True:
        wm = [sbS("wm%d" % i, [128, 8, D], BF16) for i in range(2)]
        PM = PB[3]
        fidx = {0: 0, 1: 1, 3: 2, 4: 3}
        for g in range(6):
            w = wm[g % 2]
            S.dma("pool", I("dma_start", out=w[:], in_=w_mod[:, g * D:(g + 1) * D].rearrange("(kc p) n -> p kc n", p=128)), writes=[w])
            if g in fidx:
                fi = fidx[g]
                fns = []
                for ncx in range(8):
                    for kc in range(8):
                        fns.append(I("matmul",
                            PM[:, (fi * 8 + ncx) * 2:(fi * 8 + ncx) * 2 + 2], lhsT=w[:, kc, ncx * 128:(ncx + 1) * 128], rhs=scT[:, kc, :],
                            start=(kc == 0), stop=(kc == 7)))
                S.op("pe", fns, reads=[w, scT], writes=[PM])
                for m in range(2):
                    S.op("dve", I("tensor_tensor",
                        out=modF[:, fi * 8:(fi + 1) * 8, m], in0=PM[:, fi * 16:(fi + 1) * 16].rearrange("p (n m) -> p n m", m=2)[:, :, m],
                        in1=bm[:, g * 8:(g + 1) * 8], op=ALU.add), reads=[PM, bm], writes=[modF])
            else:
                gi = 0 if g == 2 else 1
                dst = g1bc if g == 2 else g2bc
                for m in range(2):
                    for nh in range(2):
                        pp = PP()
                        fns = [I("matmul",
                            pp[:], lhsT=bcT[:, m * 8 + kc, :], rhs=w[:, kc, nh * 512:(nh + 1) * 512], start=(kc == 0), stop=(kc == 7))
                            for kc in range(8)]
                        S.op("pe", fns, reads=[w, bcT], writes=[pp])
                        S.op("dve", I("tensor_tensor",
                            out=dst[m][:, nh * 512:(nh + 1) * 512], in0=pp[:], in1=bg[:, gi, nh * 512:(nh + 1) * 512], op=ALU.add),
                            reads=[pp, bg], writes=[dst[m]])
        for m in range(2):
            S.op("dve", I("scalar_tensor_tensor", out=AB[:, 0, m, :], in0=modF[:, 8:16, m], scalar=1.0, in1=n1w[:],
                                                              op0=ALU.add, op1=ALU.mult), reads=[modF, n1w], writes=[AB])
            S.op("dve", I("tensor_copy", out=AB[:, 1, m, :], in_=modF[:, 0:8, m]), reads=[modF], writes=[AB])
            S.op("dve", I("scalar_tensor_tensor", out=AB[:, 2, m, :], in0=modF[:, 24:32, m], scalar=1.0, in1=n2w[:],
                                                              op0=ALU.add, op1=ALU.mult), reads=[modF, n2w], writes=[AB])
            S.op("dve", I("tensor_copy", out=AB[:, 3, m, :], in_=modF[:, 16:24, m]), reads=[modF], writes=[AB])
        for m in range(2):
            S.dma("sp", I("dma_start", out=g2d[m:m + 1, :], in_=g2bc[m][0:1, :]), reads=[g2bc[m]], writes=[g2d])
        S.barrier()
        S.emit()
    esS.close()
    if KSTOP == "setup":
        esBF.close()
        es.close()
        return nc

    def norm_mod_T(xtm, ntt, m, which, hT, scr):
        ssq, rstd, xn, junk = scr
        for tt in range(ntt):
            S.op("act", I("activation", out=junk[:], in_=xtm[:, tt, :], func=AF.Square, accum_out=ssq[:, tt:tt + 1]),
                 reads=[xtm], writes=[junk, ssq])
        S.op("dve", I("tensor_scalar", out=rstd[:, 0:ntt], in0=ssq[:, 0:ntt], scalar1=1.0 / D, scalar2=EPS, op0=ALU.mult, op1=ALU.add),
             reads=[ssq], writes=[rstd])
        S.op("pool", I("tensor_tensor", out=rstd[:, 0:ntt], in0=rstd[:, 0:ntt], in1=mhalf[:, 0:ntt], op=ALU.pow), reads=[rstd, mhalf], writes=[rstd])
        yield
        for tt in range(ntt):
            S.op("act", I("activation", out=xn[:, tt, :], in_=xtm[:, tt, :], func=AF.Copy, scale=rstd[:, tt:tt + 1]),
                 reads=[xtm, rstd], writes=[xn])
        yield
        for kc in range(8):
            pt = PT[kc % 2]
            fns = [I("transpose", out=pt[:, tt * 128:(tt + 1) * 128], in_=xn[:, tt, kc * 128:(kc + 1) * 128], identity=ident[:])
                   for tt in range(ntt)]
            S.op("pe", fns, reads=[xn, ident], writes=[pt])
            S.op("dve", I("tensor_scalar", out=hT[:, kc, 0:ntt * 128], in0=pt[:, 0:ntt * 128], scalar1=AB[:, which, m, kc:kc + 1],
                                                                scalar2=AB[:, which + 1, m, kc:kc + 1], op0=ALU.mult, op1=ALU.add),
                 reads=[pt, AB], writes=[hT])
            yield

    def run_merged(gens, ratio=None, after=None):
        ratio = ratio or (1,) * len(gens)
        after = after or {}
        live = list(range(len(gens)))
        while live:
            for gi in list(live):
                if gi in after and after[gi] in live:
                    continue
                for _ in range(ratio[gi]):
                    try:
                        next(gens[gi])
                    except StopIteration:
                        live.remove(gi)
                        break

    lru_cnt = [0]

    def lru_dir(sg, t0, nt, d, L, hstate, ret):
        lxw, xcs, xcbs, rrs, iis, a4, v4, t4, hhs = L
        hh = hhs[lru_cnt[0] % len(hhs)] if isinstance(hhs, list) else hhs
        lru_cnt[0] += 1
        ret.append(hh)
        lxd = sg["lx"]
        S.dma("sp", I("dma_start", out=lxw[:, :, 0:nt + 3], in_=lxd[:, :, t0:t0 + nt + 3].rearrange("c p t -> p c t")),
              reads=[lxd], writes=[lxw])
        for cc in range(4):
            xc, xcb, rr, ii = xcs[cc % 2], xcbs[cc % 2], rrs[cc % 2], iis[cc % 2]
            S.op("dve", I("tensor_scalar", out=xc[:, 0:nt], in0=lxw[:, cc, 0:nt], scalar1=cw[:, cc:cc + 1], scalar2=cb[:, cc:cc + 1],
                          op0=ALU.mult, op1=ALU.add), reads=[lxw, cw, cb], writes=[xc])
            for j in range(1, 4):
                S.op("dve", I("scalar_tensor_tensor", out=xc[:, 0:nt], in0=lxw[:, cc, j:j + nt], scalar=cw[:, j * 4 + cc:j * 4 + cc + 1],
                              in1=xc[:, 0:nt], op0=ALU.mult, op1=ALU.add), reads=[lxw, cw, xc], writes=[xc])
            S.op("act", I("activation", out=xcb[:, 0:nt], in_=xc[:, 0:nt], func=AF.Copy), reads=[xc], writes=[xcb])
            yield
            for gi, dst in ((0, rr), (1, ii)):
                pp = PP()
                mi = d * 2 + gi
                S.op("pe", I("matmul", pp[:, 0:nt], lhsT=Wbd[:, mi * 4 + cc, :], rhs=xcb[:, 0:nt], start=True, stop=True),
                     reads=[Wbd, xcb], writes=[pp])
                S.op("act", I("activation", out=dst[:, 0:nt], in_=pp[:, 0:nt], func=AF.Tanh, scale=0.5,
                              bias=lbias[:, mi * 4 + cc:mi * 4 + cc + 1]), reads=[pp, lbias], writes=[dst])
                yield
            S.op("act", I("activation", out=a4[:, cc, 0:nt], in_=rr[:, 0:nt], func=AF.Exp, scale=cf[:, d * 4 + cc:d * 4 + cc + 1],
                          bias=cf[:, d * 4 + cc:d * 4 + cc + 1]), reads=[rr, cf], writes=[a4])
            S.op("dve", I("scalar_tensor_tensor", out=t4[:, cc, 0:nt], in0=ii[:, 0:nt], scalar=1.0, in1=xc[:, 0:nt], op0=ALU.add, op1=ALU.mult),
                 reads=[ii, xc], writes=[t4])
            yield
        S.op("act", I("activation", out=v4[:, :, 0:nt], in_=a4[:, :, 0:nt], func=AF.Square), reads=[a4], writes=[v4])
        S.op("act", I("activation", out=v4[:, :, 0:nt], in_=v4[:, :, 0:nt], func=AF.Sqrt, scale=-0.25, bias=qtr[:]), reads=[v4, qtr], writes=[v4])
        yield
        S.op("dve", I("tensor_tensor", out=t4[:, :, 0:nt], in0=t4[:, :, 0:nt], in1=v4[:, :, 0:nt], op=ALU.mult), reads=[t4, v4], writes=[t4])
        yield
        for cc in range(4):
            if d == 0:
                S.op("dve", I("tensor_tensor_scan", out=hh[:, cc, 0:nt], data0=a4[:, cc, 0:nt], data1=t4[:, cc, 0:nt],
                              initial=hstate[:, cc:cc + 1], op0=ALU.mult, op1=ALU.add), reads=[a4, t4, hstate], writes=[hh])
            else:
                S.op("dve", I("tensor_tensor_scan", out=hh[:, cc, nt - 1::-1], data0=a4[:, cc, nt - 1::-1], data1=t4[:, cc, nt - 1::-1],
                              initial=hstate[:, 4 + cc:5 + cc], op0=ALU.mult, op1=ALU.add), reads=[a4, t4, hstate], writes=[hh])
            yield
        edge = nt - 1 if d == 0 else 0
        S.op("dve", I("tensor_copy", out=hstate[:, d * 4:(d + 1) * 4], in_=hh[:, :, edge]), reads=[hh], writes=[hstate])
        yield

    with ExitStack() as es2:
        def sb2(name, shape, dtype=F32):
            return Tl(es2.enter_context(nc.sbuf_tensor(name, list(shape), dtype)), name)
        winB = sb2("winB", [128, 8, 1536], BF16)
        xtm2 = [sb2("xtmB%d" % i, [128, 4, D]) for i in range(2)]
        xnB = sb2("xnB", [128, 4, D], BF16)
        junkB = sb2("junkB", [128, D], BF16)
        scr2 = [(sb2("ssqB%d" % i, [128, 4]), sb2("rstdB%d" % i, [128, 4]), xnB, junkB) for i in range(2)]
        hT2 = [sb2("hTB%d" % i, [128, 8, 512], BF16) for i in range(2)]
        lxo2 = [sb2("lxoB", [128, 4, 512])] * 2
        Ktok2 = [sb2("KtokB%d" % i, [128, 4, 512], BF16) for i in range(2)]
        VBtok2 = [sb2("VBtokB%d" % i, [128, 4, 512], BF16) for i in range(2)]
        Sb = sb2("SbB", [128, 4, 128])
        Sbb2 = [sb2("SbbB%d" % i, [128, 512], BF16) for i in range(2)]
        hst = sb2("hstB", [128, 8])
        L = (sb2("lxwB", [128, 4, 515]), [sb2("xcB%d" % i, [128, 512]) for i in range(2)], [sb2("xcbB%d" % i, [128, 512], BF16) for i in range(2)],
             [sb2("rrB%d" % i, [128, 512]) for i in range(2)], [sb2("iiB%d" % i, [128, 512]) for i in range(2)],
             sb2("a4B", [128, 4, 512]), sb2("v4B", [128, 4, 512]), sb2("t4B", [128, 4, 512]),
             sb2("hhB", [128, 4, 512]))
        if KDBG:
            print("pass B sbuf remaining", nc.sbuf_bytes_remaining, flush=True)
        winBc = []
        for (dst0, src0) in ((0, 0), (512, 1536), (1024, 2048)):
            v = Tl(winB.t[:, :, dst0:dst0 + 512], "winB_%d" % dst0)
            winBc.append(v)
            S.dma("pool", I("dma_start", out=v[:, :, :], in_=w_in[:, src0:src0 + 512].rearrange("(kc p) n -> p kc n", p=128)), writes=[v])
        work = []
        for sg in segs:
            T = sg["T"]
            S.dma("sp", I("dma_start", out=sg["lx"][:, :, 0:2].rearrange("c p t -> p c t"),
                          in_=zerof[:, 0:8].rearrange("p (c t) -> p c t", t=2)), reads=[zerof], writes=[sg["lx"]])
            S.dma("sp", I("dma_start", out=sg["lx"][:, :, T + 2:T + 3].rearrange("c p t -> p c t"),
                          in_=zerof[:, 0:4].rearrange("p (c t) -> p c t", t=1)), reads=[zerof], writes=[sg["lx"]])
            nb = len(sg["blocksB"])
            for bi in range(nb - 1, -1, -1):
                work.append((sg, bi))

        def FE_B(wi):
            sg, bi = work[wi]
            p = wi % 2
            nt, m = sg["ntB"], sg["m"]
            ntt = nt // 128
            t0 = sg["blocksB"][bi][0]
            xtm, scr, hT, lxo, Ktok, VBtok = xtm2[p], scr2[p], hT2[p], lxo2[p], Ktok2[p], VBtok2[p]
            S.dma("sp", I("dma_start", out=xtm[:, 0:ntt, :], in_=sg["xin"][t0:t0 + nt, :].rearrange("(tt p) d -> p tt d", p=128)), writes=[xtm])
            yield
            yield from norm_mod_T(xtm, ntt, m, 0, hT, scr)
            for kc0 in (0, 4):
                S.dma("sp", I("dma_start", out=sg["h1"][kc0:kc0 + 4, :, t0:t0 + nt].rearrange("k p t -> p k t"), in_=hT[:, kc0:kc0 + 4, 0:nt]),
                      reads=[hT], writes=[sg["h1"]])
            yield
            for cc in range(4):
                pp = PP()
                fns = [I("matmul", pp[:, 0:nt], lhsT=winBc[0][:, kc, cc * 128:(cc + 1) * 128], rhs=hT[:, kc, 0:nt],
                         start=(kc == 0), stop=(kc == 7)) for kc in range(8)]
                S.op("pe", fns, reads=[winBc[0], hT], writes=[pp])
                S.op("act", I("activation", out=lxo[:, cc, 0:nt], in_=pp[:, 0:nt], func=AF.Copy), reads=[pp], writes=[lxo])
                yield
            S.dma("sp", I("dma_start", out=sg["lx"][:, :, 2 + t0:2 + t0 + nt].rearrange("c p t -> p c t"), in_=lxo[:, :, 0:nt]),
                  reads=[lxo], writes=[sg["lx"]])
            yield
            for tt in range(ntt):
                for which in range(2):
                    pp = PP()
                    fns = [I("matmul", pp[:], lhsT=hT[:, kc, tt * 128:(tt + 1) * 128], rhs=winBc[1 + which][:, kc, 0:512],
                             start=(kc == 0), stop=(kc == 7)) for kc in range(8)]
                    S.op("pe", fns, reads=[winBc[1 + which], hT], writes=[pp])
                    if which == 0:
                        S.op("act", I("activation", out=Ktok[:, tt, :], in_=pp[:], func=AF.Copy, scale=128.0 ** -0.5), reads=[pp], writes=[Ktok])
                        yield
                    else:
                        S.op("dve", I("tensor_tensor", out=VBtok[:, tt, :], in0=pp[:], in1=TBm[:].rearrange("p h e -> p (h e)"), op=ALU.mult),
                             reads=[pp, TBm], writes=[VBtok])
                        yield

        def BE_Bret(wi):
            sg, bi = work[wi]
            p = wi % 2
            nt = sg["ntB"]
            ntt = nt // 128
            blocks = sg["blocksB"]
            t0 = blocks[bi][0]
            Ktok, VBtok = Ktok2[p], VBtok2[p]
            if bi == len(blocks) - 1:
                if sg["kind"] == "s":
                    S.dma("sp", I("dma_start", out=Sb[:], in_=st_rb.rearrange("h d e -> d h e")), writes=[Sb])
                else:
                    S.op("pool", I("memset", Sb[:], 0.0), writes=[Sb])
                yield
            for tt in range(ntt - 1, -1, -1):
                ci = t0 // 128 + tt
                Sbb = Sbb2[ci % 2]
                S.op("act", I("activation", out=Sbb[:], in_=Sb[:].rearrange("p h e -> p (h e)"), func=AF.Copy), reads=[Sb], writes=[Sbb])
                yield
                S.dma("sp", I("dma_start", out=sg["Sb"][ci], in_=Sbb[:]), reads=[Sbb], writes=[sg["Sb"]])
                yield
                pkv = PP()
                fns = [I("matmul", pkv[:, h * 128:(h + 1) * 128], lhsT=Ktok[:, tt, h * 128:(h + 1) * 128],
                         rhs=VBtok[:, tt, h * 128:(h + 1) * 128], start=True, stop=True) for h in range(4)]
                S.op("pe", fns, reads=[Ktok, VBtok], writes=[pkv])
                for h in range(4):
                    S.op("dve", I("scalar_tensor_tensor", out=Sb[:, h, :], in0=Sb[:, h, :], scalar=g128[:, 4 + h:5 + h],
                                  in1=pkv[:, h * 128:(h + 1) * 128], op0=ALU.mult, op1=ALU.add), reads=[Sb, g128, pkv], writes=[Sb])
                yield
            if bi == 0 and sg["kind"] == "p":
                i = sg["idx"]
                ob = obuf(nrb, "nrb")
                outs_toks.append(S.dma("sp", I("dma_start", out=nrb[i].rearrange("h d e -> d h e"), in_=Sb[:]), reads=[Sb], writes=[ob]))
                yield

        def BE_Blru(wi):
            sg, bi = work[wi]
            nt = sg["ntB"]
            blocks = sg["blocksB"]
            if bi == len(blocks) - 1:
                if sg["kind"] == "s":
                    S.op("pool", I("tensor_copy", out=hst[:], in_=h0[:]), reads=[h0], writes=[hst])
                else:
                    S.op("pool", I("memset", hst[:], 0.0), writes=[hst])
                yield
            todo = []
            if bi + 1 < len(blocks):
                todo.append(bi + 1)
            if bi == 0:
                todo.append(0)
            for bj in todo:
                tj = blocks[bj][0]
                ret = []
                yield from lru_dir(sg, tj, nt, 1, L, hst, ret)
                hh = ret[0]
                if sg["kind"] == "p" and bj == len(blocks) - 1:
                    ob2 = obuf(nlb, "nlb")
                    outs_toks.append(S.dma("sp", I("dma_start", out=nlb[sg["idx"]].rearrange("(n p) -> p n", p=128), in_=hh[:, :, nt - 1]),
                                           reads=[hh], writes=[ob2]))
                S.dma("sp", I("dma_start", out=sg["hb"][:, :, tj:tj + nt].rearrange("c p t -> p c t"), in_=hh[:, :, 0:nt]),
                      reads=[hh], writes=[sg["hb"]])
                yield

        run_merged([FE_B(0)])
        for wi in range(len(work)):
            gens = [BE_Bret(wi), BE_Blru(wi)]
            if wi + 1 < len(work):
                gens.append(FE_B(wi + 1))
            run_merged(gens)
        S.barrier()
        S.emit()
    if KSTOP == "B":
        esBF.close()
        es.close()
        return nc

    with ExitStack() as es2:
        def sb2(name, shape, dtype=F32):
            return Tl(es2.enter_context(nc.sbuf_tensor(name, list(shape), dtype)), name)
        winF = sb2("winF", [128, 8, 2560], BF16)
        wo = sb2("wo", [128, 8, D], BF16)
        xtm2 = [sb2("xtmF%d" % i, [128, 2, D]) for i in range(2)]
        junkF = sb2("junkF", [128, D], BF16)
        xnF = sb2("xnF", [128, 2, D], BF16)
        scr2 = [(sb2("ssqF%d" % i, [128, 4]), sb2("rstdF%d" % i, [128, 4]), xnF, junkF) for i in range(2)]
        scrBE = (sb2("ssqFb", [128, 4]), sb2("rstdFb", [128, 4]), xnF, junkF)
        hT2 = [sb2("hTF%d" % i, [128, 8, 256], BF16) for i in range(2)]
        h2T = sb2("h2TF", [128, 8, 256], BF16)
        gl2 = [sb2("glF%d" % i, [128, 4, 256], BF16) for i in range(2)]
        qT2 = [sb2("qTF%d" % i, [128, 4, 256], BF16) for i in range(2)]
        qf2 = [sb2("qfF%d" % i, [128, 4, 256], BF16) for i in range(2)]
        qb2 = [sb2("qbF%d" % i, [128, 4, 256], BF16) for i in range(2)]
        kT2 = [sb2("kTF%d" % i, [128, 4, 256], BF16) for i in range(2)]
        Ktok2 = [sb2("KtokF%d" % i, [128, 2, 512], BF16) for i in range(2)]
        Vtok2 = [sb2("VtokF%d" % i, [128, 2, 512], BF16) for i in range(2)]
        VFtok2 = [sb2("VFtokF%d" % i, [128, 2, 512], BF16) for i in range(2)]
        rg2 = [sb2("rgF%d" % i, [128, 2, 512], BF16) for i in range(2)]
        Sf = sb2("SfF", [128, 4, 128])
        Sfb2 = [sb2("SfbF%d" % i, [128, 512], BF16) for i in range(2)]
        Sbb2 = [sb2("SbbF%d" % i, [128, 512], BF16) for i in range(2)]
        PTs2 = [sb2("PTsF%d" % i, [128, 512], BF16) for i in range(2)]
        hst = sb2("hstF", [128, 8])
        hbw = sb2("hbwF", [128, 4, 256])
        yT2 = [sb2("yTF%d" % i, [128, 8, 256], BF16) for i in range(2)]
        ytok2 = [sb2("ytokF%d" % i, [128, 512], BF16) for i in range(2)]
        otmp2 = [sb2("otmpF", [128, 512])] * 2
        bst = sb2("bstF", [128, 2, 4, 6])
        bag = sb2("bagF", [128, 2, 4, 2])
        grs = sb2("grsF", [128, 2, 4])
        wtmp2 = [sb2("wtmpF", [128, 512])] * 2
        L = (sb2("lxwF", [128, 4, 259]), [sb2("xcF%d" % i, [128, 256]) for i in range(2)], [sb2("xcbF%d" % i, [128, 256], BF16) for i in range(2)],
             [sb2("rrF%d" % i, [128, 256]) for i in range(2)], [sb2("iiF%d" % i, [128, 256]) for i in range(2)],
             sb2("a4F", [128, 4, 256]), sb2("v4F", [128, 4, 256]), sb2("t4F", [128, 4, 256]), sb2("hhF", [128, 4, 256]))
        if KDBG:
            print("pass F sbuf remaining", nc.sbuf_bytes_remaining, flush=True)
        winFc = []
        for ci5 in range(5):
            v = Tl(winF.t[:, :, ci5 * 512:(ci5 + 1) * 512], "winF_%d" % ci5)
            winFc.append(v)
            S.dma("pool", I("dma_start", out=v[:, :, :], in_=w_in[:, 512 + ci5 * 512:1024 + ci5 * 512].rearrange("(kc p) n -> p kc n", p=128)), writes=[v])
        S.dma("pool", I("dma_start", out=wo[:], in_=w_out.rearrange("(kc p) n -> p kc n", p=128)), writes=[wo])
        for cc in range(4):
            S.op("pool", I("tensor_scalar", out=wo[:, 4 + cc, :], in0=wo[:, 4 + cc, :], scalar1=gnw[:, cc:cc + 1], scalar2=None, op0=ALU.mult),
                 reads=[wo, gnw], writes=[wo])
        work = []
        for sg in segs:
            T = sg["T"]
            for kc0 in (0, 4):
                S.dma("sp", I("dma_start", out=sg["h2"][kc0:kc0 + 4, :, 0:64].rearrange("k p t -> p k t"),
                              in_=zerob[:, 0:256].rearrange("p (k t) -> p k t", t=64)), reads=[zerob], writes=[sg["h2"]])
                S.dma("sp", I("dma_start", out=sg["h2"][kc0:kc0 + 4, :, T + 64:T + 128].rearrange("k p t -> p k t"),
                              in_=zerob[:, 0:256].rearrange("p (k t) -> p k t", t=64)), reads=[zerob], writes=[sg["h2"]])
            for bi in range(len(sg["blocksBF"])):
                work.append((sg, bi))

        def FE_F(wi):
            sg, bi = work[wi]
            p = wi % 2
            nt, m = sg["ntb"], sg["m"]
            ntt = nt // 128
            t0 = sg["blocksBF"][bi][0]
            xtm, scr, hT = xtm2[p], scr2[p], hT2[p]
            gl, qT, qf, qb, kT, Ktok, Vtok, VFtok, rg = gl2[p], qT2[p], qf2[p], qb2[p], kT2[p], Ktok2[p], Vtok2[p], VFtok2[p], rg2[p]
            S.dma("sp", I("dma_start", out=xtm[:, 0:ntt, :], in_=sg["xin"][t0:t0 + nt, :].rearrange("(tt p) d -> p tt d", p=128)), writes=[xtm])
            yield
            for kc0 in (0, 4):
                S.dma("sp", I("dma_start", out=hT[:, kc0:kc0 + 4, 0:nt], in_=sg["h1"][kc0:kc0 + 4, :, t0:t0 + nt].rearrange("k p t -> p k t")),
                      reads=[sg["h1"]], writes=[hT])
            yield

            def fm_proj(col0, cc):
                pp = PP()
                wv = winFc[col0 // 512]
                fns = [I("matmul", pp[:, 0:nt], lhsT=wv[:, kc, cc * 128:(cc + 1) * 128], rhs=hT[:, kc, 0:nt],
                         start=(kc == 0), stop=(kc == 7)) for kc in range(8)]
                S.op("pe", fns, reads=[wv, hT], writes=[pp])
                return pp

            def tm_proj(col0, tt):
                pp = PP()
                wv = winFc[col0 // 512]
                fns = [I("matmul", pp[:], lhsT=hT[:, kc, tt * 128:(tt + 1) * 128], rhs=wv[:, kc, 0:512],
                         start=(kc == 0), stop=(kc == 7)) for kc in range(8)]
                S.op("pe", fns, reads=[wv, hT], writes=[pp])
                return pp

            for cc in range(4):
                pp = fm_proj(0, cc)
                S.op("act", I("activation", out=gl[:, cc, 0:nt], in_=pp[:, 0:nt], func=GELU), reads=[pp], writes=[gl])
            for h in range(4):
                pp = fm_proj(512, h)
                S.op("act", I("activation", out=qT[:, h, 0:nt], in_=pp[:, 0:nt], func=AF.Copy), reads=[pp], writes=[qT])
                S.op("dve", I("tensor_tensor", out=qf[:, h, 0:nt].rearrange("p (c i) -> p c i", i=128),
                              in0=pp[:, 0:nt].rearrange("p (c i) -> p c i", i=128),
                              in1=HF[:, h:h + 1, :].to_broadcast([128, ntt, 128]), op=ALU.mult), reads=[pp, HF], writes=[qf])
                S.op("dve", I("tensor_tensor", out=qb[:, h, 0:nt].rearrange("p (c i) -> p c i", i=128),
                              in0=pp[:, 0:nt].rearrange("p (c i) -> p c i", i=128),
                              in1=HB[:, h:h + 1, :].to_broadcast([128, ntt, 128]), op=ALU.mult), reads=[pp, HB], writes=[qb])
                yield
            for h in range(4):
                pp = fm_proj(1024, h)
                S.op("act", I("activation", out=kT[:, h, 0:nt], in_=pp[:, 0:nt], func=AF.Copy, scale=128.0 ** -0.5), reads=[pp], writes=[kT])
                yield
            for tt in range(ntt):
                pp = tm_proj(1024, tt)
                S.op("act", I("activation", out=Ktok[:, tt, :], in_=pp[:], func=AF.Copy, scale=128.0 ** -0.5), reads=[pp], writes=[Ktok])
                yield
                pp = tm_proj(1536, tt)
                S.op("act", I("activation", out=Vtok[:, tt, :], in_=pp[:], func=AF.Copy), reads=[pp], writes=[Vtok])
                S.op("dve", I("tensor_tensor", out=VFtok[:, tt, :], in0=pp[:], in1=TFm[:].rearrange("p h e -> p (h e)"), op=ALU.mult),
                     reads=[pp, TFm], writes=[VFtok])
                yield
                pp = tm_proj(2048, tt)
                S.op("act", I("activation", out=rg[:, tt, :], in_=pp[:], func=AF.Tanh, scale=0.5), reads=[pp], writes=[rg])
                S.op("dve", I("scalar_tensor_tensor", out=rg[:, tt, :], in0=rg[:, tt, :], scalar=1.0, in1=pp[:], op0=ALU.add, op1=ALU.mult),
                     reads=[rg, pp], writes=[rg])
                yield

        def BE_lru(wi):
            sg, bi = work[wi]
            p = wi % 2
            nt, m = sg["ntb"], sg["m"]
            ntt = nt // 128
            blocks = sg["blocksBF"]
            t0 = blocks[bi][0]
            xtm = xtm2[p]
            yT = yT2[p]
            gl, qT, qf, qb, kT, Ktok, Vtok, VFtok, rg = gl2[p], qT2[p], qf2[p], qb2[p], kT2[p], Ktok2[p], Vtok2[p], VFtok2[p], rg2[p]
            if bi == 0:
                if sg["kind"] == "s":
                    S.op("pool", I("tensor_copy", out=hst[:], in_=h0[:]), reads=[h0], writes=[hst])
                else:
                    S.op("pool", I("memset", hst[:], 0.0), writes=[hst])
            ret = []
            yield from lru_dir(sg, t0, nt, 0, L, hst, ret)
            hh = ret[0]
            if sg["kind"] == "p" and t0 == 0:
                ob2 = obuf(nlf, "nlf")
                outs_toks.append(S.dma("sp", I("dma_start", out=nlf[sg["idx"]].rearrange("(n p) -> p n", p=128), in_=hh[:, :, 0]),
                                       reads=[hh], writes=[ob2]))
            S.dma("sp", I("dma_start", out=hbw[:, :, 0:nt], in_=sg["hb"][:, :, t0:t0 + nt].rearrange("c p t -> p c t")),
                  reads=[sg["hb"]], writes=[hbw])
            yield
            S.op("pool", I("tensor_tensor", out=hbw[:, :, 0:nt], in0=hbw[:, :, 0:nt], in1=hh[:, :, 0:nt], op=ALU.add), reads=[hbw, hh], writes=[hbw])
            yield
            S.op("dve", I("tensor_tensor", out=yT[:, 0:4, 0:nt], in0=hbw[:, :, 0:nt], in1=gl[:, :, 0:nt], op=ALU.mult), reads=[hbw, gl], writes=[yT])
            yield
        def BE_ret(wi):
            sg, bi = work[wi]
            p = wi % 2
            nt, m = sg["ntb"], sg["m"]
            ntt = nt // 128
            blocks = sg["blocksBF"]
            t0 = blocks[bi][0]
            xtm = xtm2[p]
            yT = yT2[p]
            gl, qT, qf, qb, kT, Ktok, Vtok, VFtok, rg = gl2[p], qT2[p], qf2[p], qb2[p], kT2[p], Ktok2[p], Vtok2[p], VFtok2[p], rg2[p]
            if bi == 0:
                if sg["kind"] == "s":
                    S.dma("sp", I("dma_start", out=Sf[:], in_=st_rf.rearrange("h d e -> d h e")), writes=[Sf])
                else:
                    S.op("pool", I("memset", Sf[:], 0.0), writes=[Sf])
            for tt in range(ntt):
                ci = t0 // 128 + tt
                q2 = ci % 2
                Sbb, Sfb, PTs, ytok, otmp = Sbb2[q2], Sfb2[q2], PTs2[q2], ytok2[q2], otmp2[q2]
                tk = slice(tt * 128, (tt + 1) * 128)
                S.dma("sp", I("dma_start", out=Sbb[:], in_=sg["Sb"][ci]), reads=[sg["Sb"]], writes=[Sbb])
                yield
                S.op("act", I("activation", out=Sfb[:], in_=Sf[:].rearrange("p h e -> p (h e)"), func=AF.Copy), reads=[Sf], writes=[Sfb])
                yield
                ps_ = PP()
                fns = [I("matmul", ps_[:, h * 128:(h + 1) * 128], lhsT=kT[:, h, tk], rhs=qT[:, h, tk], start=True, stop=True) for h in range(4)]
                S.op("pe", fns, reads=[kT, qT], writes=[ps_])
                S.op("dve", I("tensor_tensor", out=PTs[:], in0=ps_[:], in1=Dm[:].rearrange("p h i -> p (h i)"), op=ALU.mult), reads=[ps_, Dm], writes=[PTs])
                yield
                po_ = PP()
                fns = []
                for h in range(4):
                    hs = slice(h * 128, (h + 1) * 128)
                    fns.append(I("matmul", po_[:, hs], lhsT=PTs[:, hs], rhs=Vtok[:, tt, hs], start=True, stop=False))
                    fns.append(I("matmul", po_[:, hs], lhsT=qf[:, h, tk], rhs=Sfb[:, hs], start=False, stop=False))
                    fns.append(I("matmul", po_[:, hs], lhsT=qb[:, h, tk], rhs=Sbb[:, hs], start=False, stop=True))
                S.op("pe", fns, reads=[PTs, Vtok, qf, qb, Sfb, Sbb], writes=[po_])
                S.op("act", I("activation", out=otmp[:], in_=po_[:], func=AF.Copy), reads=[po_], writes=[otmp])
                yield
                pkv = PP()
                fns = [I("matmul", pkv[:, h * 128:(h + 1) * 128], lhsT=Ktok[:, tt, h * 128:(h + 1) * 128],
                         rhs=VFtok[:, tt, h * 128:(h + 1) * 128], start=True, stop=True) for h in range(4)]
                S.op("pe", fns, reads=[Ktok, VFtok], writes=[pkv])
                for h in range(4):
                    S.op("dve", I("scalar_tensor_tensor", out=Sf[:, h, :], in0=Sf[:, h, :], scalar=g128[:, h:h + 1],
                                  in1=pkv[:, h * 128:(h + 1) * 128], op0=ALU.mult, op1=ALU.add), reads=[Sf, g128, pkv], writes=[Sf])
                yield
                for h in range(4):
                    S.op("dve", I("bn_stats", out=bst[:, q2, h, :], in_=otmp[:, h * 128:(h + 1) * 128]), reads=[otmp], writes=[bst])
                    yield
                for h in range(4):
                    S.op("dve", I("bn_aggr", out=bag[:, q2, h, :], in_=bst[:, q2, h, :]), reads=[bst], writes=[bag])
                    yield
                S.op("pool", I("tensor_scalar", out=grs[:, q2, :], in0=bag[:, q2, :, 1], scalar1=EPS, scalar2=None, op0=ALU.add), reads=[bag], writes=[grs])
                S.op("pool", I("tensor_tensor", out=grs[:, q2, :], in0=grs[:, q2, :], in1=mhalf[:, 0:4], op=ALU.pow), reads=[grs, mhalf], writes=[grs])
                yield
                for h in range(4):
                    S.op("dve", I("tensor_scalar", out=otmp[:, h * 128:(h + 1) * 128], in0=otmp[:, h * 128:(h + 1) * 128],
                                  scalar1=bag[:, q2, h, 0:1], scalar2=grs[:, q2, h:h + 1], op0=ALU.subtract, op1=ALU.mult),
                         reads=[otmp, bag, grs], writes=[otmp])
                    yield
                S.op("pool", I("tensor_tensor", out=ytok[:], in0=otmp[:], in1=rg[:, tt, :], op=ALU.mult), reads=[otmp, rg], writes=[ytok])
                yield
                pt = PT[tt % 2]
                fns = [I("transpose", out=pt[:, h * 128:(h + 1) * 128], in_=ytok[:, h * 128:(h + 1) * 128], identity=ident[:]) for h in range(4)]
                S.op("pe", fns, reads=[ytok, ident], writes=[pt])
                S.op("act", I("activation", out=yT[:, 4:8, tk], in_=pt[:, 0:512].rearrange("p (h i) -> p h i", i=128), func=AF.Copy),
                     reads=[pt], writes=[yT])
                yield
            if bi == len(blocks) - 1 and sg["kind"] == "p":
                i = sg["idx"]
                ob = obuf(nrf, "nrf")
                outs_toks.append(S.dma("sp", I("dma_start", out=nrf[i].rearrange("h d e -> d h e"), in_=Sf[:]), reads=[Sf], writes=[ob]))

            yield

        def BE_out(wi):
            sg, bi = work[wi]
            p = wi % 2
            nt, m = sg["ntb"], sg["m"]
            ntt = nt // 128
            blocks = sg["blocksBF"]
            t0 = blocks[bi][0]
            xtm = xtm2[p]
            yT = yT2[p]
            gl, qT, qf, qb, kT, Ktok, Vtok, VFtok, rg = gl2[p], qT2[p], qf2[p], qb2[p], kT2[p], Ktok2[p], Vtok2[p], VFtok2[p], rg2[p]
            k2 = 0
            for tt in range(ntt):
                for nh in range(2):
                    wtmp = wtmp2[k2 % 2]
                    k2 += 1
                    pp = PP()
                    fns = [I("matmul", pp[:], lhsT=yT[:, kc, tt * 128:(tt + 1) * 128], rhs=wo[:, kc, nh * 512:(nh + 1) * 512],
                             start=(kc == 0), stop=(kc == 7)) for kc in range(8)]
                    S.op("pe", fns, reads=[yT, wo], writes=[pp])
                    S.op("dve", I("tensor_tensor", out=wtmp[:], in0=pp[:], in1=g1bc[m][:, nh * 512:(nh + 1) * 512], op=ALU.mult),
                         reads=[pp, g1bc[m]], writes=[wtmp])
                    yield
                    S.op("pool", I("tensor_tensor", out=xtm[:, tt, nh * 512:(nh + 1) * 512], in0=xtm[:, tt, nh * 512:(nh + 1) * 512],
                                   in1=wtmp[:], op=ALU.add), reads=[xtm, wtmp], writes=[xtm])
                    yield
            S.dma("sp", I("dma_start", out=sg["x1"][t0:t0 + nt, :].rearrange("(tt p) d -> p tt d", p=128), in_=xtm[:, 0:ntt, :]),
                  reads=[xtm], writes=[sg["x1"]])
            yield
            yield from norm_mod_T(xtm, ntt, m, 2, h2T, scrBE)
            for kc0 in (0, 4):
                S.dma("sp", I("dma_start", out=sg["h2"][kc0:kc0 + 4, :, 64 + t0:64 + t0 + nt].rearrange("k p t -> p k t"),
                              in_=h2T[:, kc0:kc0 + 4, 0:nt]), reads=[h2T], writes=[sg["h2"]])
                yield
            yield

        run_merged([FE_F(0)])
        nW = len(work)
        for wi in range(nW + 1):
            gens, after = [], {}
            if wi >= 1:
                gens.append(BE_out(wi - 1))
            if wi < nW:
                gens.append(BE_lru(wi))
                gens.append(BE_ret(wi))
                if wi + 1 < nW:
                    gens.append(FE_F(wi + 1))
                    if wi >= 1:
                        after[len(gens) - 1] = 0
            run_merged(gens, after=after)
        S.barrier()
        S.emit()
    if KSTOP == "F":
        esBF.close()
        es.close()
        return nc

    esBF.close()
    with ExitStack() as es2:
        def sb2(name, shape, dtype=F32):
            return Tl(es2.enter_context(nc.sbuf_tensor(name, list(shape), dtype)), name)
        g2bc = [sb2("g2bcG%d" % m, [128, D]) for m in range(2)]
        fnwbc = sb2("fnwbc", [128, D])
        S.dma("sp", I("dma_start", out=fnwbc[:], in_=fnw.partition_broadcast(128)), writes=[fnwbc])
        for m in range(2):
            S.dma("sp", I("dma_start", out=g2bc[m][:], in_=g2d[m].partition_broadcast(128)), reads=[g2d], writes=[g2bc[m]])
        wg = sb2("wg", [128, 8, FH], BF16)
        wu = sb2("wu", [128, 8, FH], BF16)
        wd = sb2("wd", [128, NCC, D], BF16)
        h2w = sb2("h2w", [128, 8, 640], BF16)
        gp = [sb2("gp%d" % i, [128, 660]) for i in range(2)]
        cv = [sb2("cv%d" % i, [128, 512]) for i in range(2)]
        actT = sb2("actT", [128, NCC, 512], BF16)
        x1t = [sb2("x1t%d" % i, [128, D]) for i in range(4)]
        wtmps = [cv[0], cv[1], cv[0], cv[1]]
        junk = sb2("junkG", [128, D], BF16)
        ssqs = sb2("ssqG", [128, 4])
        rstds = sb2("rstdG", [128, 4])
        if KDBG:
            print("pass G sbuf remaining", nc.sbuf_bytes_remaining, flush=True)
        ccb = [0, 6, 12, 17, 22]
        wgc, wuc = [], []
        for k4 in range(4):
            for (lst, w_, src_, nm) in ((wgc, wg, w_gate, "wg"), (wuc, wu, w_up, "wu")):
                c_lo, c_hi = ccb[k4] * 128, ccb[k4 + 1] * 128
                v = Tl(w_.t[:, :, c_lo:c_hi], "%s_c%d" % (nm, k4))
                lst.append(v)
                S.dma("pool", I("dma_start", out=v[:, :, :], in_=src_[:, c_lo:c_hi].rearrange("(kc p) n -> p kc n", p=128)), writes=[v])

        def wsel(lst, cc):
            k4 = max(k for k in range(4) if ccb[k] <= cc)
            lo = (cc - ccb[k4]) * 128
            return lst[k4], slice(lo, lo + 128)

        for c0 in (0, 11):
            S.dma("pool", I("dma_start", out=wd[:, c0:c0 + 11, :], in_=w_down[c0 * 128:(c0 + 11) * 128, :].rearrange("(kc p) n -> p kc n", p=128)),
                  writes=[wd])
        for g_ in gp:
            S.op("pool", I("memset", g_[:], 0.0), writes=[g_])
        blk = 0
        for sg in segs:
            T, nt, m = sg["T"], sg["nt"], sg["m"]
            ntt = nt // 128
            samp = sg["kind"] == "s"
            if not samp:
                for g_ in gp:
                    S.op("pool", I("memset", g_[:], 0.0), writes=[g_])
            for (t0, _) in sg["blocks"]:
                W = nt + 128 if samp else nt
                c0 = t0 if samp else 64
                for kc0 in (0, 4):
                    S.dma("sp", I("dma_start", out=h2w[:, kc0:kc0 + 4, 0:W], in_=sg["h2"][kc0:kc0 + 4, :, c0:c0 + W].rearrange("k p t -> p k t")),
                          reads=[sg["h2"]], writes=[h2w])
                own = 64 if samp else 0

                def G_cc(cc, slot, nt=nt, samp=samp, own=own):
                    g_ = gp[slot]
                    cv_ = cv[slot]
                    cs = slice(cc * 128, (cc + 1) * 128)
                    if samp:
                        p1, p2 = PP(), PP()
                        wgv, wcs = wsel(wgc, cc)
                        fns = [I("matmul", p1[:], lhsT=wgv[:, kc, wcs], rhs=h2w[:, kc, 0:512], start=(kc == 0), stop=(kc == 7)) for kc in range(8)]
                        S.op("pe", fns, reads=[wgv, h2w], writes=[p1])
                        fns = [I("matmul", p2[:, 0:128], lhsT=wgv[:, kc, wcs], rhs=h2w[:, kc, 512:640], start=(kc == 0), stop=(kc == 7)) for kc in range(8)]
                        S.op("pe", fns, reads=[wgv, h2w], writes=[p2])
                        gv = g_[:, 0:660].rearrange("p (r c) -> p r c", c=66)
                        S.op("act", I("activation", out=gv[:, 0:8, 1:65], in_=p1[:].rearrange("p (r c) -> p r c", c=64), func=AF.Copy),
                             reads=[p1], writes=[g_])
                        S.op("act", I("activation", out=gv[:, 8:10, 1:65], in_=p2[:, 0:128].rearrange("p (r c) -> p r c", c=64), func=AF.Copy),
                             reads=[p2], writes=[g_])
                        taps = [(dy, dx) for dy in (-1, 0, 1) for dx in (-1, 0, 1)]
                        cvv = cv_[:, 0:512].rearrange("p (r c) -> p r c", c=64)

                        def gsl(dy, dx):
                            return gv[:, 1 + dy:9 + dy, 1 + dx:65 + dx]
                    else:
                        p1 = PP()
                        wgv, wcs = wsel(wgc, cc)
                        fns = [I("matmul", p1[:, 0:nt], lhsT=wgv[:, kc, wcs], rhs=h2w[:, kc, 0:nt], start=(kc == 0), stop=(kc == 7)) for kc in range(8)]
                        S.op("pe", fns, reads=[wgv, h2w], writes=[p1])
                        gv = g_[:, 0:nt + 2]
                        S.op("act", I("activation", out=gv[:, 1:nt + 1], in_=p1[:, 0:nt], func=AF.Copy), reads=[p1], writes=[g_])
                        taps = [(0, dx) for dx in (-1, 0, 1)]
                        cvv = cv_[:, 0:nt]

                        def gsl(dy, dx):
                            return gv[:, 1 + dx:1 + dx + nt]
                    yield
                    pu = PP()
                    wuv, wcs2 = wsel(wuc, cc)
                    fns = [I("matmul", pu[:, 0:nt], lhsT=wuv[:, kc, wcs2], rhs=h2w[:, kc, own:own + nt], start=(kc == 0), stop=(kc == 7)) for kc in range(8)]
                    S.op("pe", fns, reads=[wuv, h2w], writes=[pu])
                    for ti, (dy, dx) in enumerate(taps):
                        tcol = ((dy + 1) * 3 + (dx + 1)) * NCC + cc
                        if ti == 0:
                            S.op("act", I("activation", out=cvv, in_=gsl(dy, dx), func=AF.Identity, scale=fcw[:, tcol:tcol + 1], bias=fcb[:, cc:cc + 1]),
                                 reads=[g_, fcw, fcb], writes=[cv_])
                        elif cc in POOL_CCS:
                            tmpw = wtmps[slot]
                            tmpv = tmpw[:, 0:512].rearrange("p (r c) -> p r c", c=64) if samp else tmpw[:, 0:nt]
                            S.op("pool", I("tensor_scalar", out=tmpv, in0=gsl(dy, dx), scalar1=fcw[:, tcol:tcol + 1], scalar2=None, op0=ALU.mult),
                                 reads=[g_, fcw], writes=[tmpw])
                            S.op("pool", I("tensor_tensor", out=cvv, in0=cvv, in1=tmpv, op=ALU.add), reads=[cv_, tmpw], writes=[cv_])
                        else:
                            S.op("dve", I("scalar_tensor_tensor",
                                out=cvv, in0=gsl(dy, dx), scalar=fcw[:, tcol:tcol + 1], in1=cvv, op0=ALU.mult, op1=ALU.add),
                                reads=[g_, fcw, cv_], writes=[cv_])
                        yield
                    S.op("act", I("activation", out=cv_[:, 0:nt], in_=cv_[:, 0:nt], func=GELU), reads=[cv_], writes=[cv_])
                    yield
                    S.op("dve", I("tensor_tensor", out=actT[:, cc, 0:nt], in0=pu[:, 0:nt], in1=cv_[:, 0:nt], op=ALU.mult),
                         reads=[pu, cv_], writes=[actT])
                    yield

                for cc in range(0, NCC, 2):
                    run_merged([G_cc(cc, 0), G_cc(cc + 1, 1)])
                def G_down(tt, slot, sg=sg, t0=t0, m=m):
                    xt = x1t[slot]
                    wtmp = wtmps[slot]
                    S.dma("sp", I("dma_start", out=xt[:], in_=sg["x1"][t0 + tt * 128:t0 + (tt + 1) * 128, :]), reads=[sg["x1"]], writes=[xt])
                    yield
                    for nh in range(2):
                        pp = PP()
                        fns = [I("matmul", pp[:], lhsT=actT[:, cc, tt * 128:(tt + 1) * 128], rhs=wd[:, cc, nh * 512:(nh + 1) * 512],
                                 start=(cc == 0), stop=(cc == NCC - 1)) for cc in range(NCC)]
                        S.op("pe", fns, reads=[actT, wd], writes=[pp])
                        S.op("dve", I("tensor_tensor", out=wtmp[:], in0=pp[:], in1=g2bc[m][:, nh * 512:(nh + 1) * 512], op=ALU.mult),
                             reads=[pp, g2bc[m]], writes=[wtmp])
                        S.op("pool", I("tensor_tensor", out=xt[:, nh * 512:(nh + 1) * 512], in0=xt[:, nh * 512:(nh + 1) * 512], in1=wtmp[:], op=ALU.add),
                             reads=[xt, wtmp], writes=[xt])
                        yield
                    S.op("act", I("activation", out=junk[:, 0:D], in_=xt[:], func=AF.Square, accum_out=ssqs[:, slot:slot + 1]), reads=[xt], writes=[junk, ssqs])
                    yield
                    S.op("dve", I("tensor_scalar", out=rstds[:, slot:slot + 1], in0=ssqs[:, slot:slot + 1], scalar1=1.0 / D, scalar2=EPS, op0=ALU.mult, op1=ALU.add),
                         reads=[ssqs], writes=[rstds])
                    S.op("pool", I("tensor_tensor", out=rstds[:, slot:slot + 1], in0=rstds[:, slot:slot + 1], in1=mhalf[:, 0:1], op=ALU.pow),
                         reads=[rstds, mhalf], writes=[rstds])
                    yield
                    S.op("dve", I("scalar_tensor_tensor", out=xt[:], in0=xt[:], scalar=rstds[:, slot:slot + 1], in1=fnwbc[:], op0=ALU.mult, op1=ALU.mult),
                         reads=[xt, rstds, fnwbc], writes=[xt])
                    yb = sg["yb"][slot % 2]
                    S.dma("sp", I("dma_start", out=sg["yout"][t0 + tt * 128:t0 + (tt + 1) * 128, :], in_=xt[:]), reads=[xt], writes=[yb])
                    yield

                run_merged([G_down(tt, tt) for tt in range(ntt)])
        for sg in segs:
            for yb in sg["yb"]:
                outs_toks.append(yb.b.wtok)
        waits = []
        for t in outs_toks:
            S._need("sp", t, waits)
        S.ops["sp"].append(("op", waits, [], None))
        S.barrier()
        S.emit()

    es.close()
    return nc


_NC_CACHE = {}


def kernel(**inputs):
    x_prompt = np.ascontiguousarray(inputs["x_prompt"], dtype=np.float32)
    x_sample = np.ascontiguousarray(inputs["x_sample"], dtype=np.float32)
    BP, TP, _ = x_prompt.shape
    BS, TS, _ = x_sample.shape
    ncores = 8
    NP = BP // ncores
    assert BS == ncores
    key = (TS, NP, TP)
    if key not in _NC_CACHE:
        _NC_CACHE[key] = build_nc(TS, NP, TP)
    nc = _NC_CACHE[key]

    def f(n):
        return np.ascontiguousarray(inputs[n], dtype=np.float32)

    shared = {
        "c_ctx": f("c_ctx"), "norm1_w": f("norm1_w")[0], "w_mod": f("w_mod")[0], "b_mod": f("b_mod")[0], "w_in": f("w_in")[0],
        "lru_conv_w": f("lru_conv_w")[0], "lru_conv_b": f("lru_conv_b")[0],
        "lru_wa_fw": f("lru_wa_fw")[0], "lru_wx_fw": f("lru_wx_fw")[0], "lru_wa_bw": f("lru_wa_bw")[0], "lru_wx_bw": f("lru_wx_bw")[0],
        "lru_ba_fw": f("lru_ba_fw")[0], "lru_bx_fw": f("lru_bx_fw")[0], "lru_ba_bw": f("lru_ba_bw")[0], "lru_bx_bw": f("lru_bx_bw")[0],
        "lru_lambda_fw": f("lru_lambda_fw")[0], "lru_lambda_bw": f("lru_lambda_bw")[0],
        "ret_decay_fw": f("ret_decay_fw")[0], "ret_decay_bw": f("ret_decay_bw")[0], "ret_gn_w": f("ret_gn_w")[0],
        "w_out": f("w_out")[0], "norm2_w": f("norm2_w")[0], "ffn_w_gate": f("ffn_w_gate")[0], "ffn_w_up": f("ffn_w_up")[0],
        "ffn_conv_w": f("ffn_conv_w")[0].reshape(9, FH), "ffn_conv_b": f("ffn_conv_b")[0], "ffn_w_down": f("ffn_w_down")[0],
        "final_norm_w": f("final_norm_w"),
    }
    shared = {k: np.ascontiguousarray(v) for k, v in shared.items()}
    in_maps = []
    for ci in range(ncores):
        mp = dict(shared)
        mp["xs"] = x_sample[ci]
        mp["xp"] = np.ascontiguousarray(x_prompt[ci * NP:(ci + 1) * NP].reshape(NP * TP, D))
        mp["st_lf"] = np.ascontiguousarray(f("state_lru_fw")[ci, 0])
        mp["st_lb"] = np.ascontiguousarray(f("state_lru_bw")[ci, 0])
        mp["st_rf"] = np.ascontiguousarray(f("state_ret_fw")[ci, 0])
        mp["st_rb"] = np.ascontiguousarray(f("state_ret_bw")[ci, 0])
        mp["c"] = np.ascontiguousarray(f("c")[ci])
        in_maps.append(mp)
    res = run_bass_kernel_spmd(nc, in_maps, core_ids=list(range(ncores)))
    R = res.results
    y_prompt = np.concatenate([r["yp"].reshape(NP, TP, D) for r in R], axis=0)
    y_sample = np.stack([r["ys"] for r in R], axis=0)
    new_lf = np.concatenate([r["nlf"].reshape(NP, 1, LW) for r in R], axis=0)
    new_lb = np.concatenate([r["nlb"].reshape(NP, 1, LW) for r in R], axis=0)
    new_rf = np.concatenate([r["nrf"].reshape(NP, 1, 4, 128, 128) for r in R], axis=0)
    new_rb = np.concatenate([r["nrb"].reshape(NP, 1, 4, 128, 128) for r in R], axis=0)
    return (y_prompt.astype(np.float32), y_sample.astype(np.float32), new_lf.astype(np.float32), new_lb.astype(np.float32),
            new_rf.astype(np.float32), new_rb.astype(np.float32))
```

```python
import numpy as np
from contextlib import ExitStack
import concourse.bass as bass
import concourse.mybir as mybir
from concourse.bass_utils import run_bass_kernel_spmd

F32 = mybir.dt.float32
BF16 = mybir.dt.bfloat16
AF = mybir.ActivationFunctionType
ALU = mybir.AluOpType
D = 1024
LW = 512
FH = 2816
NCC = FH // 128
EPS = 1e-6
GW = 64
DEFAULT_FILL = 0
DEFAULT_FILLN = 512
DEFAULT_POOLCC = ""


_UNIQ = [0]


class Buf:
    def __init__(self, name):
        _UNIQ[0] += 1
        self.name = "%s_%d" % (name, _UNIQ[0])
        self.wtok = None
        self.rtoks = {}
        self.dsem = None
        self.dcnt = 0
        self.excl = False


class Tl:
    def __init__(self, t, name):
        self.t = t
        self.b = Buf(name)

    def __getitem__(self, k):
        return self.t[k]


def _b(x):
    return x.b if isinstance(x, Tl) else x


def I(name, *args, **kwargs):
    return lambda e: getattr(e, name)(*args, **kwargs)


class Sched:
    def __init__(self, nc, es):
        self.nc = nc
        self.es = es
        self.engs = ["pe", "act", "dve", "pool", "sp"]
        self.ops = {e: [] for e in self.engs}
        self.sem = {}
        self.cnt = {e: 0 for e in self.engs}
        self.seen = {e: {} for e in self.engs}
        self.dbufs = []
        self.fill_n = 0
        self.fill_fn = None
        for e in self.engs:
            self.sem[e] = es.enter_context(nc.semaphore("prog_" + e))

    def _need(self, eng, tok, waits):
        if tok is None:
            return
        key, val = tok
        if self.seen[eng].get(key, 0) >= val:
            return
        self.seen[eng][key] = val
        waits.append(tok)

    def _deps(self, eng, reads, writes):
        waits = []
        for b in reads:
            self._need(eng, b.wtok, waits)
        for b in writes:
            self._need(eng, b.wtok, waits)
            for t in b.rtoks.values():
                self._need(eng, t, waits)
        return waits

    def op(self, eng, fns, reads=(), writes=()):
        reads = [_b(x) for x in reads]
        writes = [_b(x) for x in writes]
        writes = writes + [b for b in reads if b.excl and b not in writes]
        reads = [b for b in reads if not b.excl]
        if callable(fns):
            fns = [fns]
        waits = self._deps(eng, reads, writes)
        self.cnt[eng] += 1
        tok = (eng, self.cnt[eng])
        for b in reads:
            b.rtoks[eng] = tok
        for b in writes:
            b.wtok = tok
            b.rtoks = {}
        pre = [self.fill_fn] * self.fill_n if (eng == "pe" and self.fill_n > 0 and self.fill_fn is not None and fns) else []
        self.ops[eng].append(("op", waits, fns, pre))
        return tok

    def dma(self, eng, fn, reads=(), writes=()):
        reads = [_b(x) for x in reads]
        writes = [_b(x) for x in writes]
        waits = self._deps(eng, reads, writes)
        owner = writes[0]
        if owner.dsem is None:
            owner.dsem = self.es.enter_context(self.nc.semaphore("d_" + owner.name))
            self.sem["d_" + owner.name] = owner.dsem
            self.dbufs.append(owner)
        owner.dcnt += 16
        tok = ("d_" + owner.name, owner.dcnt)
        for b in reads:
            b.rtoks["d_" + owner.name] = tok
        for b in writes:
            b.wtok = tok
            b.rtoks = {}
        self.ops[eng].append(("dma", waits, [fn], owner))
        return tok

    def wait_for(self, eng, bufs):
        waits = []
        for b in bufs:
            self._need(eng, _b(b).wtok, waits)
        self.ops[eng].append(("op", waits, [], None))

    def barrier(self):
        toks = [(e, self.cnt[e]) for e in self.engs if self.cnt[e] > 0]
        toks += [("d_" + b.name, b.dcnt) for b in self.dbufs]
        for e in self.engs:
            waits = []
            for t in toks:
                self._need(e, t, waits)
            self.ops[e].append(("op", waits, [], None))

    def emit(self):
        nc = self.nc
        import os
        if os.environ.get("KDBG"):
            print("emit: ops per engine", {e: sum(len(o[2]) for o in self.ops[e]) for e in self.engs},
                  "waits", {e: sum(len(o[1]) for o in self.ops[e]) for e in self.engs}, "cnt", dict(self.cnt), "nsem", len(self.sem), flush=True)
        with nc.allow_non_contiguous_dma(reason="small parameter / state layouts"), nc.Block() as block:
            def replay(e, name):
                for kind, waits, fns, owner in self.ops[name]:
                    if kind == "op" and owner:
                        for f in owner:
                            f(e)
                    for key, val in waits:
                        e.wait_ge(self.sem[key], val)
                    ins = None
                    for f in fns:
                        ins = f(e)
                    if ins is not None:
                        if kind == "dma":
                            ins.then_inc(owner.dsem, 16)
                        else:
                            ins.then_inc(self.sem[name], 1)

            @block.tensor
            def _(e):
                replay(e, "pe")

            @block.scalar
            def _(e):
                replay(e, "act")

            @block.vector
            def _(e):
                replay(e, "dve")

            @block.gpsimd
            def _(e):
                replay(e, "pool")

            @block.sync
            def _(e):
                replay(e, "sp")
        self.ops = {e: [] for e in self.engs}


def build_nc(TS=4096, NP=4, TP=256, dbg=False):
    import os
    KSTOP = os.environ.get("KSTOP", "")
    FLVL = int(os.environ.get("FLVL", "9"))
    KDBG = bool(os.environ.get("KDBG"))
    GELU = AF.Gelu if os.environ.get("GELU") == "erf" else AF.Gelu_apprx_tanh
    NOQ = os.environ.get("NOQ") == "1"
    POOL_CCS = set(int(x) for x in os.environ.get("POOLCC", DEFAULT_POOLCC).split(",") if x)
    FSUB = os.environ.get("FSUB", "gqkt")
    nc = bass.Bass("TRN2", target_bir_lowering=False)

    def inp(name, shape):
        return nc.dram_tensor(name, list(shape), F32, kind="ExternalInput").ap()

    def outp(name, shape):
        return nc.dram_tensor(name, list(shape), F32, kind="ExternalOutput").ap()

    xs = inp("xs", [TS, D])
    xp = inp("xp", [NP * TP, D])
    st_lf = inp("st_lf", [LW])
    st_lb = inp("st_lb", [LW])
    st_rf = inp("st_rf", [4, 128, 128])
    st_rb = inp("st_rb", [4, 128, 128])
    c_in = inp("c", [D])
    cx_in = inp("c_ctx", [D])
    norm1_w = inp("norm1_w", [D])
    w_mod = inp("w_mod", [D, 6 * D])
    b_mod = inp("b_mod", [6 * D])
    w_in = inp("w_in", [D, 3072])
    lru_conv_w = inp("lru_conv_w", [4, LW])
    lru_conv_b = inp("lru_conv_b", [LW])
    lru_w = [inp(n, [8, 64, 64]) for n in ("lru_wa_fw", "lru_wx_fw", "lru_wa_bw", "lru_wx_bw")]
    lru_bias = [inp(n, [LW]) for n in ("lru_ba_fw", "lru_bx_fw", "lru_ba_bw", "lru_bx_bw")]
    lam_f = inp("lru_lambda_fw", [LW])
    lam_b = inp("lru_lambda_bw", [LW])
    dec_f = inp("ret_decay_fw", [4])
    dec_b = inp("ret_decay_bw", [4])
    ret_gn_w = inp("ret_gn_w", [LW])
    w_out = inp("w_out", [D, D])
    norm2_w = inp("norm2_w", [D])
    w_gate = inp("ffn_w_gate", [D, FH])
    w_up = inp("ffn_w_up", [D, FH])
    ffn_conv_w = inp("ffn_conv_w", [9, FH])
    ffn_conv_b = inp("ffn_conv_b", [FH])
    w_down = inp("ffn_w_down", [FH, D])
    fnw = inp("final_norm_w", [D])

    ys = outp("ys", [TS, D])
    yp = outp("yp", [NP * TP, D])
    nlf = outp("nlf", [NP, LW])
    nlb = outp("nlb", [NP, LW])
    nrf = outp("nrf", [NP, 4, 128, 128])
    nrb = outp("nrb", [NP, 4, 128, 128])

    es = ExitStack()
    S = Sched(nc, es)
    outs_toks = []
    OB = {}

    def obuf(ap, name):
        if name not in OB:
            OB[name] = Tl(ap, name)
        return OB[name]

    def sb(name, shape, dtype=F32):
        return Tl(es.enter_context(nc.sbuf_tensor(name, list(shape), dtype)), name)

    esBF = ExitStack()
    esS = ExitStack()

    def sbS(name, shape, dtype=F32):
        return Tl(esS.enter_context(nc.sbuf_tensor(name, list(shape), dtype)), name)

    def sbBF(name, shape, dtype=F32):
        return Tl(esBF.enter_context(nc.sbuf_tensor(name, list(shape), dtype)), name)

    def ps(name, shape, dtype=F32):
        t = Tl(es.enter_context(nc.psum_tensor(name, list(shape), dtype)), name)
        t.b.excl = True
        return t

    def dr(name, shape, dtype=F32):
        return Tl(nc.dram_tensor(name, list(shape), dtype, kind="Internal").ap(), name)

    segs = [dict(name="s", kind="s", T=TS, nt=512, xin=xs, yout=ys, m=0, idx=0)]
    for i in range(NP):
        segs.append(dict(name="p%d" % i, kind="p", T=TP, nt=TP, xin=xp[i * TP:(i + 1) * TP, :],
                         yout=yp[i * TP:(i + 1) * TP, :], m=1, idx=i))
    for sg in segs:
        T = sg["T"]
        n = sg["name"]
        sg["lx"] = dr("lx_" + n, [4, 128, T + 3])
        sg["h1"] = dr("h1_" + n, [8, 128, T], BF16)
        sg["h1"].b = sg["lx"].b
        sg["hb"] = dr("hb_" + n, [4, 128, T])
        sg["Sb"] = dr("Sb_" + n, [T // 128, 128, 512], BF16)
        sg["x1"] = dr("x1_" + n, [T, D])
        sg["h2"] = dr("h2_" + n, [8, 128, T + 128], BF16)
        sg["yb"] = [Tl(sg["yout"], "y_%s_%d" % (n, k)) for k in range(2)]
        sg["blocks"] = [(t0, sg["nt"]) for t0 in range(0, T, sg["nt"])]
        sg["ntb"] = min(256, sg["nt"])
        sg["blocksBF"] = [(t0, sg["ntb"]) for t0 in range(0, T, sg["ntb"])]
        sg["ntB"] = sg["nt"]
        sg["blocksB"] = [(t0, sg["ntB"]) for t0 in range(0, T, sg["ntB"])]

    g2d = dr("g2d", [2, D])
    PB = [ps("pb%d" % i, [128, 512]) for i in range(6)]
    PT = [ps("ptA", [128, 1024], BF16), ps("ptB", [128, 1024], BF16)]
    pp_rr = [0, 6 if int(os.environ.get("FILL", str(DEFAULT_FILL))) == 0 else 5]
    PJ = PB[5]

    def PP():
        k = pp_rr[0] % pp_rr[1]
        pp_rr[0] += 1
        return PB[k]

    PS_, PO_, PKV_ = PB[3], PB[4], PB[5]

    ident = sb("ident", [128, 128], BF16)
    zerof = sb("zerof", [128, 512])
    zerob = sb("zerob", [128, 512], BF16)
    epsc = sb("epsc", [128, 1])
    mhalf = sb("mhalf", [128, 8])
    qtr = sb("qtr", [128, 1])
    fcw = sb("fcw", [128, 9 * NCC])
    fcb = sb("fcb", [128, NCC])
    AB = sb("AB", [128, 4, 2, 8])
    cw = sbBF("cw", [128, 16])
    cb = sbBF("cb", [128, 4])
    lbias = sbBF("lbias", [128, 16])
    cf = sbBF("cf", [128, 8])
    gnw = sbBF("gnw", [128, 4])
    g1bc = [sbBF("g1bc%d" % m, [128, D]) for m in range(2)]
    Wbd = sbBF("Wbd", [128, 16, 128], BF16)
    g128 = sbBF("g128", [128, 8])
    Dm = sbBF("Dm", [128, 4, 128])
    HF = sbBF("HF", [128, 4, 128])
    HB = sbBF("HB", [128, 4, 128])
    TFm = sbBF("TFm", [128, 4, 128])
    TBm = sbBF("TBm", [128, 4, 128])
    h0 = sbBF("h0", [128, 8])
    g2bc = [sbS("g2bc%d" % m, [128, D]) for m in range(2)]
    identf = sbS("identf", [128, 128])
    ones_bf = sbS("ones_bf", [128, 128], BF16)
    onesf = sbS("onesf", [128, 128])
    n1w = sbS("n1w", [128, 8])
    n2w = sbS("n2w", [128, 8])
    bm = sbS("bm", [128, 48])
    lam = sbS("lam", [128, 8])
    cT = sbS("cT", [128, 16])
    scf = sbS("scf", [128, 16])
    scT = sbS("scT", [128, 8, 2], BF16)
    bcT = sbS("bcT", [128, 16, 128], BF16)
    modF = sbS("modF", [128, 32, 2])
    bg = sbS("bg", [128, 2, D])
    dec = sbS("dec", [128, 8])
    lg = sbS("lg", [128, 8])
    rel = sbS("rel", [128, 128])
    tmpA = sbS("tmpA", [128, 128])
    tmpB = sbS("tmpB", [128, 128])
    mge = sbS("mge", [128, 128])
    mle = sbS("mle", [128, 128])
    pj = sbS("pj", [128, 2])
    tl8 = sbS("tl8", [128, 8])

    def ldcol(dst, dst_ap, src1d):
        S.dma("sp", I("dma_start", out=dst_ap, in_=src1d.rearrange("(n p) -> p n", p=128)), writes=[dst])

    S.op("pool", I("memset", identf[:], 0.0), writes=[identf])
    S.op("pool", I("affine_select", out=identf[:], in_=identf[:], pattern=[[-1, 128]], compare_op=ALU.not_equal,
                                           fill=1.0, base=0, channel_multiplier=1), reads=[identf], writes=[identf])
    S.op("dve", I("tensor_copy", out=ident[:], in_=identf[:]), reads=[identf], writes=[ident])
    S.op("pool", I("memset", onesf[:], 1.0), writes=[onesf])
    S.op("pool", I("memset", zerof[:], 0.0), writes=[zerof])
    S.op("pool", I("memset", zerob[:], 0.0), writes=[zerob])
    S.op("pool", I("memset", epsc[:], EPS), writes=[epsc])
    S.op("pool", I("memset", mhalf[:], -0.5), writes=[mhalf])
    S.op("pool", I("memset", qtr[:], 0.25), writes=[qtr])

    ldcol(n1w, n1w[:], norm1_w)
    ldcol(n2w, n2w[:], norm2_w)
    ldcol(bm, bm[:], b_mod)
    for j in range(4):
        ldcol(cw, cw[:, j * 4:(j + 1) * 4], lru_conv_w[j])
    ldcol(cb, cb[:], lru_conv_b)
    for k in range(4):
        ldcol(lbias, lbias[:, k * 4:(k + 1) * 4], lru_bias[k])
    ldcol(lam, lam[:, 0:4], lam_f)
    ldcol(lam, lam[:, 4:8], lam_b)
    for tp in range(9):
        ldcol(fcw, fcw[:, tp * NCC:(tp + 1) * NCC], ffn_conv_w[tp])
    ldcol(fcb, fcb[:], ffn_conv_b)
    ldcol(gnw, gnw[:], ret_gn_w)
    ldcol(cT, cT[:, 0:8], c_in)
    ldcol(cT, cT[:, 8:16], cx_in)
    ldcol(h0, h0[:, 0:4], st_lf)
    ldcol(h0, h0[:, 4:8], st_lb)
    S.dma("sp", I("dma_start", out=bg[:, 0, :], in_=b_mod[2 * D:3 * D].partition_broadcast(128)), writes=[bg])
    S.dma("sp", I("dma_start", out=bg[:, 1, :], in_=b_mod[5 * D:6 * D].partition_broadcast(128)), writes=[bg])
    S.dma("sp", I("dma_start", out=dec[:, 0:4], in_=dec_f.partition_broadcast(128)), writes=[dec])
    S.dma("sp", I("dma_start", out=dec[:, 4:8], in_=dec_b.partition_broadcast(128)), writes=[dec])

    S.op("pool", I("memset", Wbd[:], 0.0), writes=[Wbd])
    for mi in range(4):
        for half in range(2):
            src = lru_w[mi].rearrange("(n two) c d -> two c n d", two=2)[half]
            S.dma("pool", I("dma_start",
                out=Wbd[half * 64:(half + 1) * 64, mi * 4:(mi + 1) * 4, half * 64:(half + 1) * 64], in_=src), writes=[Wbd])

    S.op("act", I("activation", out=cf[:], in_=lam[:], func=AF.Exp, scale=-1.0), reads=[lam], writes=[cf])
    S.op("act", I("activation", out=cf[:], in_=cf[:], func=AF.Ln, bias=1.0), reads=[cf], writes=[cf])
    S.op("dve", I("tensor_scalar", out=cf[:], in0=cf[:], scalar1=-4.0, scalar2=None, op0=ALU.mult), reads=[cf], writes=[cf])
    S.op("dve", I("tensor_scalar", out=lbias[:], in0=lbias[:], scalar1=0.5, scalar2=None, op0=ALU.mult), reads=[lbias], writes=[lbias])
    S.op("dve", I("tensor_scalar", out=gnw[:], in0=gnw[:], scalar1=0.5, scalar2=None, op0=ALU.mult), reads=[gnw], writes=[gnw])
    S.op("act", I("activation", out=lg[:], in_=dec[:], func=AF.Exp, scale=-1.0), reads=[dec], writes=[lg])
    S.op("act", I("activation", out=lg[:], in_=lg[:], func=AF.Ln, bias=1.0), reads=[lg], writes=[lg])
    S.op("dve", I("tensor_scalar", out=lg[:], in0=lg[:], scalar1=-1.0, scalar2=None, op0=ALU.mult), reads=[lg], writes=[lg])
    S.op("act", I("activation", out=g128[:], in_=lg[:], func=AF.Exp, scale=128.0), reads=[lg], writes=[g128])
    S.op("pool", I("iota", rel[:], pattern=[[1, 128]], base=0, channel_multiplier=-1,
                                  allow_small_or_imprecise_dtypes=True), writes=[rel])
    S.op("dve", I("tensor_scalar", out=mge[:], in0=rel[:], scalar1=0.0, scalar2=None, op0=ALU.is_ge), reads=[rel], writes=[mge])
    S.op("dve", I("tensor_scalar", out=mle[:], in0=rel[:], scalar1=0.0, scalar2=None, op0=ALU.is_le), reads=[rel], writes=[mle])
    for h in range(4):
        S.op("dve", I("tensor_scalar", out=tmpA[:], in0=rel[:], scalar1=0.0, scalar2=None, op0=ALU.max), reads=[rel], writes=[tmpA])
        S.op("act", I("activation", out=tmpA[:], in_=tmpA[:], func=AF.Exp, scale=lg[:, h:h + 1]), reads=[tmpA, lg], writes=[tmpA])
        S.op("dve", I("tensor_tensor", out=tmpA[:], in0=tmpA[:], in1=mge[:], op=ALU.mult), reads=[tmpA, mge], writes=[tmpA])
        S.op("dve", I("tensor_scalar", out=tmpB[:], in0=rel[:], scalar1=-1.0, scalar2=0.0, op0=ALU.mult, op1=ALU.max), reads=[rel], writes=[tmpB])
        S.op("act", I("activation", out=tmpB[:], in_=tmpB[:], func=AF.Exp, scale=lg[:, 4 + h:5 + h]), reads=[tmpB, lg], writes=[tmpB])
        S.op("dve", I("tensor_tensor", out=tmpB[:], in0=tmpB[:], in1=mle[:], op=ALU.mult), reads=[tmpB, mle], writes=[tmpB])
        S.op("dve", I("tensor_tensor", out=Dm[:, h, :], in0=tmpA[:], in1=tmpB[:], op=ALU.add), reads=[tmpA, tmpB], writes=[Dm])
    S.op("pool", I("iota", tmpA[:], pattern=[[1, 128]], base=1, channel_multiplier=0,
                                  allow_small_or_imprecise_dtypes=True), reads=[Dm], writes=[tmpA])
    S.op("pool", I("iota", tmpB[:], pattern=[[-1, 128]], base=128, channel_multiplier=0,
                                  allow_small_or_imprecise_dtypes=True), reads=[Dm], writes=[tmpB])
    for h in range(4):
        S.op("act", I("activation", out=HF[:, h, :], in_=tmpA[:], func=AF.Exp, scale=lg[:, h:h + 1]), reads=[tmpA, lg], writes=[HF])
        S.op("act", I("activation", out=HB[:, h, :], in_=tmpB[:], func=AF.Exp, scale=lg[:, 4 + h:5 + h]), reads=[tmpB, lg], writes=[HB])
    S.op("pool", I("iota", pj[:, 0:1], pattern=[[0, 1]], base=127, channel_multiplier=-1,
                                  allow_small_or_imprecise_dtypes=True), writes=[pj])
    S.op("pool", I("iota", pj[:, 1:2], pattern=[[0, 1]], base=0, channel_multiplier=1,
                                  allow_small_or_imprecise_dtypes=True), reads=[pj], writes=[pj])
    S.op("dve", I("tensor_scalar", out=tl8[:, 0:4], in0=lg[:, 0:4], scalar1=pj[:, 0:1], scalar2=None, op0=ALU.mult), reads=[lg, pj], writes=[tl8])
    S.op("dve", I("tensor_scalar", out=tl8[:, 4:8], in0=lg[:, 4:8], scalar1=pj[:, 1:2], scalar2=None, op0=ALU.mult), reads=[lg, pj, tl8], writes=[tl8])
    S.op("act", I("activation", out=tl8[:], in_=tl8[:], func=AF.Exp), reads=[tl8], writes=[tl8])
    for h in range(4):
        S.op("dve", I("tensor_scalar", out=TFm[:, h, :], in0=onesf[:], scalar1=tl8[:, h:h + 1], scalar2=None, op0=ALU.mult), reads=[onesf, tl8], writes=[TFm])
        S.op("dve", I("tensor_scalar", out=TBm[:, h, :], in0=onesf[:], scalar1=tl8[:, 4 + h:5 + h], scalar2=None, op0=ALU.mult), reads=[onesf, tl8], writes=[TBm])

    S.wait_for("pe", [ident, zerob])
    S.fill_n = int(os.environ.get("FILL", str(DEFAULT_FILL)))
    FILLN = int(os.environ.get("FILLN", str(DEFAULT_FILLN)))
    S.fill_fn = I("matmul", PJ[:, 0:FILLN], lhsT=ident[:], rhs=zerob[:, 0:FILLN], start=True, stop=True)
    S.op("act", I("activation", out=scf[:], in_=cT[:], func=AF.Silu), reads=[cT], writes=[scf])
    for m in range(2):
        S.op("dve", I("tensor_copy", out=scT[:, :, m], in_=scf[:, m * 8:(m + 1) * 8]), reads=[scf], writes=[scT])
    for m in range(2):
        for kc in range(8):
            S.op("dve", I("tensor_scalar", out=bcT[:, m * 8 + kc, :], in0=onesf[:], scalar1=scf[:, m * 8 + kc:m * 8 + kc + 1],
                                                              scalar2=None, op0=ALU.mult), reads=[onesf, scf], writes=[bcT])
    if True:
        wm = [sbS("wm%d" % i, [128, 8, D], BF16) for i in range(2)]
        PM = PB[3]
        fidx = {0: 0, 1: 1, 3: 2, 4: 3}
        for g in range(6):
            w = wm[g % 2]
            S.dma("pool", I("dma_start", out=w[:], in_=w_mod[:, g * D:(g + 1) * D].rearrange("(kc p) n -> p kc n", p=128)), writes=[w])
            if g in fidx:
                fi = fidx[g]
                fns = []
                for ncx in range(8):
                    for kc in range(8):
                        fns.append(I("matmul",
                            PM[:, (fi * 8 + ncx) * 2:(fi * 8 + ncx) * 2 + 2], lhsT=w[:, kc, ncx * 128:(ncx + 1) * 128], rhs=scT[:, kc, :],
                            start=(kc == 0), stop=(kc == 7)))
                S.op("pe", fns, reads=[w, scT], writes=[PM])
                for m in range(2):
                    S.op("dve", I("tensor_tensor",
                        out=modF[:, fi * 8:(fi + 1) * 8, m], in0=PM[:, fi * 16:(fi + 1) * 16].rearrange("p (n m) -> p n m", m=2)[:, :, m],
                        in1=bm[:, g * 8:(g + 1) * 8], op=ALU.add), reads=[PM, bm], writes=[modF])
            else:
                gi = 0 if g == 2 else 1
                dst = g1bc if g == 2 else g2bc
                for m in range(2):
                    for nh in range(2):
                        pp = PP()
                        fns = [I("matmul",
                            pp[:], lhsT=bcT[:, m * 8 + kc, :], rhs=w[:, kc, nh * 512:(nh + 1) * 512], start=(kc == 0), stop=(kc == 7))
                            for kc in range(8)]
                        S.op("pe", fns, reads=[w, bcT], writes=[pp])
                        S.op("dve", I("tensor_tensor",
                            out=dst[m][:, nh * 512:(nh + 1) * 512], in0=pp[:], in1=bg[:, gi, nh * 512:(nh + 1) * 512], op=ALU.add),
                            reads=[pp, bg], writes=[dst[m]])
        for m in range(2):
            S.op("dve", I("scalar_tensor_tensor", out=AB[:, 0, m, :], in0=modF[:, 8:16, m], scalar=1.0, in1=n1w[:],
                                                              op0=ALU.add, op1=ALU.mult), reads=[modF, n1w], writes=[AB])
            S.op("dve", I("tensor_copy", out=AB[:, 1, m, :], in_=modF[:, 0:8, m]), reads=[modF], writes=[AB])
            S.op("dve", I("scalar_tensor_tensor", out=AB[:, 2, m, :], in0=modF[:, 24:32, m], scalar=1.0, in1=n2w[:],
                                                              op0=ALU.add, op1=ALU.mult), reads=[modF, n2w], writes=[AB])
            S.op("dve", I("tensor_copy", out=AB[:, 3, m, :], in_=modF[:, 16:24, m]), reads=[modF], writes=[AB])
        for m in range(2):
            S.dma("sp", I("dma_start", out=g2d[m:m + 1, :], in_=g2bc[m][0:1, :]), reads=[g2bc[m]], writes=[g2d])
        S.barrier()
        S.emit()
    esS.close()
    if KSTOP == "setup":
        esBF.close()
        es.close()
        return nc

    def norm_mod_T(xtm, ntt, m, which, hT, scr):
        ssq, rstd, xn, junk = scr
        for tt in range(ntt):
            S.op("act", I("activation", out=junk[:], in_=xtm[:, tt, :], func=AF.Square, accum_out=ssq[:, tt:tt + 1]),
                 reads=[xtm], writes=[junk, ssq])
        S.op("dve", I("tensor_scalar", out=rstd[:, 0:ntt], in0=ssq[:, 0:ntt], scalar1=1.0 / D, scalar2=EPS, op0=ALU.mult, op1=ALU.add),
             reads=[ssq], writes=[rstd])
        S.op("pool", I("tensor_tensor", out=rstd[:, 0:ntt], in0=rstd[:, 0:ntt], in1=mhalf[:, 0:ntt], op=ALU.pow), reads=[rstd, mhalf], writes=[rstd])
        yield
        for tt in range(ntt):
            S.op("act", I("activation", out=xn[:, tt, :], in_=xtm[:, tt, :], func=AF.Copy, scale=rstd[:, tt:tt + 1]),
                 reads=[xtm, rstd], writes=[xn])
        yield
        for kc in range(8):
            pt = PT[kc % 2]
            fns = [I("transpose", out=pt[:, tt * 128:(tt + 1) * 128], in_=xn[:, tt, kc * 128:(kc + 1) * 128], identity=ident[:])
                   for tt in range(ntt)]
            S.op("pe", fns, reads=[xn, ident], writes=[pt])
            S.op("dve", I("tensor_scalar", out=hT[:, kc, 0:ntt * 128], in0=pt[:, 0:ntt * 128], scalar1=AB[:, which, m, kc:kc + 1],
                                                                scalar2=AB[:, which + 1, m, kc:kc + 1], op0=ALU.mult, op1=ALU.add),
                 reads=[pt, AB], writes=[hT])
            yield

    def run_merged(gens, ratio=None, after=None):
        ratio = ratio or (1,) * len(gens)
        after = after or {}
        live = list(range(len(gens)))
        while live:
            for gi in list(live):
                if gi in after and after[gi] in live:
                    continue
                for _ in range(ratio[gi]):
                    try:
                        next(gens[gi])
                    except StopIteration:
                        live.remove(gi)
                        break

    lru_cnt = [0]

    def lru_dir(sg, t0, nt, d, L, hstate, ret):
        lxw, xcs, xcbs, rrs, iis, a4, v4, t4, hhs = L
        hh = hhs[lru_cnt[0] % len(hhs)] if isinstance(hhs, list) else hhs
        lru_cnt[0] += 1
        ret.append(hh)
        lxd = sg["lx"]
        S.dma("sp", I("dma_start", out=lxw[:, :, 0:nt + 3], in_=lxd[:, :, t0:t0 + nt + 3].rearrange("c p t -> p c t")),
              reads=[lxd], writes=[lxw])
        for cc in range(4):
            xc, xcb, rr, ii = xcs[cc % 2], xcbs[cc % 2], rrs[cc % 2], iis[cc % 2]
            S.op("dve", I("tensor_scalar", out=xc[:, 0:nt], in0=lxw[:, cc, 0:nt], scalar1=cw[:, cc:cc + 1], scalar2=cb[:, cc:cc + 1],
                          op0=ALU.mult, op1=ALU.add), reads=[lxw, cw, cb], writes=[xc])
            for j in range(1, 4):
                S.op("dve", I("scalar_tensor_tensor", out=xc[:, 0:nt], in0=lxw[:, cc, j:j + nt], scalar=cw[:, j * 4 + cc:j * 4 + cc + 1],
                              in1=xc[:, 0:nt], op0=ALU.mult, op1=ALU.add), reads=[lxw, cw, xc], writes=[xc])
            S.op("act", I("activation", out=xcb[:, 0:nt], in_=xc[:, 0:nt], func=AF.Copy), reads=[xc], writes=[xcb])
            yield
            for gi, dst in ((0, rr), (1, ii)):
                pp = PP()
                mi = d * 2 + gi
                S.op("pe", I("matmul", pp[:, 0:nt], lhsT=Wbd[:, mi * 4 + cc, :], rhs=xcb[:, 0:nt], start=True, stop=True),
                     reads=[Wbd, xcb], writes=[pp])
                S.op("act", I("activation", out=dst[:, 0:nt], in_=pp[:, 0:nt], func=AF.Tanh, scale=0.5,
                              bias=lbias[:, mi * 4 + cc:mi * 4 + cc + 1]), reads=[pp, lbias], writes=[dst])
                yield
            S.op("act", I("activation", out=a4[:, cc, 0:nt], in_=rr[:, 0:nt], func=AF.Exp, scale=cf[:, d * 4 + cc:d * 4 + cc + 1],
                          bias=cf[:, d * 4 + cc:d * 4 + cc + 1]), reads=[rr, cf], writes=[a4])
            S.op("dve", I("scalar_tensor_tensor", out=t4[:, cc, 0:nt], in0=ii[:, 0:nt], scalar=1.0, in1=xc[:, 0:nt], op0=ALU.add, op1=ALU.mult),
                 reads=[ii, xc], writes=[t4])
            yield
        S.op("act", I("activation", out=v4[:, :, 0:nt], in_=a4[:, :, 0:nt], func=AF.Square), reads=[a4], writes=[v4])
        S.op("act", I("activation", out=v4[:, :, 0:nt], in_=v4[:, :, 0:nt], func=AF.Sqrt, scale=-0.25, bias=qtr[:]), reads=[v4, qtr], writes=[v4])
        yield
        S.op("dve", I("tensor_tensor", out=t4[:, :, 0:nt], in0=t4[:, :, 0:nt], in1=v4[:, :, 0:nt], op=ALU.mult), reads=[t4, v4], writes=[t4])
        yield
        for cc in range(4):
            if d == 0:
                S.op("dve", I("tensor_tensor_scan", out=hh[:, cc, 0:nt], data0=a4[:, cc, 0:nt], data1=t4[:, cc, 0:nt],
                              initial=hstate[:, cc:cc + 1], op0=ALU.mult, op1=ALU.add), reads=[a4, t4, hstate], writes=[hh])
            else:
                S.op("dve", I("tensor_tensor_scan", out=hh[:, cc, nt - 1::-1], data0=a4[:, cc, nt - 1::-1], data1=t4[:, cc, nt - 1::-1],
                              initial=hstate[:, 4 + cc:5 + cc], op0=ALU.mult, op1=ALU.add), reads=[a4, t4, hstate], writes=[hh])
            yield
        edge = nt - 1 if d == 0 else 0
        S.op("dve", I("tensor_copy", out=hstate[:, d * 4:(d + 1) * 4], in_=hh[:, :, edge]), reads=[hh], writes=[hstate])
        yield

    with ExitStack() as es2:
        def sb2(name, shape, dtype=F32):
            return Tl(es2.enter_context(nc.sbuf_tensor(name, list(shape), dtype)), name)
        winB = sb2("winB", [128, 8, 1536], BF16)
        xtm2 = [sb2("xtmB%d" % i, [128, 4, D]) for i in range(2)]
        xnB = sb2("xnB", [128, 4, D], BF16)
        junkB = sb2("junkB", [128, D], BF16)
        scr2 = [(sb2("ssqB%d" % i, [128, 4]), sb2("rstdB%d" % i, [128, 4]), xnB, junkB) for i in range(2)]
        hT2 = [sb2("hTB%d" % i, [128, 8, 512], BF16) for i in range(2)]
        lxo2 = [sb2("lxoB", [128, 4, 512])] * 2
        Ktok2 = [sb2("KtokB%d" % i, [128, 4, 512], BF16) for i in range(2)]
        VBtok2 = [sb2("VBtokB%d" % i, [128, 4, 512], BF16) for i in range(2)]
        Sb = sb2("SbB", [128, 4, 128])
        Sbb2 = [sb2("SbbB%d" % i, [128, 512], BF16) for i in range(2)]
        hst = sb2("hstB", [128, 8])
        L = (sb2("lxwB", [128, 4, 515]), [sb2("xcB%d" % i, [128, 512]) for i in range(2)], [sb2("xcbB%d" % i, [128, 512], BF16) for i in range(2)],
             [sb2("rrB%d" % i, [128, 512]) for i in range(2)], [sb2("iiB%d" % i, [128, 512]) for i in range(2)],
             sb2("a4B", [128, 4, 512]), sb2("v4B", [128, 4, 512]), sb2("t4B", [128, 4, 512]),
             sb2("hhB", [128, 4, 512]))
        if KDBG:
            print("pass B sbuf remaining", nc.sbuf_bytes_remaining, flush=True)
        winBc = []
        for (dst0, src0) in ((0, 0), (512, 1536), (1024, 2048)):
            v = Tl(winB.t[:, :, dst0:dst0 + 512], "winB_%d" % dst0)
            winBc.append(v)
            S.dma("pool", I("dma_start", out=v[:, :, :], in_=w_in[:, src0:src0 + 512].rearrange("(kc p) n -> p kc n", p=128)), writes=[v])
        work = []
        for sg in segs:
            T = sg["T"]
            S.dma("sp", I("dma_start", out=sg["lx"][:, :, 0:2].rearrange("c p t -> p c t"),
                          in_=zerof[:, 0:8].rearrange("p (c t) -> p c t", t=2)), reads=[zerof], writes=[sg["lx"]])
            S.dma("sp", I("dma_start", out=sg["lx"][:, :, T + 2:T + 3].rearrange("c p t -> p c t"),
                          in_=zerof[:, 0:4].rearrange("p (c t) -> p c t", t=1)), reads=[zerof], writes=[sg["lx"]])
            nb = len(sg["blocksB"])
            for bi in range(nb - 1, -1, -1):
                work.append((sg, bi))

        def FE_B(wi):
            sg, bi = work[wi]
            p = wi % 2
            nt, m = sg["ntB"], sg["m"]
            ntt = nt // 128
            t0 = sg["blocksB"][bi][0]
            xtm, scr, hT, lxo, Ktok, VBtok = xtm2[p], scr2[p], hT2[p], lxo2[p], Ktok2[p], VBtok2[p]
            S.dma("sp", I("dma_start", out=xtm[:, 0:ntt, :], in_=sg["xin"][t0:t0 + nt, :].rearrange("(tt p) d -> p tt d", p=128)), writes=[xtm])
            yield
            yield from norm_mod_T(xtm, ntt, m, 0, hT, scr)
            for kc0 in (0, 4):
                S.dma("sp", I("dma_start", out=sg["h1"][kc0:kc0 + 4, :, t0:t0 + nt].rearrange("k p t -> p k t"), in_=hT[:, kc0:kc0 + 4, 0:nt]),
                      reads=[hT], writes=[sg["h1"]])
            yield
            for cc in range(4):
                pp = PP()
                fns = [I("matmul", pp[:, 0:nt], lhsT=winBc[0][:, kc, cc * 128:(cc + 1) * 128], rhs=hT[:, kc, 0:nt],
                         start=(kc == 0), stop=(kc == 7)) for kc in range(8)]
                S.op("pe", fns, reads=[winBc[0], hT], writes=[pp])
                S.op("act", I("activation", out=lxo[:, cc, 0:nt], in_=pp[:, 0:nt], func=AF.Copy), reads=[pp], writes=[lxo])
                yield
            S.dma("sp", I("dma_start", out=sg["lx"][:, :, 2 + t0:2 + t0 + nt].rearrange("c p t -> p c t"), in_=lxo[:, :, 0:nt]),
                  reads=[lxo], writes=[sg["lx"]])
            yield
            for tt in range(ntt):
                for which in range(2):
                    pp = PP()
                    fns = [I("matmul", pp[:], lhsT=hT[:, kc, tt * 128:(tt + 1) * 128], rhs=winBc[1 + which][:, kc, 0:512],
                             start=(kc == 0), stop=(kc == 7)) for kc in range(8)]
                    S.op("pe", fns, reads=[winBc[1 + which], hT], writes=[pp])
                    if which == 0:
                        S.op("act", I("activation", out=Ktok[:, tt, :], in_=pp[:], func=AF.Copy, scale=128.0 ** -0.5), reads=[pp], writes=[Ktok])
                        yield
                    else:
                        S.op("dve", I("tensor_tensor", out=VBtok[:, tt, :], in0=pp[:], in1=TBm[:].rearrange("p h e -> p (h e)"), op=ALU.mult),
                             reads=[pp, TBm], writes=[VBtok])
                        yield

        def BE_Bret(wi):
            sg, bi = work[wi]
            p = wi % 2
            nt = sg["ntB"]
            ntt = nt // 128
            blocks = sg["blocksB"]
            t0 = blocks[bi][0]
            Ktok, VBtok = Ktok2[p], VBtok2[p]
            if bi == len(blocks) - 1:
                if sg["kind"] == "s":
                    S.dma("sp", I("dma_start", out=Sb[:], in_=st_rb.rearrange("h d e -> d h e")), writes=[Sb])
                else:
                    S.op("pool", I("memset", Sb[:], 0.0), writes=[Sb])
                yield
            for tt in range(ntt - 1, -1, -1):
                ci = t0 // 128 + tt
                Sbb = Sbb2[ci % 2]
                S.op("act", I("activation", out=Sbb[:], in_=Sb[:].rearrange("p h e -> p (h e)"), func=AF.Copy), reads=[Sb], writes=[Sbb])
                yield
                S.dma("sp", I("dma_start", out=sg["Sb"][ci], in_=Sbb[:]), reads=[Sbb], writes=[sg["Sb"]])
                yield
                pkv = PP()
                fns = [I("matmul", pkv[:, h * 128:(h + 1) * 128], lhsT=Ktok[:, tt, h * 128:(h + 1) * 128],
                         rhs=VBtok[:, tt, h * 128:(h + 1) * 128], start=True, stop=True) for h in range(4)]
                S.op("pe", fns, reads=[Ktok, VBtok], writes=[pkv])
                for h in range(4):
                    S.op("dve", I("scalar_tensor_tensor", out=Sb[:, h, :], in0=Sb[:, h, :], scalar=g128[:, 4 + h:5 + h],
                                  in1=pkv[:, h * 128:(h + 1) * 128], op0=ALU.mult, op1=ALU.add), reads=[Sb, g128, pkv], writes=[Sb])
                yield
            if bi == 0 and sg["kind"] == "p":
                i = sg["idx"]
                ob = obuf(nrb, "nrb")
                outs_toks.append(S.dma("sp", I("dma_start", out=nrb[i].rearrange("h d e -> d h e"), in_=Sb[:]), reads=[Sb], writes=[ob]))
                yield

        def BE_Blru(wi):
            sg, bi = work[wi]
            nt = sg["ntB"]
            blocks = sg["blocksB"]
            if bi == len(blocks) - 1:
                if sg["kind"] == "s":
                    S.op("pool", I("tensor_copy", out=hst[:], in_=h0[:]), reads=[h0], writes=[hst])
                else:
                    S.op("pool", I("memset", hst[:], 0.0), writes=[hst])
                yield
            todo = []
            if bi + 1 < len(blocks):
                todo.append(bi + 1)
            if bi == 0:
                todo.append(0)
            for bj in todo:
                tj = blocks[bj][0]
                ret = []
                yield from lru_dir(sg, tj, nt, 1, L, hst, ret)
                hh = ret[0]
                if sg["kind"] == "p" and bj == len(blocks) - 1:
                    ob2 = obuf(nlb, "nlb")
                    outs_toks.append(S.dma("sp", I("dma_start", out=nlb[sg["idx"]].rearrange("(n p) -> p n", p=128), in_=hh[:, :, nt - 1]),
                                           reads=[hh], writes=[ob2]))
                S.dma("sp", I("dma_start", out=sg["hb"][:, :, tj:tj + nt].rearrange("c p t -> p c t"), in_=hh[:, :, 0:nt]),
                      reads=[hh], writes=[sg["hb"]])
                yield

        run_merged([FE_B(0)])
        for wi in range(len(work)):
            gens = [BE_Bret(wi), BE_Blru(wi)]
            if wi + 1 < len(work):
                gens.append(FE_B(wi + 1))
            run_merged(gens)
        S.barrier()
        S.emit()
    if KSTOP == "B":
        esBF.close()
        es.close()
        return nc

    with ExitStack() as es2:
        def sb2(name, shape, dtype=F32):
            return Tl(es2.enter_context(nc.sbuf_tensor(name, list(shape), dtype)), name)
        winF = sb2("winF", [128, 8, 2560], BF16)
        wo = sb2("wo", [128, 8, D], BF16)
        xtm2 = [sb2("xtmF%d" % i, [128, 2, D]) for i in range(2)]
        junkF = sb2("junkF", [128, D], BF16)
        xnF = sb2("xnF", [128, 2, D], BF16)
        scr2 = [(sb2("ssqF%d" % i, [128, 4]), sb2("rstdF%d" % i, [128, 4]), xnF, junkF) for i in range(2)]
        scrBE = (sb2("ssqFb", [128, 4]), sb2("rstdFb", [128, 4]), xnF, junkF)
        hT2 = [sb2("hTF%d" % i, [128, 8, 256], BF16) for i in range(2)]
        h2T = sb2("h2TF", [128, 8, 256], BF16)
        gl2 = [sb2("glF%d" % i, [128, 4, 256], BF16) for i in range(2)]
        qT2 = [sb2("qTF%d" % i, [128, 4, 256], BF16) for i in range(2)]
        qf2 = [sb2("qfF%d" % i, [128, 4, 256], BF16) for i in range(2)]
        qb2 = [sb2("qbF%d" % i, [128, 4, 256], BF16) for i in range(2)]
        kT2 = [sb2("kTF%d" % i, [128, 4, 256], BF16) for i in range(2)]
        Ktok2 = [sb2("KtokF%d" % i, [128, 2, 512], BF16) for i in range(2)]
        Vtok2 = [sb2("VtokF%d" % i, [128, 2, 512], BF16) for i in range(2)]
        VFtok2 = [sb2("VFtokF%d" % i, [128, 2, 512], BF16) for i in range(2)]
        rg2 = [sb2("rgF%d" % i, [128, 2, 512], BF16) for i in range(2)]
        Sf = sb2("SfF", [128, 4, 128])
        Sfb2 = [sb2("SfbF%d" % i, [128, 512], BF16) for i in range(2)]
        Sbb2 = [sb2("SbbF%d" % i, [128, 512], BF16) for i in range(2)]
        PTs2 = [sb2("PTsF%d" % i, [128, 512], BF16) for i in range(2)]
        hst = sb2("hstF", [128, 8])
        hbw = sb2("hbwF", [128, 4, 256])
        yT2 = [sb2("yTF%d" % i, [128, 8, 256], BF16) for i in range(2)]
        ytok2 = [sb2("ytokF%d" % i, [128, 512], BF16) for i in range(2)]
        otmp2 = [sb2("otmpF", [128, 512])] * 2
        bst = sb2("bstF", [128, 2, 4, 6])
        bag = sb2("bagF", [128, 2, 4, 2])
        grs = sb2("grsF", [128, 2, 4])
        wtmp2 = [sb2("wtmpF", [128, 512])] * 2
        L = (sb2("lxwF", [128, 4, 259]), [sb2("xcF%d" % i, [128, 256]) for i in range(2)], [sb2("xcbF%d" % i, [128, 256], BF16) for i in range(2)],
             [sb2("rrF%d" % i, [128, 256]) for i in range(2)], [sb2("iiF%d" % i, [128, 256]) for i in range(2)],
             sb2("a4F", [128, 4, 256]), sb2("v4F", [128, 4, 256]), sb2("t4F", [128, 4, 256]), sb2("hhF", [128, 4, 256]))
        if KDBG:
            print("pass F sbuf remaining", nc.sbuf_bytes_remaining, flush=True)
        winFc = []
        for ci5 in range(5):
            v = Tl(winF.t[:, :, ci5 * 512:(ci5 + 1) * 512], "winF_%d" % ci5)
            winFc.append(v)
            S.dma("pool", I("dma_start", out=v[:, :, :], in_=w_in[:, 512 + ci5 * 512:1024 + ci5 * 512].rearrange("(kc p) n -> p kc n", p=128)), writes=[v])
        S.dma("pool", I("dma_start", out=wo[:], in_=w_out.rearrange("(kc p) n -> p kc n", p=128)), writes=[wo])
        for cc in range(4):
            S.op("pool", I("tensor_scalar", out=wo[:, 4 + cc, :], in0=wo[:, 4 + cc, :], scalar1=gnw[:, cc:cc + 1], scalar2=None, op0=ALU.mult),
                 reads=[wo, gnw], writes=[wo])
        work = []
        for sg in segs:
            T = sg["T"]
            for kc0 in (0, 4):
                S.dma("sp", I("dma_start", out=sg["h2"][kc0:kc0 + 4, :, 0:64].rearrange("k p t -> p k t"),
                              in_=zerob[:, 0:256].rearrange("p (k t) -> p k t", t=64)), reads=[zerob], writes=[sg["h2"]])
                S.dma("sp", I("dma_start", out=sg["h2"][kc0:kc0 + 4, :, T + 64:T + 128].rearrange("k p t -> p k t"),
                              in_=zerob[:, 0:256].rearrange("p (k t) -> p k t", t=64)), reads=[zerob], writes=[sg["h2"]])
            for bi in range(len(sg["blocksBF"])):
                work.append((sg, bi))

        def FE_F(wi):
            sg, bi = work[wi]
            p = wi % 2
            nt, m = sg["ntb"], sg["m"]
            ntt = nt // 128
            t0 = sg["blocksBF"][bi][0]
            xtm, scr, hT = xtm2[p], scr2[p], hT2[p]
            gl, qT, qf, qb, kT, Ktok, Vtok, VFtok, rg = gl2[p], qT2[p], qf2[p], qb2[p], kT2[p], Ktok2[p], Vtok2[p], VFtok2[p], rg2[p]
            S.dma("sp", I("dma_start", out=xtm[:, 0:ntt, :], in_=sg["xin"][t0:t0 + nt, :].rearrange("(tt p) d -> p tt d", p=128)), writes=[xtm])
            yield
            for kc0 in (0, 4):
                S.dma("sp", I("dma_start", out=hT[:, kc0:kc0 + 4, 0:nt], in_=sg["h1"][kc0:kc0 + 4, :, t0:t0 + nt].rearrange("k p t -> p k t")),
                      reads=[sg["h1"]], writes=[hT])
            yield

            def fm_proj(col0, cc):
                pp = PP()
                wv = winFc[col0 // 512]
                fns = [I("matmul", pp[:, 0:nt], lhsT=wv[:, kc, cc * 128:(cc + 1) * 128], rhs=hT[:, kc, 0:nt],
                         start=(kc == 0), stop=(kc == 7)) for kc in range(8)]
                S.op("pe", fns, reads=[wv, hT], writes=[pp])
                return pp

            def tm_proj(col0, tt):
                pp = PP()
                wv = winFc[col0 // 512]
                fns = [I("matmul", pp[:], lhsT=hT[:, kc, tt * 128:(tt + 1) * 128], rhs=wv[:, kc, 0:512],
                         start=(kc == 0), stop=(kc == 7)) for kc in range(8)]
                S.op("pe", fns, reads=[wv, hT], writes=[pp])
                return pp

            for cc in range(4):
                pp = fm_proj(0, cc)
                S.op("act", I("activation", out=gl[:, cc, 0:nt], in_=pp[:, 0:nt], func=GELU), reads=[pp], writes=[gl])
            for h in range(4):
                pp = fm_proj(512, h)
                S.op("act", I("activation", out=qT[:, h, 0:nt], in_=pp[:, 0:nt], func=AF.Copy), reads=[pp], writes=[qT])
                S.op("dve", I("tensor_tensor", out=qf[:, h, 0:nt].rearrange("p (c i) -> p c i", i=128),
                              in0=pp[:, 0:nt].rearrange("p (c i) -> p c i", i=128),
                              in1=HF[:, h:h + 1, :].to_broadcast([128, ntt, 128]), op=ALU.mult), reads=[pp, HF], writes=[qf])
                S.op("dve", I("tensor_tensor", out=qb[:, h, 0:nt].rearrange("p (c i) -> p c i", i=128),
                              in0=pp[:, 0:nt].rearrange("p (c i) -> p c i", i=128),
                              in1=HB[:, h:h + 1, :].to_broadcast([128, ntt, 128]), op=ALU.mult), reads=[pp, HB], writes=[qb])
                yield
            for h in range(4):
                pp = fm_proj(1024, h)
                S.op("act", I("activation", out=kT[:, h, 0:nt], in_=pp[:, 0:nt], func=AF.Copy, scale=128.0 ** -0.5), reads=[pp], writes=[kT])
                yield
            for tt in range(ntt):
                pp = tm_proj(1024, tt)
                S.op("act", I("activation", out=Ktok[:, tt, :], in_=pp[:], func=AF.Copy, scale=128.0 ** -0.5), reads=[pp], writes=[Ktok])
                yield
                pp = tm_proj(1536, tt)
                S.op("act", I("activation", out=Vtok[:, tt, :], in_=pp[:], func=AF.Copy), reads=[pp], writes=[Vtok])
                S.op("dve", I("tensor_tensor", out=VFtok[:, tt, :], in0=pp[:], in1=TFm[:].rearrange("p h e -> p (h e)"), op=ALU.mult),
                     reads=[pp, TFm], writes=[VFtok])
                yield
                pp = tm_proj(2048, tt)
                S.op("act", I("activation", out=rg[:, tt, :], in_=pp[:], func=AF.Tanh, scale=0.5), reads=[pp], writes=[rg])
                S.op("dve", I("scalar_tensor_tensor", out=rg[:, tt, :], in0=rg[:, tt, :], scalar=1.0, in1=pp[:], op0=ALU.add, op1=ALU.mult),
                     reads=[rg, pp], writes=[rg])
                yield

        def BE_lru(wi):
            sg, bi = work[wi]
            p = wi % 2
            nt, m = sg["ntb"], sg["m"]
            ntt = nt // 128
            blocks = sg["blocksBF"]
            t0 = blocks[bi][0]
            xtm = xtm2[p]
            yT = yT2[p]
            gl, qT, qf, qb, kT, Ktok, Vtok, VFtok, rg = gl2[p], qT2[p], qf2[p], qb2[p], kT2[p], Ktok2[p], Vtok2[p], VFtok2[p], rg2[p]
            if bi == 0:
                if sg["kind"] == "s":
                    S.op("pool", I("tensor_copy", out=hst[:], in_=h0[:]), reads=[h0], writes=[hst])
                else:
                    S.op("pool", I("memset", hst[:], 0.0), writes=[hst])
            ret = []
            yield from lru_dir(sg, t0, nt, 0, L, hst, ret)
            hh = ret[0]
            if sg["kind"] == "p" and t0 == 0:
                ob2 = obuf(nlf, "nlf")
                outs_toks.append(S.dma("sp", I("dma_start", out=nlf[sg["idx"]].rearrange("(n p) -> p n", p=128), in_=hh[:, :, 0]),
                                       reads=[hh], writes=[ob2]))
            S.dma("sp", I("dma_start", out=hbw[:, :, 0:nt], in_=sg["hb"][:, :, t0:t0 + nt].rearrange("c p t -> p c t")),
                  reads=[sg["hb"]], writes=[hbw])
            yield
            S.op("pool", I("tensor_tensor", out=hbw[:, :, 0:nt], in0=hbw[:, :, 0:nt], in1=hh[:, :, 0:nt], op=ALU.add), reads=[hbw, hh], writes=[hbw])
            yield
            S.op("dve", I("tensor_tensor", out=yT[:, 0:4, 0:nt], in0=hbw[:, :, 0:nt], in1=gl[:, :, 0:nt], op=ALU.mult), reads=[hbw, gl], writes=[yT])
            yield
        def BE_ret(wi):
            sg, bi = work[wi]
            p = wi % 2
            nt, m = sg["ntb"], sg["m"]
            ntt = nt // 128
            blocks = sg["blocksBF"]
            t0 = blocks[bi][0]
            xtm = xtm2[p]
            yT = yT2[p]
            gl, qT, qf, qb, kT, Ktok, Vtok, VFtok, rg = gl2[p], qT2[p], qf2[p], qb2[p], kT2[p], Ktok2[p], Vtok2[p], VFtok2[p], rg2[p]
            if bi == 0:
                if sg["kind"] == "s":
                    S.dma("sp", I("dma_start", out=Sf[:], in_=st_rf.rearrange("h d e -> d h e")), writes=[Sf])
                else:
                    S.op("pool", I("memset", Sf[:], 0.0), writes=[Sf])
            for tt in range(ntt):
                ci = t0 // 128 + tt
                q2 = ci % 2
                Sbb, Sfb, PTs, ytok, otmp = Sbb2[q2], Sfb2[q2], PTs2[q2], ytok2[q2], otmp2[q2]
                tk = slice(tt * 128, (tt + 1) * 128)
                S.dma("sp", I("dma_start", out=Sbb[:], in_=sg["Sb"][ci]), reads=[sg["Sb"]], writes=[Sbb])
                yield
                S.op("act", I("activation", out=Sfb[:], in_=Sf[:].rearrange("p h e -> p (h e)"), func=AF.Copy), reads=[Sf], writes=[Sfb])
                yield
                ps_ = PP()
                fns = [I("matmul", ps_[:, h * 128:(h + 1) * 128], lhsT=kT[:, h, tk], rhs=qT[:, h, tk], start=True, stop=True) for h in range(4)]
                S.op("pe", fns, reads=[kT, qT], writes=[ps_])
                S.op("dve", I("tensor_tensor", out=PTs[:], in0=ps_[:], in1=Dm[:].rearrange("p h i -> p (h i)"), op=ALU.mult), reads=[ps_, Dm], writes=[PTs])
                yield
                po_ = PP()
                fns = []
                for h in range(4):
                    hs = slice(h * 128, (h + 1) * 128)
                    fns.append(I("matmul", po_[:, hs], lhsT=PTs[:, hs], rhs=Vtok[:, tt, hs], start=True, stop=False))
                    fns.append(I("matmul", po_[:, hs], lhsT=qf[:, h, tk], rhs=Sfb[:, hs], start=False, stop=False))
                    fns.append(I("matmul", po_[:, hs], lhsT=qb[:, h, tk], rhs=Sbb[:, hs], start=False, stop=True))
                S.op("pe", fns, reads=[PTs, Vtok, qf, qb, Sfb, Sbb], writes=[po_])
                S.op("act", I("activation", out=otmp[:], in_=po_[:], func=AF.Copy), reads=[po_], writes=[otmp])
                yield
                pkv = PP()
                fns = [I("matmul", pkv[:, h * 128:(h + 1) * 128], lhsT=Ktok[:, tt, h * 128:(h + 1) * 128],
                         rhs=VFtok[:, tt, h * 128:(h + 1) * 128], start=True, stop=True) for h in range(4)]
                S.op("pe", fns, reads=[Ktok, VFtok], writes=[pkv])
                for h in range(4):
                    S.op("dve", I("scalar_tensor_tensor", out=Sf[:, h, :], in0=Sf[:, h, :], scalar=g128[:, h:h + 1],
                                  in1=pkv[:, h * 128:(h + 1) * 128], op0=ALU.mult, op1=ALU.add), reads=[Sf, g128, pkv], writes=[Sf])
                yield
                for h in range(4):
                    S.op("dve", I("bn_stats", out=bst[:, q2, h, :], in_=otmp[:, h * 128:(h + 1) * 128]), reads=[otmp], writes=[bst])
                    yield
                for h in range(4):
                    S.op("dve", I("bn_aggr", out=bag[:, q2, h, :], in_=bst[:, q2, h, :]), reads=[bst], writes=[bag])
                    yield
                S.op("pool", I("tensor_scalar", out=grs[:, q2, :], in0=bag[:, q2, :, 1], scalar1=EPS, scalar2=None, op0=ALU.add), reads=[bag], writes=[grs])
                S.op("pool", I("tensor_tensor", out=grs[:, q2, :], in0=grs[:, q2, :], in1=mhalf[:, 0:4], op=ALU.pow), reads=[grs, mhalf], writes=[grs])
                yield
                for h in range(4):
                    S.op("dve", I("tensor_scalar", out=otmp[:, h * 128:(h + 1) * 128], in0=otmp[:, h * 128:(h + 1) * 128],
                                  scalar1=bag[:, q2, h, 0:1], scalar2=grs[:, q2, h:h + 1], op0=ALU.subtract, op1=ALU.mult),
                         reads=[otmp, bag, grs], writes=[otmp])
                    yield
                S.op("pool", I("tensor_tensor", out=ytok[:], in0=otmp[:], in1=rg[:, tt, :], op=ALU.mult), reads=[otmp, rg], writes=[ytok])
                yield
                pt = PT[tt % 2]
                fns = [I("transpose", out=pt[:, h * 128:(h + 1) * 128], in_=ytok[:, h * 128:(h + 1) * 128], identity=ident[:]) for h in range(4)]
                S.op("pe", fns, reads=[ytok, ident], writes=[pt])
                S.op("act", I("activation", out=yT[:, 4:8, tk], in_=pt[:, 0:512].rearrange("p (h i) -> p h i", i=128), func=AF.Copy),
                     reads=[pt], writes=[yT])
                yield
            if bi == len(blocks) - 1 and sg["kind"] == "p":
                i = sg["idx"]
                ob = obuf(nrf, "nrf")
                outs_toks.append(S.dma("sp", I("dma_start", out=nrf[i].rearrange("h d e -> d h e"), in_=Sf[:]), reads=[Sf], writes=[ob]))

            yield

        def BE_out(wi):
            sg, bi = work[wi]
            p = wi % 2
            nt, m = sg["ntb"], sg["m"]
            ntt = nt // 128
            blocks = sg["blocksBF"]
            t0 = blocks[bi][0]
            xtm = xtm2[p]
            yT = yT2[p]
            gl, qT, qf, qb, kT, Ktok, Vtok, VFtok, rg = gl2[p], qT2[p], qf2[p], qb2[p], kT2[p], Ktok2[p], Vtok2[p], VFtok2[p], rg2[p]
            k2 = 0
            for tt in range(ntt):
                for nh in range(2):
                    wtmp = wtmp2[k2 % 2]
                    k2 += 1
                    pp = PP()
                    fns = [I("matmul", pp[:], lhsT=yT[:, kc, tt * 128:(tt + 1) * 128], rhs=wo[:, kc, nh * 512:(nh + 1) * 512],
                             start=(kc == 0), stop=(kc == 7)) for kc in range(8)]
                    S.op("pe", fns, reads=[yT, wo], writes=[pp])
                    S.op("dve", I("tensor_tensor", out=wtmp[:], in0=pp[:], in1=g1bc[m][:, nh * 512:(nh + 1) * 512], op=ALU.mult),
                         reads=[pp, g1bc[m]], writes=[wtmp])
                    yield
                    S.op("pool", I("tensor_tensor", out=xtm[:, tt, nh * 512:(nh + 1) * 512], in0=xtm[:, tt, nh * 512:(nh + 1) * 512],
                                   in1=wtmp[:], op=ALU.add), reads=[xtm, wtmp], writes=[xtm])
                    yield
            S.dma("sp", I("dma_start", out=sg["x1"][t0:t0 + nt, :].rearrange("(tt p) d -> p tt d", p=128), in_=xtm[:, 0:ntt, :]),
                  reads=[xtm], writes=[sg["x1"]])
            yield
            yield from norm_mod_T(xtm, ntt, m, 2, h2T, scrBE)
            for kc0 in (0, 4):
                S.dma("sp", I("dma_start", out=sg["h2"][kc0:kc0 + 4, :, 64 + t0:64 + t0 + nt].rearrange("k p t -> p k t"),
                              in_=h2T[:, kc0:kc0 + 4, 0:nt]), reads=[h2T], writes=[sg["h2"]])
                yield
            yield

        run_merged([FE_F(0)])
        nW = len(work)
        for wi in range(nW + 1):
            gens, after = [], {}
            if wi >= 1:
                gens.append(BE_out(wi - 1))
            if wi < nW:
                gens.append(BE_lru(wi))
                gens.append(BE_ret(wi))
                if wi + 1 < nW:
                    gens.append(FE_F(wi + 1))
                    if wi >= 1:
                        after[len(gens) - 1] = 0
            run_merged(gens, after=after)
        S.barrier()
        S.emit()
    if KSTOP == "F":
        esBF.close()
        es.close()
        return nc

    esBF.close()
    with ExitStack() as es2:
        def sb2(name, shape, dtype=F32):
            return Tl(es2.enter_context(nc.sbuf_tensor(name, list(shape), dtype)), name)
        g2bc = [sb2("g2bcG%d" % m, [128, D]) for m in range(2)]
        fnwbc = sb2("fnwbc", [128, D])
        S.dma("sp", I("dma_start", out=fnwbc[:], in_=fnw.partition_broadcast(128)), writes=[fnwbc])
        for m in range(2):
            S.dma("sp", I("dma_start", out=g2bc[m][:], in_=g2d[m].partition_broadcast(128)), reads=[g2d], writes=[g2bc[m]])
        wg = sb2("wg", [128, 8, FH], BF16)
        wu = sb2("wu", [128, 8, FH], BF16)
        wd = sb2("wd", [128, NCC, D], BF16)
        h2w = sb2("h2w", [128, 8, 640], BF16)
        gp = [sb2("gp%d" % i, [128, 660]) for i in range(2)]
        cv = [sb2("cv%d" % i, [128, 512]) for i in range(2)]
        actT = sb2("actT", [128, NCC, 512], BF16)
        x1t = [sb2("x1t%d" % i, [128, D]) for i in range(2)]
        wtmps = [sb2("wtmpG%d" % i, [128, 512]) for i in range(2)]
        junk = sb2("junkG", [128, D], BF16)
        ssqs = sb2("ssqG", [128, 4])
        rstds = sb2("rstdG", [128, 4])
        if KDBG:
            print("pass G sbuf remaining", nc.sbuf_bytes_remaining, flush=True)
        ccb = [0, 6, 12, 17, 22]
        wgc, wuc = [], []
        for k4 in range(4):
            for (lst, w_, src_, nm) in ((wgc, wg, w_gate, "wg"), (wuc, wu, w_up, "wu")):
                c_lo, c_hi = ccb[k4] * 128, ccb[k4 + 1] * 128
                v = Tl(w_.t[:, :, c_lo:c_hi], "%s_c%d" % (nm, k4))
                lst.append(v)
                S.dma("pool", I("dma_start", out=v[:, :, :], in_=src_[:, c_lo:c_hi].rearrange("(kc p) n -> p kc n", p=128)), writes=[v])

        def wsel(lst, cc):
            k4 = max(k for k in range(4) if ccb[k] <= cc)
            lo = (cc - ccb[k4]) * 128
            return lst[k4], slice(lo, lo + 128)

        for c0 in (0, 11):
            S.dma("pool", I("dma_start", out=wd[:, c0:c0 + 11, :], in_=w_down[c0 * 128:(c0 + 11) * 128, :].rearrange("(kc p) n -> p kc n", p=128)),
                  writes=[wd])
        for g_ in gp:
            S.op("pool", I("memset", g_[:], 0.0), writes=[g_])
        blk = 0
        gblocks = [(sg_["name"], t0_) for sg_ in segs for (t0_, _) in sg_["blocks"]]
        segby = {sg_["name"]: sg_ for sg_ in segs}

        def load_h2w(gi):
            sgx = segby[gblocks[gi][0]]
            t0x = gblocks[gi][1]
            sx = sgx["kind"] == "s"
            Wx = sgx["nt"] + 128 if sx else sgx["nt"]
            c0x = t0x if sx else 64
            for kc0 in (0, 4):
                S.dma("sp", I("dma_start", out=h2w[:, kc0:kc0 + 4, 0:Wx], in_=sgx["h2"][kc0:kc0 + 4, :, c0x:c0x + Wx].rearrange("k p t -> p k t")),
                      reads=[sgx["h2"]], writes=[h2w])

        for sg in segs:
            T, nt, m = sg["T"], sg["nt"], sg["m"]
            ntt = nt // 128
            samp = sg["kind"] == "s"
            if not samp:
                for g_ in gp:
                    S.op("pool", I("memset", g_[:], 0.0), writes=[g_])
            for (t0, _) in sg["blocks"]:
                gi_ = gblocks.index((sg["name"], t0))
                if gi_ == 0:
                    load_h2w(0)
                own = 64 if samp else 0

                def G_cc(cc, slot, nt=nt, samp=samp, own=own):
                    g_ = gp[slot]
                    cv_ = cv[slot]
                    cs = slice(cc * 128, (cc + 1) * 128)
                    if samp:
                        p1, p2 = PP(), PP()
                        wgv, wcs = wsel(wgc, cc)
                        fns = [I("matmul", p1[:], lhsT=wgv[:, kc, wcs], rhs=h2w[:, kc, 0:512], start=(kc == 0), stop=(kc == 7)) for kc in range(8)]
                        S.op("pe", fns, reads=[wgv, h2w], writes=[p1])
                        fns = [I("matmul", p2[:, 0:128], lhsT=wgv[:, kc, wcs], rhs=h2w[:, kc, 512:640], start=(kc == 0), stop=(kc == 7)) for kc in range(8)]
                        S.op("pe", fns, reads=[wgv, h2w], writes=[p2])
                        gv = g_[:, 0:660].rearrange("p (r c) -> p r c", c=66)
                        S.op("act", I("activation", out=gv[:, 0:8, 1:65], in_=p1[:].rearrange("p (r c) -> p r c", c=64), func=AF.Copy),
                             reads=[p1], writes=[g_])
                        S.op("act", I("activation", out=gv[:, 8:10, 1:65], in_=p2[:, 0:128].rearrange("p (r c) -> p r c", c=64), func=AF.Copy),
                             reads=[p2], writes=[g_])
                        taps = [(dy, dx) for dy in (-1, 0, 1) for dx in (-1, 0, 1)]
                        cvv = cv_[:, 0:512].rearrange("p (r c) -> p r c", c=64)

                        def gsl(dy, dx):
                            return gv[:, 1 + dy:9 + dy, 1 + dx:65 + dx]
                    else:
                        p1 = PP()
                        wgv, wcs = wsel(wgc, cc)
                        fns = [I("matmul", p1[:, 0:nt], lhsT=wgv[:, kc, wcs], rhs=h2w[:, kc, 0:nt], start=(kc == 0), stop=(kc == 7)) for kc in range(8)]
                        S.op("pe", fns, reads=[wgv, h2w], writes=[p1])
                        gv = g_[:, 0:nt + 2]
                        S.op("act", I("activation", out=gv[:, 1:nt + 1], in_=p1[:, 0:nt], func=AF.Copy), reads=[p1], writes=[g_])
                        taps = [(0, dx) for dx in (-1, 0, 1)]
                        cvv = cv_[:, 0:nt]

                        def gsl(dy, dx):
                            return gv[:, 1 + dx:1 + dx + nt]
                    yield
                    pu = PP()
                    wuv, wcs2 = wsel(wuc, cc)
                    fns = [I("matmul", pu[:, 0:nt], lhsT=wuv[:, kc, wcs2], rhs=h2w[:, kc, own:own + nt], start=(kc == 0), stop=(kc == 7)) for kc in range(8)]
                    S.op("pe", fns, reads=[wuv, h2w], writes=[pu])
                    for ti, (dy, dx) in enumerate(taps):
                        tcol = ((dy + 1) * 3 + (dx + 1)) * NCC + cc
                        if ti == 0:
                            S.op("act", I("activation", out=cvv, in_=gsl(dy, dx), func=AF.Identity, scale=fcw[:, tcol:tcol + 1], bias=fcb[:, cc:cc + 1]),
                                 reads=[g_, fcw, fcb], writes=[cv_])
                        elif cc in POOL_CCS:
                            tmpw = wtmps[slot]
                            tmpv = tmpw[:, 0:512].rearrange("p (r c) -> p r c", c=64) if samp else tmpw[:, 0:nt]
                            S.op("pool", I("tensor_scalar", out=tmpv, in0=gsl(dy, dx), scalar1=fcw[:, tcol:tcol + 1], scalar2=None, op0=ALU.mult),
                                 reads=[g_, fcw], writes=[tmpw])
                            S.op("pool", I("tensor_tensor", out=cvv, in0=cvv, in1=tmpv, op=ALU.add), reads=[cv_, tmpw], writes=[cv_])
                        else:
                            S.op("dve", I("scalar_tensor_tensor",
                                out=cvv, in0=gsl(dy, dx), scalar=fcw[:, tcol:tcol + 1], in1=cvv, op0=ALU.mult, op1=ALU.add),
                                reads=[g_, fcw, cv_], writes=[cv_])
                        yield
                    S.op("act", I("activation", out=cv_[:, 0:nt], in_=cv_[:, 0:nt], func=GELU), reads=[cv_], writes=[cv_])
                    yield
                    S.op("dve", I("tensor_tensor", out=actT[:, cc, 0:nt], in0=pu[:, 0:nt], in1=cv_[:, 0:nt], op=ALU.mult),
                         reads=[pu, cv_], writes=[actT])
                    yield

                for cc in range(0, NCC, 2):
                    run_merged([G_cc(cc, 0), G_cc(cc + 1, 1)])
                if gi_ + 1 < len(gblocks):
                    load_h2w(gi_ + 1)
                def G_down(tt, slot, sg=sg, t0=t0, m=m):
                    xt = x1t[slot]
                    wtmp = wtmps[slot]
                    S.dma("sp", I("dma_start", out=xt[:], in_=sg["x1"][t0 + tt * 128:t0 + (tt + 1) * 128, :]), reads=[sg["x1"]], writes=[xt])
                    yield
                    for nh in range(2):
                        pp = PP()
                        fns = [I("matmul", pp[:], lhsT=actT[:, cc, tt * 128:(tt + 1) * 128], rhs=wd[:, cc, nh * 512:(nh + 1) * 512],
                                 start=(cc == 0), stop=(cc == NCC - 1)) for cc in range(NCC)]
                        S.op("pe", fns, reads=[actT, wd], writes=[pp])
                        S.op("dve", I("tensor_tensor", out=wtmp[:], in0=pp[:], in1=g2bc[m][:, nh * 512:(nh + 1) * 512], op=ALU.mult),
                             reads=[pp, g2bc[m]], writes=[wtmp])
                        yield
                        S.op("pool", I("tensor_tensor", out=xt[:, nh * 512:(nh + 1) * 512], in0=xt[:, nh * 512:(nh + 1) * 512], in1=wtmp[:], op=ALU.add),
                             reads=[xt, wtmp], writes=[xt])
                        yield
                    S.op("act", I("activation", out=junk[:], in_=xt[:], func=AF.Square, accum_out=ssqs[:, slot:slot + 1]), reads=[xt], writes=[junk, ssqs])
                    yield
                    S.op("dve", I("tensor_scalar", out=rstds[:, slot:slot + 1], in0=ssqs[:, slot:slot + 1], scalar1=1.0 / D, scalar2=EPS, op0=ALU.mult, op1=ALU.add),
                         reads=[ssqs], writes=[rstds])
                    S.op("pool", I("tensor_tensor", out=rstds[:, slot:slot + 1], in0=rstds[:, slot:slot + 1], in1=mhalf[:, 0:1], op=ALU.pow),
                         reads=[rstds, mhalf], writes=[rstds])
                    yield
                    S.op("dve", I("scalar_tensor_tensor", out=xt[:], in0=xt[:], scalar=rstds[:, slot:slot + 1], in1=fnwbc[:], op0=ALU.mult, op1=ALU.mult),
                         reads=[xt, rstds, fnwbc], writes=[xt])
                    yb = sg["yb"][slot % 2]
                    S.dma("sp", I("dma_start", out=sg["yout"][t0 + tt * 128:t0 + (tt + 1) * 128, :], in_=xt[:]), reads=[xt], writes=[yb])
                    yield

                for tt0 in range(0, ntt, 2):
                    run_merged([G_down(tt0 + k, k) for k in range(min(2, ntt - tt0))])
        for sg in segs:
            for yb in sg["yb"]:
                outs_toks.append(yb.b.wtok)
        waits = []
        for t in outs_toks:
            S._need("sp", t, waits)
        S.ops["sp"].append(("op", waits, [], None))
        S.barrier()
        S.emit()

    es.close()
    return nc


_NC_CACHE = {}


def kernel(**inputs):
    x_prompt = np.ascontiguousarray(inputs["x_prompt"], dtype=np.float32)
    x_sample = np.ascontiguousarray(inputs["x_sample"], dtype=np.float32)
    BP, TP, _ = x_prompt.shape
    BS, TS, _ = x_sample.shape
    ncores = 8
    NP = BP // ncores
    assert BS == ncores
    key = (TS, NP, TP)
    if key not in _NC_CACHE:
        _NC_CACHE[key] = build_nc(TS, NP, TP)
    nc = _NC_CACHE[key]

    def f(n):
        return np.ascontiguousarray(inputs[n], dtype=np.float32)

    shared = {
        "c_ctx": f("c_ctx"), "norm1_w": f("norm1_w")[0], "w_mod": f("w_mod")[0], "b_mod": f("b_mod")[0], "w_in": f("w_in")[0],
        "lru_conv_w": f("lru_conv_w")[0], "lru_conv_b": f("lru_conv_b")[0],
        "lru_wa_fw": f("lru_wa_fw")[0], "lru_wx_fw": f("lru_wx_fw")[0], "lru_wa_bw": f("lru_wa_bw")[0], "lru_wx_bw": f("lru_wx_bw")[0],
        "lru_ba_fw": f("lru_ba_fw")[0], "lru_bx_fw": f("lru_bx_fw")[0], "lru_ba_bw": f("lru_ba_bw")[0], "lru_bx_bw": f("lru_bx_bw")[0],
        "lru_lambda_fw": f("lru_lambda_fw")[0], "lru_lambda_bw": f("lru_lambda_bw")[0],
        "ret_decay_fw": f("ret_decay_fw")[0], "ret_decay_bw": f("ret_decay_bw")[0], "ret_gn_w": f("ret_gn_w")[0],
        "w_out": f("w_out")[0], "norm2_w": f("norm2_w")[0], "ffn_w_gate": f("ffn_w_gate")[0], "ffn_w_up": f("ffn_w_up")[0],
        "ffn_conv_w": f("ffn_conv_w")[0].reshape(9, FH), "ffn_conv_b": f("ffn_conv_b")[0], "ffn_w_down": f("ffn_w_down")[0],
        "final_norm_w": f("final_norm_w"),
    }
    shared = {k: np.ascontiguousarray(v) for k, v in shared.items()}
    in_maps = []
    for ci in range(ncores):
        mp = dict(shared)
        mp["xs"] = x_sample[ci]
        mp["xp"] = np.ascontiguousarray(x_prompt[ci * NP:(ci + 1) * NP].reshape(NP * TP, D))
        mp["st_lf"] = np.ascontiguousarray(f("state_lru_fw")[ci, 0])
        mp["st_lb"] = np.ascontiguousarray(f("state_lru_bw")[ci, 0])
        mp["st_rf"] = np.ascontiguousarray(f("state_ret_fw")[ci, 0])
        mp["st_rb"] = np.ascontiguousarray(f("state_ret_bw")[ci, 0])
        mp["c"] = np.ascontiguousarray(f("c")[ci])
        in_maps.append(mp)
    res = run_bass_kernel_spmd(nc, in_maps, core_ids=list(range(ncores)))
    R = res.results
    y_prompt = np.concatenate([r["yp"].reshape(NP, TP, D) for r in R], axis=0)
    y_sample = np.stack([r["ys"] for r in R], axis=0)
    new_lf = np.concatenate([r["nlf"].reshape(NP, 1, LW) for r in R], axis=0)
    new_lb = np.concatenate([r["nlb"].reshape(NP, 1, LW) for r in R], axis=0)
    new_rf = np.concatenate([r["nrf"].reshape(NP, 1, 4, 128, 128) for r in R], axis=0)
    new_rb = np.concatenate([r["nrb"].reshape(NP, 1, 4, 128, 128) for r in R], axis=0)
    return (y_prompt.astype(np.float32), y_sample.astype(np.float32), new_lf.astype(np.float32), new_lb.astype(np.float32),
            new_rf.astype(np.float32), new_rb.astype(np.float32))
```

```python
import numpy as np
from contextlib import ExitStack
import concourse.bass as bass
import concourse.mybir as mybir
from concourse.bass_utils import run_bass_kernel_spmd

F32 = mybir.dt.float32
BF16 = mybir.dt.bfloat16
AF = mybir.ActivationFunctionType
ALU = mybir.AluOpType
D = 1024
LW = 512
FH = 2816
NCC = FH // 128
EPS = 1e-6
GW = 64
DEFAULT_FILL = 0
DEFAULT_FILLN = 512
DEFAULT_POOLCC = ""


_UNIQ = [0]


class Buf:
    def __init__(self, name):
        _UNIQ[0] += 1
        self.name = "%s_%d" % (name, _UNIQ[0])
        self.wtok = None
        self.rtoks = {}
        self.dsem = None
        self.dcnt = 0
        self.excl = False


class Tl:
    def __init__(self, t, name):
        self.t = t
        self.b = Buf(name)

    def __getitem__(self, k):
        return self.t[k]


def _b(x):
    return x.b if isinstance(x, Tl) else x


def I(name, *args, **kwargs):
    return lambda e: getattr(e, name)(*args, **kwargs)


class Sched:
    def __init__(self, nc, es):
        self.nc = nc
        self.es = es
        self.engs = ["pe", "act", "dve", "pool", "sp"]
        self.ops = {e: [] for e in self.engs}
        self.sem = {}
        self.cnt = {e: 0 for e in self.engs}
        self.seen = {e: {} for e in self.engs}
        self.dbufs = []
        self.fill_n = 0
        self.fill_fn = None
        for e in self.engs:
            self.sem[e] = es.enter_context(nc.semaphore("prog_" + e))

    def _need(self, eng, tok, waits):
        if tok is None:
            return
        key, val = tok
        if self.seen[eng].get(key, 0) >= val:
            return
        self.seen[eng][key] = val
        waits.append(tok)

    def _deps(self, eng, reads, writes):
        waits = []
        for b in reads:
            self._need(eng, b.wtok, waits)
        for b in writes:
            self._need(eng, b.wtok, waits)
            for t in b.rtoks.values():
                self._need(eng, t, waits)
        return waits

    def op(self, eng, fns, reads=(), writes=()):
        reads = [_b(x) for x in reads]
        writes = [_b(x) for x in writes]
        writes = writes + [b for b in reads if b.excl and b not in writes]
        reads = [b for b in reads if not b.excl]
        if callable(fns):
            fns = [fns]
        waits = self._deps(eng, reads, writes)
        self.cnt[eng] += 1
        tok = (eng, self.cnt[eng])
        for b in reads:
            b.rtoks[eng] = tok
        for b in writes:
            b.wtok = tok
            b.rtoks = {}
        pre = [self.fill_fn] * self.fill_n if (eng == "pe" and self.fill_n > 0 and self.fill_fn is not None and fns) else []
        self.ops[eng].append(("op", waits, fns, pre))
        return tok

    def dma(self, eng, fn, reads=(), writes=()):
        reads = [_b(x) for x in reads]
        writes = [_b(x) for x in writes]
        waits = self._deps(eng, reads, writes)
        owner = writes[0]
        if owner.dsem is None:
            owner.dsem = self.es.enter_context(self.nc.semaphore("d_" + owner.name))
            self.sem["d_" + owner.name] = owner.dsem
            self.dbufs.append(owner)
        owner.dcnt += 16
        tok = ("d_" + owner.name, owner.dcnt)
        for b in reads:
            b.rtoks["d_" + owner.name] = tok
        for b in writes:
            b.wtok = tok
            b.rtoks = {}
        self.ops[eng].append(("dma", waits, [fn], owner))
        return tok

    def wait_for(self, eng, bufs):
        waits = []
        for b in bufs:
            self._need(eng, _b(b).wtok, waits)
        self.ops[eng].append(("op", waits, [], None))

    def barrier(self):
        toks = [(e, self.cnt[e]) for e in self.engs if self.cnt[e] > 0]
        toks += [("d_" + b.name, b.dcnt) for b in self.dbufs]
        for e in self.engs:
            waits = []
            for t in toks:
                self._need(e, t, waits)
            self.ops[e].append(("op", waits, [], None))

    def emit(self):
        nc = self.nc
        import os
        if os.environ.get("KDBG"):
            print("emit: ops per engine", {e: sum(len(o[2]) for o in self.ops[e]) for e in self.engs},
                  "waits", {e: sum(len(o[1]) for o in self.ops[e]) for e in self.engs}, "cnt", dict(self.cnt), "nsem", len(self.sem), flush=True)
        with nc.allow_non_contiguous_dma(reason="small parameter / state layouts"), nc.Block() as block:
            def replay(e, name):
                for kind, waits, fns, owner in self.ops[name]:
                    if kind == "op" and owner:
                        for f in owner:
                            f(e)
                    for key, val in waits:
                        e.wait_ge(self.sem[key], val)
                    ins = None
                    for f in fns:
                        ins = f(e)
                    if ins is not None:
                        if kind == "dma":
                            ins.then_inc(owner.dsem, 16)
                        else:
                            ins.then_inc(self.sem[name], 1)

            @block.tensor
            def _(e):
                replay(e, "pe")

            @block.scalar
            def _(e):
                replay(e, "act")

            @block.vector
            def _(e):
                replay(e, "dve")

            @block.gpsimd
            def _(e):
                replay(e, "pool")

            @block.sync
            def _(e):
                replay(e, "sp")
        self.ops = {e: [] for e in self.engs}


def build_nc(TS=4096, NP=4, TP=256, dbg=False):
    import os
    KSTOP = os.environ.get("KSTOP", "")
    FLVL = int(os.environ.get("FLVL", "9"))
    KDBG = bool(os.environ.get("KDBG"))
    GELU = AF.Gelu if os.environ.get("GELU") == "erf" else AF.Gelu_apprx_tanh
    NOQ = os.environ.get("NOQ") == "1"
    POOL_CCS = set(int(x) for x in os.environ.get("POOLCC", DEFAULT_POOLCC).split(",") if x)
    FSUB = os.environ.get("FSUB", "gqkt")
    nc = bass.Bass("TRN2", target_bir_lowering=False)

    def inp(name, shape):
        return nc.dram_tensor(name, list(shape), F32, kind="ExternalInput").ap()

    def outp(name, shape):
        return nc.dram_tensor(name, list(shape), F32, kind="ExternalOutput").ap()

    xs = inp("xs", [TS, D])
    xp = inp("xp", [NP * TP, D])
    st_lf = inp("st_lf", [LW])
    st_lb = inp("st_lb", [LW])
    st_rf = inp("st_rf", [4, 128, 128])
    st_rb = inp("st_rb", [4, 128, 128])
    c_in = inp("c", [D])
    cx_in = inp("c_ctx", [D])
    norm1_w = inp("norm1_w", [D])
    w_mod = inp("w_mod", [D, 6 * D])
    b_mod = inp("b_mod", [6 * D])
    w_in = inp("w_in", [D, 3072])
    lru_conv_w = inp("lru_conv_w", [4, LW])
    lru_conv_b = inp("lru_conv_b", [LW])
    lru_w = [inp(n, [8, 64, 64]) for n in ("lru_wa_fw", "lru_wx_fw", "lru_wa_bw", "lru_wx_bw")]
    lru_bias = [inp(n, [LW]) for n in ("lru_ba_fw", "lru_bx_fw", "lru_ba_bw", "lru_bx_bw")]
    lam_f = inp("lru_lambda_fw", [LW])
    lam_b = inp("lru_lambda_bw", [LW])
    dec_f = inp("ret_decay_fw", [4])
    dec_b = inp("ret_decay_bw", [4])
    ret_gn_w = inp("ret_gn_w", [LW])
    w_out = inp("w_out", [D, D])
    norm2_w = inp("norm2_w", [D])
    w_gate = inp("ffn_w_gate", [D, FH])
    w_up = inp("ffn_w_up", [D, FH])
    ffn_conv_w = inp("ffn_conv_w", [9, FH])
    ffn_conv_b = inp("ffn_conv_b", [FH])
    w_down = inp("ffn_w_down", [FH, D])
    fnw = inp("final_norm_w", [D])

    ys = outp("ys", [TS, D])
    yp = outp("yp", [NP * TP, D])
    nlf = outp("nlf", [NP, LW])
    nlb = outp("nlb", [NP, LW])
    nrf = outp("nrf", [NP, 4, 128, 128])
    nrb = outp("nrb", [NP, 4, 128, 128])

    es = ExitStack()
    S = Sched(nc, es)
    outs_toks = []
    OB = {}

    def obuf(ap, name):
        if name not in OB:
            OB[name] = Tl(ap, name)
        return OB[name]

    def sb(name, shape, dtype=F32):
        return Tl(es.enter_context(nc.sbuf_tensor(name, list(shape), dtype)), name)

    esBF = ExitStack()
    esS = ExitStack()

    def sbS(name, shape, dtype=F32):
        return Tl(esS.enter_context(nc.sbuf_tensor(name, list(shape), dtype)), name)

    def sbBF(name, shape, dtype=F32):
        return Tl(esBF.enter_context(nc.sbuf_tensor(name, list(shape), dtype)), name)

    def ps(name, shape, dtype=F32):
        t = Tl(es.enter_context(nc.psum_tensor(name, list(shape), dtype)), name)
        t.b.excl = True
        return t

    def dr(name, shape, dtype=F32):
        return Tl(nc.dram_tensor(name, list(shape), dtype, kind="Internal").ap(), name)

    segs = [dict(name="s", kind="s", T=TS, nt=512, xin=xs, yout=ys, m=0, idx=0)]
    for i in range(NP):
        segs.append(dict(name="p%d" % i, kind="p", T=TP, nt=TP, xin=xp[i * TP:(i + 1) * TP, :],
                         yout=yp[i * TP:(i + 1) * TP, :], m=1, idx=i))
    for sg in segs:
        T = sg["T"]
        n = sg["name"]
        sg["lx"] = dr("lx_" + n, [4, 128, T + 3])
        sg["h1"] = dr("h1_" + n, [8, 128, T], BF16)
        sg["h1"].b = sg["lx"].b
        sg["hb"] = dr("hb_" + n, [4, 128, T])
        sg["Sb"] = dr("Sb_" + n, [T // 128, 128, 512], BF16)
        sg["x1"] = dr("x1_" + n, [T, D])
        sg["h2"] = dr("h2_" + n, [8, 128, T + 128], BF16)
        sg["yb"] = [Tl(sg["yout"], "y_%s_%d" % (n, k)) for k in range(2)]
        sg["blocks"] = [(t0, sg["nt"]) for t0 in range(0, T, sg["nt"])]
        sg["ntb"] = min(256, sg["nt"])
        sg["blocksBF"] = [(t0, sg["ntb"]) for t0 in range(0, T, sg["ntb"])]
        sg["ntB"] = sg["nt"]
        sg["blocksB"] = [(t0, sg["ntB"]) for t0 in range(0, T, sg["ntB"])]

    g2d = dr("g2d", [2, D])
    PB = [ps("pb%d" % i, [128, 512]) for i in range(6)]
    PT = [ps("ptA", [128, 1024], BF16), ps("ptB", [128, 1024], BF16)]
    pp_rr = [0, 6 if int(os.environ.get("FILL", str(DEFAULT_FILL))) == 0 else 5]
    PJ = PB[5]

    def PP():
        k = pp_rr[0] % pp_rr[1]
        pp_rr[0] += 1
        return PB[k]

    PS_, PO_, PKV_ = PB[3], PB[4], PB[5]

    ident = sb("ident", [128, 128], BF16)
    zerof = sb("zerof", [128, 512])
    zerob = sb("zerob", [128, 512], BF16)
    epsc = sb("epsc", [128, 1])
    mhalf = sb("mhalf", [128, 8])
    qtr = sb("qtr", [128, 1])
    fcw = sb("fcw", [128, 9 * NCC])
    fcb = sb("fcb", [128, NCC])
    AB = sb("AB", [128, 4, 2, 8])
    cw = sbBF("cw", [128, 16])
    cb = sbBF("cb", [128, 4])
    lbias = sbBF("lbias", [128, 16])
    cf = sbBF("cf", [128, 8])
    gnw = sbBF("gnw", [128, 4])
    g1bc = [sbBF("g1bc%d" % m, [128, D]) for m in range(2)]
    Wbd = sbBF("Wbd", [128, 16, 128], BF16)
    g128 = sbBF("g128", [128, 8])
    Dm = sbBF("Dm", [128, 4, 128])
    HF = sbBF("HF", [128, 4, 128])
    HB = sbBF("HB", [128, 4, 128])
    TFm = sbBF("TFm", [128, 4, 128])
    TBm = sbBF("TBm", [128, 4, 128])
    h0 = sbBF("h0", [128, 8])
    g2bc = [sbS("g2bc%d" % m, [128, D]) for m in range(2)]
    identf = sbS("identf", [128, 128])
    ones_bf = sbS("ones_bf", [128, 128], BF16)
    onesf = sbS("onesf", [128, 128])
    n1w = sbS("n1w", [128, 8])
    n2w = sbS("n2w", [128, 8])
    bm = sbS("bm", [128, 48])
    lam = sbS("lam", [128, 8])
    cT = sbS("cT", [128, 16])
    scf = sbS("scf", [128, 16])
    scT = sbS("scT", [128, 8, 2], BF16)
    bcT = sbS("bcT", [128, 16, 128], BF16)
    modF = sbS("modF", [128, 32, 2])
    bg = sbS("bg", [128, 2, D])
    dec = sbS("dec", [128, 8])
    lg = sbS("lg", [128, 8])
    rel = sbS("rel", [128, 128])
    tmpA = sbS("tmpA", [128, 128])
    tmpB = sbS("tmpB", [128, 128])
    mge = sbS("mge", [128, 128])
    mle = sbS("mle", [128, 128])
    pj = sbS("pj", [128, 2])
    tl8 = sbS("tl8", [128, 8])

    def ldcol(dst, dst_ap, src1d):
        S.dma("sp", I("dma_start", out=dst_ap, in_=src1d.rearrange("(n p) -> p n", p=128)), writes=[dst])

    S.op("pool", I("memset", identf[:], 0.0), writes=[identf])
    S.op("pool", I("affine_select", out=identf[:], in_=identf[:], pattern=[[-1, 128]], compare_op=ALU.not_equal,
                                           fill=1.0, base=0, channel_multiplier=1), reads=[identf], writes=[identf])
    S.op("dve", I("tensor_copy", out=ident[:], in_=identf[:]), reads=[identf], writes=[ident])
    S.op("pool", I("memset", onesf[:], 1.0), writes=[onesf])
    S.op("pool", I("memset", zerof[:], 0.0), writes=[zerof])
    S.op("pool", I("memset", zerob[:], 0.0), writes=[zerob])
    S.op("pool", I("memset", epsc[:], EPS), writes=[epsc])
    S.op("pool", I("memset", mhalf[:], -0.5), writes=[mhalf])
    S.op("pool", I("memset", qtr[:], 0.25), writes=[qtr])

    ldcol(n1w, n1w[:], norm1_w)
    ldcol(n2w, n2w[:], norm2_w)
    ldcol(bm, bm[:], b_mod)
    for j in range(4):
        ldcol(cw, cw[:, j * 4:(j + 1) * 4], lru_conv_w[j])
    ldcol(cb, cb[:], lru_conv_b)
    for k in range(4):
        ldcol(lbias, lbias[:, k * 4:(k + 1) * 4], lru_bias[k])
    ldcol(lam, lam[:, 0:4], lam_f)
    ldcol(lam, lam[:, 4:8], lam_b)
    for tp in range(9):
        ldcol(fcw, fcw[:, tp * NCC:(tp + 1) * NCC], ffn_conv_w[tp])
    ldcol(fcb, fcb[:], ffn_conv_b)
    ldcol(gnw, gnw[:], ret_gn_w)
    ldcol(cT, cT[:, 0:8], c_in)
    ldcol(cT, cT[:, 8:16], cx_in)
    ldcol(h0, h0[:, 0:4], st_lf)
    ldcol(h0, h0[:, 4:8], st_lb)
    S.dma("sp", I("dma_start", out=bg[:, 0, :], in_=b_mod[2 * D:3 * D].partition_broadcast(128)), writes=[bg])
    S.dma("sp", I("dma_start", out=bg[:, 1, :], in_=b_mod[5 * D:6 * D].partition_broadcast(128)), writes=[bg])
    S.dma("sp", I("dma_start", out=dec[:, 0:4], in_=dec_f.partition_broadcast(128)), writes=[dec])
    S.dma("sp", I("dma_start", out=dec[:, 4:8], in_=dec_b.partition_broadcast(128)), writes=[dec])

    S.op("pool", I("memset", Wbd[:], 0.0), writes=[Wbd])
    for mi in range(4):
        for half in range(2):
            src = lru_w[mi].rearrange("(n two) c d -> two c n d", two=2)[half]
            S.dma("pool", I("dma_start",
                out=Wbd[half * 64:(half + 1) * 64, mi * 4:(mi + 1) * 4, half * 64:(half + 1) * 64], in_=src), writes=[Wbd])

    S.op("act", I("activation", out=cf[:], in_=lam[:], func=AF.Exp, scale=-1.0), reads=[lam], writes=[cf])
    S.op("act", I("activation", out=cf[:], in_=cf[:], func=AF.Ln, bias=1.0), reads=[cf], writes=[cf])
    S.op("dve", I("tensor_scalar", out=cf[:], in0=cf[:], scalar1=-4.0, scalar2=None, op0=ALU.mult), reads=[cf], writes=[cf])
    S.op("dve", I("tensor_scalar", out=lbias[:], in0=lbias[:], scalar1=0.5, scalar2=None, op0=ALU.mult), reads=[lbias], writes=[lbias])
    S.op("dve", I("tensor_scalar", out=gnw[:], in0=gnw[:], scalar1=0.5, scalar2=None, op0=ALU.mult), reads=[gnw], writes=[gnw])
    S.op("act", I("activation", out=lg[:], in_=dec[:], func=AF.Exp, scale=-1.0), reads=[dec], writes=[lg])
    S.op("act", I("activation", out=lg[:], in_=lg[:], func=AF.Ln, bias=1.0), reads=[lg], writes=[lg])
    S.op("dve", I("tensor_scalar", out=lg[:], in0=lg[:], scalar1=-1.0, scalar2=None, op0=ALU.mult), reads=[lg], writes=[lg])
    S.op("act", I("activation", out=g128[:], in_=lg[:], func=AF.Exp, scale=128.0), reads=[lg], writes=[g128])
    S.op("pool", I("iota", rel[:], pattern=[[1, 128]], base=0, channel_multiplier=-1,
                                  allow_small_or_imprecise_dtypes=True), writes=[rel])
    S.op("dve", I("tensor_scalar", out=mge[:], in0=rel[:], scalar1=0.0, scalar2=None, op0=ALU.is_ge), reads=[rel], writes=[mge])
    S.op("dve", I("tensor_scalar", out=mle[:], in0=rel[:], scalar1=0.0, scalar2=None, op0=ALU.is_le), reads=[rel], writes=[mle])
    for h in range(4):
        S.op("dve", I("tensor_scalar", out=tmpA[:], in0=rel[:], scalar1=0.0, scalar2=None, op0=ALU.max), reads=[rel], writes=[tmpA])
        S.op("act", I("activation", out=tmpA[:], in_=tmpA[:], func=AF.Exp, scale=lg[:, h:h + 1]), reads=[tmpA, lg], writes=[tmpA])
        S.op("dve", I("tensor_tensor", out=tmpA[:], in0=tmpA[:], in1=mge[:], op=ALU.mult), reads=[tmpA, mge], writes=[tmpA])
        S.op("dve", I("tensor_scalar", out=tmpB[:], in0=rel[:], scalar1=-1.0, scalar2=0.0, op0=ALU.mult, op1=ALU.max), reads=[rel], writes=[tmpB])
        S.op("act", I("activation", out=tmpB[:], in_=tmpB[:], func=AF.Exp, scale=lg[:, 4 + h:5 + h]), reads=[tmpB, lg], writes=[tmpB])
        S.op("dve", I("tensor_tensor", out=tmpB[:], in0=tmpB[:], in1=mle[:], op=ALU.mult), reads=[tmpB, mle], writes=[tmpB])
        S.op("dve", I("tensor_tensor", out=Dm[:, h, :], in0=tmpA[:], in1=tmpB[:], op=ALU.add), reads=[tmpA, tmpB], writes=[Dm])
    S.op("pool", I("iota", tmpA[:], pattern=[[1, 128]], base=1, channel_multiplier=0,
                                  allow_small_or_imprecise_dtypes=True), reads=[Dm], writes=[tmpA])
    S.op("pool", I("iota", tmpB[:], pattern=[[-1, 128]], base=128, channel_multiplier=0,
                                  allow_small_or_imprecise_dtypes=True), reads=[Dm], writes=[tmpB])
    for h in range(4):
        S.op("act", I("activation", out=HF[:, h, :], in_=tmpA[:], func=AF.Exp, scale=lg[:, h:h + 1]), reads=[tmpA, lg], writes=[HF])
        S.op("act", I("activation", out=HB[:, h, :], in_=tmpB[:], func=AF.Exp, scale=lg[:, 4 + h:5 + h]), reads=[tmpB, lg], writes=[HB])
    S.op("pool", I("iota", pj[:, 0:1], pattern=[[0, 1]], base=127, channel_multiplier=-1,
                                  allow_small_or_imprecise_dtypes=True), writes=[pj])
    S.op("pool", I("iota", pj[:, 1:2], pattern=[[0, 1]], base=0, channel_multiplier=1,
                                  allow_small_or_imprecise_dtypes=True), reads=[pj], writes=[pj])
    S.op("dve", I("tensor_scalar", out=tl8[:, 0:4], in0=lg[:, 0:4], scalar1=pj[:, 0:1], scalar2=None, op0=ALU.mult), reads=[lg, pj], writes=[tl8])
    S.op("dve", I("tensor_scalar", out=tl8[:, 4:8], in0=lg[:, 4:8], scalar1=pj[:, 1:2], scalar2=None, op0=ALU.mult), reads=[lg, pj, tl8], writes=[tl8])
    S.op("act", I("activation", out=tl8[:], in_=tl8[:], func=AF.Exp), reads=[tl8], writes=[tl8])
    for h in range(4):
        S.op("dve", I("tensor_scalar", out=TFm[:, h, :], in0=onesf[:], scalar1=tl8[:, h:h + 1], scalar2=None, op0=ALU.mult), reads=[onesf, tl8], writes=[TFm])
        S.op("dve", I("tensor_scalar", out=TBm[:, h, :], in0=onesf[:], scalar1=tl8[:, 4 + h:5 + h], scalar2=None, op0=ALU.mult), reads=[onesf, tl8], writes=[TBm])

    S.wait_for("pe", [ident, zerob])
    S.fill_n = int(os.environ.get("FILL", str(DEFAULT_FILL)))
    FILLN = int(os.environ.get("FILLN", str(DEFAULT_FILLN)))
    S.fill_fn = I("matmul", PJ[:, 0:FILLN], lhsT=ident[:], rhs=zerob[:, 0:FILLN], start=True, stop=True)
    S.op("act", I("activation", out=scf[:], in_=cT[:], func=AF.Silu), reads=[cT], writes=[scf])
    for m in range(2):
        S.op("dve", I("tensor_copy", out=scT[:, :, m], in_=scf[:, m * 8:(m + 1) * 8]), reads=[scf], writes=[scT])
    for m in range(2):
        for kc in range(8):
            S.op("dve", I("tensor_scalar", out=bcT[:, m * 8 + kc, :], in0=onesf[:], scalar1=scf[:, m * 8 + kc:m * 8 + kc + 1],
                                                              scalar2=None, op0=ALU.mult), reads=[onesf, scf], writes=[bcT])
    if True:
        wm = [sbS("wm%d" % i, [128, 8, D], BF16) for i in range(2)]
        PM = PB[3]
        fidx = {0: 0, 1: 1, 3: 2, 4: 3}
        for g in range(6):
            w = wm[g % 2]
            S.dma("pool", I("dma_start", out=w[:], in_=w_mod[:, g * D:(g + 1) * D].rearrange("(kc p) n -> p kc n", p=128)), writes=[w])
            if g in fidx:
                fi = fidx[g]
                fns = []
                for ncx in range(8):
                    for kc in range(8):
                        fns.append(I("matmul",
                            PM[:, (fi * 8 + ncx) * 2:(fi * 8 + ncx) * 2 + 2], lhsT=w[:, kc, ncx * 128:(ncx + 1) * 128], rhs=scT[:, kc, :],
                            start=(kc == 0), stop=(kc == 7)))
                S.op("pe", fns, reads=[w, scT], writes=[PM])
                for m in range(2):
                    S.op("dve", I("tensor_tensor",
                        out=modF[:, fi * 8:(fi + 1) * 8, m], in0=PM[:, fi * 16:(fi + 1) * 16].rearrange("p (n m) -> p n m", m=2)[:, :, m],
                        in1=bm[:, g * 8:(g + 1) * 8], op=ALU.add), reads=[PM, bm], writes=[modF])
            else:
                gi = 0 if g == 2 else 1
                dst = g1bc if g == 2 else g2bc
                for m in range(2):
                    for nh in range(2):
                        pp = PP()
                        fns = [I("matmul",
                            pp[:], lhsT=bcT[:, m * 8 + kc, :], rhs=w[:, kc, nh * 512:(nh + 1) * 512], start=(kc == 0), stop=(kc == 7))
                            for kc in range(8)]
                        S.op("pe", fns, reads=[w, bcT], writes=[pp])
                        S.op("dve", I("tensor_tensor",
                            out=dst[m][:, nh * 512:(nh + 1) * 512], in0=pp[:], in1=bg[:, gi, nh * 512:(nh + 1) * 512], op=ALU.add),
                            reads=[pp, bg], writes=[dst[m]])
        for m in range(2):
            S.op("dve", I("scalar_tensor_tensor", out=AB[:, 0, m, :], in0=modF[:, 8:16, m], scalar=1.0, in1=n1w[:],
                                                              op0=ALU.add, op1=ALU.mult), reads=[modF, n1w], writes=[AB])
            S.op("dve", I("tensor_copy", out=AB[:, 1, m, :], in_=modF[:, 0:8, m]), reads=[modF], writes=[AB])
            S.op("dve", I("scalar_tensor_tensor", out=AB[:, 2, m, :], in0=modF[:, 24:32, m], scalar=1.0, in1=n2w[:],
                                                              op0=ALU.add, op1=ALU.mult), reads=[modF, n2w], writes=[AB])
            S.op("dve", I("tensor_copy", out=AB[:, 3, m, :], in_=modF[:, 16:24, m]), reads=[modF], writes=[AB])
        for m in range(2):
            S.dma("sp", I("dma_start", out=g2d[m:m + 1, :], in_=g2bc[m][0:1, :]), reads=[g2bc[m]], writes=[g2d])
        S.barrier()
        S.emit()
    esS.close()
    if KSTOP == "setup":
        esBF.close()
        es.close()
        return nc

    def norm_mod_T(xtm, ntt, m, which, hT, scr):
        ssq, rstd, xn, junk = scr
        for tt in range(ntt):
            S.op("act", I("activation", out=junk[:], in_=xtm[:, tt, :], func=AF.Square, accum_out=ssq[:, tt:tt + 1]),
                 reads=[xtm], writes=[junk, ssq])
        S.op("dve", I("tensor_scalar", out=rstd[:, 0:ntt], in0=ssq[:, 0:ntt], scalar1=1.0 / D, scalar2=EPS, op0=ALU.mult, op1=ALU.add),
             reads=[ssq], writes=[rstd])
        S.op("pool", I("tensor_tensor", out=rstd[:, 0:ntt], in0=rstd[:, 0:ntt], in1=mhalf[:, 0:ntt], op=ALU.pow), reads=[rstd, mhalf], writes=[rstd])
        yield
        for tt in range(ntt):
            S.op("act", I("activation", out=xn[:, tt, :], in_=xtm[:, tt, :], func=AF.Copy, scale=rstd[:, tt:tt + 1]),
                 reads=[xtm, rstd], writes=[xn])
        yield
        for kc in range(8):
            pt = PT[kc % 2]
            fns = [I("transpose", out=pt[:, tt * 128:(tt + 1) * 128], in_=xn[:, tt, kc * 128:(kc + 1) * 128], identity=ident[:])
                   for tt in range(ntt)]
            S.op("pe", fns, reads=[xn, ident], writes=[pt])
            S.op("dve", I("tensor_scalar", out=hT[:, kc, 0:ntt * 128], in0=pt[:, 0:ntt * 128], scalar1=AB[:, which, m, kc:kc + 1],
                                                                scalar2=AB[:, which + 1, m, kc:kc + 1], op0=ALU.mult, op1=ALU.add),
                 reads=[pt, AB], writes=[hT])
            yield

    def run_merged(gens, ratio=None, after=None):
        ratio = ratio or (1,) * len(gens)
        after = after or {}
        live = list(range(len(gens)))
        while live:
            for gi in list(live):
                if gi in after and after[gi] in live:
                    continue
                for _ in range(ratio[gi]):
                    try:
                        next(gens[gi])
                    except StopIteration:
                        live.remove(gi)
                        break

    lru_cnt = [0]

    def lru_dir(sg, t0, nt, d, L, hstate, ret):
        lxw, xcs, xcbs, rrs, iis, a4, v4, t4, hhs = L
        hh = hhs[lru_cnt[0] % len(hhs)] if isinstance(hhs, list) else hhs
        lru_cnt[0] += 1
        ret.append(hh)
        lxd = sg["lx"]
        S.dma("sp", I("dma_start", out=lxw[:, :, 0:nt + 3], in_=lxd[:, :, t0:t0 + nt + 3].rearrange("c p t -> p c t")),
              reads=[lxd], writes=[lxw])
        for cc in range(4):
            xc, xcb, rr, ii = xcs[cc % 2], xcbs[cc % 2], rrs[cc % 2], iis[cc % 2]
            S.op("dve", I("tensor_scalar", out=xc[:, 0:nt], in0=lxw[:, cc, 0:nt], scalar1=cw[:, cc:cc + 1], scalar2=cb[:, cc:cc + 1],
                          op0=ALU.mult, op1=ALU.add), reads=[lxw, cw, cb], writes=[xc])
            for j in range(1, 4):
                S.op("dve", I("scalar_tensor_tensor", out=xc[:, 0:nt], in0=lxw[:, cc, j:j + nt], scalar=cw[:, j * 4 + cc:j * 4 + cc + 1],
                              in1=xc[:, 0:nt], op0=ALU.mult, op1=ALU.add), reads=[lxw, cw, xc], writes=[xc])
            S.op("act", I("activation", out=xcb[:, 0:nt], in_=xc[:, 0:nt], func=AF.Copy), reads=[xc], writes=[xcb])
            yield
            for gi, dst in ((0, rr), (1, ii)):
                pp = PP()
                mi = d * 2 + gi
                S.op("pe", I("matmul", pp[:, 0:nt], lhsT=Wbd[:, mi * 4 + cc, :], rhs=xcb[:, 0:nt], start=True, stop=True),
                     reads=[Wbd, xcb], writes=[pp])
                S.op("act", I("activation", out=dst[:, 0:nt], in_=pp[:, 0:nt], func=AF.Tanh, scale=0.5,
                              bias=lbias[:, mi * 4 + cc:mi * 4 + cc + 1]), reads=[pp, lbias], writes=[dst])
                yield
            S.op("act", I("activation", out=a4[:, cc, 0:nt], in_=rr[:, 0:nt], func=AF.Exp, scale=cf[:, d * 4 + cc:d * 4 + cc + 1],
                          bias=cf[:, d * 4 + cc:d * 4 + cc + 1]), reads=[rr, cf], writes=[a4])
            S.op("dve", I("scalar_tensor_tensor", out=t4[:, cc, 0:nt], in0=ii[:, 0:nt], scalar=1.0, in1=xc[:, 0:nt], op0=ALU.add, op1=ALU.mult),
                 reads=[ii, xc], writes=[t4])
            yield
        S.op("act", I("activation", out=v4[:, :, 0:nt], in_=a4[:, :, 0:nt], func=AF.Square), reads=[a4], writes=[v4])
        S.op("act", I("activation", out=v4[:, :, 0:nt], in_=v4[:, :, 0:nt], func=AF.Sqrt, scale=-0.25, bias=qtr[:]), reads=[v4, qtr], writes=[v4])
        yield
        S.op("dve", I("tensor_tensor", out=t4[:, :, 0:nt], in0=t4[:, :, 0:nt], in1=v4[:, :, 0:nt], op=ALU.mult), reads=[t4, v4], writes=[t4])
        yield
        for cc in range(4):
            if d == 0:
                S.op("dve", I("tensor_tensor_scan", out=hh[:, cc, 0:nt], data0=a4[:, cc, 0:nt], data1=t4[:, cc, 0:nt],
                              initial=hstate[:, cc:cc + 1], op0=ALU.mult, op1=ALU.add), reads=[a4, t4, hstate], writes=[hh])
            else:
                S.op("dve", I("tensor_tensor_scan", out=hh[:, cc, nt - 1::-1], data0=a4[:, cc, nt - 1::-1], data1=t4[:, cc, nt - 1::-1],
                              initial=hstate[:, 4 + cc:5 + cc], op0=ALU.mult, op1=ALU.add), reads=[a4, t4, hstate], writes=[hh])
            yield
        edge = nt - 1 if d == 0 else 0
        S.op("dve", I("tensor_copy", out=hstate[:, d * 4:(d + 1) * 4], in_=hh[:, :, edge]), reads=[hh], writes=[hstate])
        yield

    with ExitStack() as es2:
        def sb2(name, shape, dtype=F32):
            return Tl(es2.enter_context(nc.sbuf_tensor(name, list(shape), dtype)), name)
        winB = sb2("winB", [128, 8, 1536], BF16)
        xtm2 = [sb2("xtmB%d" % i, [128, 4, D]) for i in range(2)]
        xnB = sb2("xnB", [128, 4, D], BF16)
        junkB = sb2("junkB", [128, D], BF16)
        scr2 = [(sb2("ssqB%d" % i, [128, 4]), sb2("rstdB%d" % i, [128, 4]), xnB, junkB) for i in range(2)]
        hT2 = [sb2("hTB%d" % i, [128, 8, 512], BF16) for i in range(2)]
        lxo2 = [sb2("lxoB", [128, 4, 512])] * 2
        Ktok2 = [sb2("KtokB%d" % i, [128, 4, 512], BF16) for i in range(2)]
        VBtok2 = [sb2("VBtokB%d" % i, [128, 4, 512], BF16) for i in range(2)]
        Sb = sb2("SbB", [128, 4, 128])
        Sbb2 = [sb2("SbbB%d" % i, [128, 512], BF16) for i in range(2)]
        hst = sb2("hstB", [128, 8])
        L = (sb2("lxwB", [128, 4, 515]), [sb2("xcB%d" % i, [128, 512]) for i in range(2)], [sb2("xcbB%d" % i, [128, 512], BF16) for i in range(2)],
             [sb2("rrB%d" % i, [128, 512]) for i in range(2)], [sb2("iiB%d" % i, [128, 512]) for i in range(2)],
             sb2("a4B", [128, 4, 512]), sb2("v4B", [128, 4, 512]), sb2("t4B", [128, 4, 512]),
             sb2("hhB", [128, 4, 512]))
        if KDBG:
            print("pass B sbuf remaining", nc.sbuf_bytes_remaining, flush=True)
        winBc = []
        for (dst0, src0) in ((0, 0), (512, 1536), (1024, 2048)):
            v = Tl(winB.t[:, :, dst0:dst0 + 512], "winB_%d" % dst0)
            winBc.append(v)
            S.dma("pool", I("dma_start", out=v[:, :, :], in_=w_in[:, src0:src0 + 512].rearrange("(kc p) n -> p kc n", p=128)), writes=[v])
        work = []
        for sg in segs:
            T = sg["T"]
            S.dma("sp", I("dma_start", out=sg["lx"][:, :, 0:2].rearrange("c p t -> p c t"),
                          in_=zerof[:, 0:8].rearrange("p (c t) -> p c t", t=2)), reads=[zerof], writes=[sg["lx"]])
            S.dma("sp", I("dma_start", out=sg["lx"][:, :, T + 2:T + 3].rearrange("c p t -> p c t"),
                          in_=zerof[:, 0:4].rearrange("p (c t) -> p c t", t=1)), reads=[zerof], writes=[sg["lx"]])
            nb = len(sg["blocksB"])
            for bi in range(nb - 1, -1, -1):
                work.append((sg, bi))

        def FE_B(wi):
            sg, bi = work[wi]
            p = wi % 2
            nt, m = sg["ntB"], sg["m"]
            ntt = nt // 128
            t0 = sg["blocksB"][bi][0]
            xtm, scr, hT, lxo, Ktok, VBtok = xtm2[p], scr2[p], hT2[p], lxo2[p], Ktok2[p], VBtok2[p]
            S.dma("sp", I("dma_start", out=xtm[:, 0:ntt, :], in_=sg["xin"][t0:t0 + nt, :].rearrange("(tt p) d -> p tt d", p=128)), writes=[xtm])
            yield
            yield from norm_mod_T(xtm, ntt, m, 0, hT, scr)
            for kc0 in (0, 4):
                S.dma("sp", I("dma_start", out=sg["h1"][kc0:kc0 + 4, :, t0:t0 + nt].rearrange("k p t -> p k t"), in_=hT[:, kc0:kc0 + 4, 0:nt]),
                      reads=[hT], writes=[sg["h1"]])
            yield
            for cc in range(4):
                pp = PP()
                fns = [I("matmul", pp[:, 0:nt], lhsT=winBc[0][:, kc, cc * 128:(cc + 1) * 128], rhs=hT[:, kc, 0:nt],
                         start=(kc == 0), stop=(kc == 7)) for kc in range(8)]
                S.op("pe", fns, reads=[winBc[0], hT], writes=[pp])
                S.op("act", I("activation", out=lxo[:, cc, 0:nt], in_=pp[:, 0:nt], func=AF.Copy), reads=[pp], writes=[lxo])
                yield
            S.dma("sp", I("dma_start", out=sg["lx"][:, :, 2 + t0:2 + t0 + nt].rearrange("c p t -> p c t"), in_=lxo[:, :, 0:nt]),
                  reads=[lxo], writes=[sg["lx"]])
            yield
            for tt in range(ntt):
                for which in range(2):
                    pp = PP()
                    fns = [I("matmul", pp[:], lhsT=hT[:, kc, tt * 128:(tt + 1) * 128], rhs=winBc[1 + which][:, kc, 0:512],
                             start=(kc == 0), stop=(kc == 7)) for kc in range(8)]
                    S.op("pe", fns, reads=[winBc[1 + which], hT], writes=[pp])
                    if which == 0:
                        S.op("act", I("activation", out=Ktok[:, tt, :], in_=pp[:], func=AF.Copy, scale=128.0 ** -0.5), reads=[pp], writes=[Ktok])
                        yield
                    else:
                        S.op("dve", I("tensor_tensor", out=VBtok[:, tt, :], in0=pp[:], in1=TBm[:].rearrange("p h e -> p (h e)"), op=ALU.mult),
                             reads=[pp, TBm], writes=[VBtok])
                        yield

        def BE_Bret(wi):
            sg, bi = work[wi]
            p = wi % 2
            nt = sg["ntB"]
            ntt = nt // 128
            blocks = sg["blocksB"]
            t0 = blocks[bi][0]
            Ktok, VBtok = Ktok2[p], VBtok2[p]
            if bi == len(blocks) - 1:
                if sg["kind"] == "s":
                    S.dma("sp", I("dma_start", out=Sb[:], in_=st_rb.rearrange("h d e -> d h e")), writes=[Sb])
                else:
                    S.op("pool", I("memset", Sb[:], 0.0), writes=[Sb])
                yield
            for tt in range(ntt - 1, -1, -1):
                ci = t0 // 128 + tt
                Sbb = Sbb2[ci % 2]
                S.op("act", I("activation", out=Sbb[:], in_=Sb[:].rearrange("p h e -> p (h e)"), func=AF.Copy), reads=[Sb], writes=[Sbb])
                yield
                S.dma("sp", I("dma_start", out=sg["Sb"][ci], in_=Sbb[:]), reads=[Sbb], writes=[sg["Sb"]])
                yield
                pkv = PP()
                fns = [I("matmul", pkv[:, h * 128:(h + 1) * 128], lhsT=Ktok[:, tt, h * 128:(h + 1) * 128],
                         rhs=VBtok[:, tt, h * 128:(h + 1) * 128], start=True, stop=True) for h in range(4)]
                S.op("pe", fns, reads=[Ktok, VBtok], writes=[pkv])
                for h in range(4):
                    S.op("dve", I("scalar_tensor_tensor", out=Sb[:, h, :], in0=Sb[:, h, :], scalar=g128[:, 4 + h:5 + h],
                                  in1=pkv[:, h * 128:(h + 1) * 128], op0=ALU.mult, op1=ALU.add), reads=[Sb, g128, pkv], writes=[Sb])
                yield
            if bi == 0 and sg["kind"] == "p":
                i = sg["idx"]
                ob = obuf(nrb, "nrb")
                outs_toks.append(S.dma("sp", I("dma_start", out=nrb[i].rearrange("h d e -> d h e"), in_=Sb[:]), reads=[Sb], writes=[ob]))
                yield

        def BE_Blru(wi):
            sg, bi = work[wi]
            nt = sg["ntB"]
            blocks = sg["blocksB"]
            if bi == len(blocks) - 1:
                if sg["kind"] == "s":
                    S.op("pool", I("tensor_copy", out=hst[:], in_=h0[:]), reads=[h0], writes=[hst])
                else:
                    S.op("pool", I("memset", hst[:], 0.0), writes=[hst])
                yield
            todo = []
            if bi + 1 < len(blocks):
                todo.append(bi + 1)
            if bi == 0:
                todo.append(0)
            for bj in todo:
                tj = blocks[bj][0]
                ret = []
                yield from lru_dir(sg, tj, nt, 1, L, hst, ret)
                hh = ret[0]
                if sg["kind"] == "p" and bj == len(blocks) - 1:
                    ob2 = obuf(nlb, "nlb")
                    outs_toks.append(S.dma("sp", I("dma_start", out=nlb[sg["idx"]].rearrange("(n p) -> p n", p=128), in_=hh[:, :, nt - 1]),
                                           reads=[hh], writes=[ob2]))
                S.dma("sp", I("dma_start", out=sg["hb"][:, :, tj:tj + nt].rearrange("c p t -> p c t"), in_=hh[:, :, 0:nt]),
                      reads=[hh], writes=[sg["hb"]])
                yield

        run_merged([FE_B(0)])
        for wi in range(len(work)):
            gens = [BE_Bret(wi), BE_Blru(wi)]
            if wi + 1 < len(work):
                gens.append(FE_B(wi + 1))
            run_merged(gens)
        S.barrier()
        S.emit()
    if KSTOP == "B":
        esBF.close()
        es.close()
        return nc

    with ExitStack() as es2:
        def sb2(name, shape, dtype=F32):
            return Tl(es2.enter_context(nc.sbuf_tensor(name, list(shape), dtype)), name)
        winF = sb2("winF", [128, 8, 2560], BF16)
        wo = sb2("wo", [128, 8, D], BF16)
        xtm2 = [sb2("xtmF%d" % i, [128, 2, D]) for i in range(2)]
        junkF = sb2("junkF", [128, D], BF16)
        xnF = sb2("xnF", [128, 2, D], BF16)
        scr2 = [(sb2("ssqF%d" % i, [128, 4]), sb2("rstdF%d" % i, [128, 4]), xnF, junkF) for i in range(2)]
        scrBE = (sb2("ssqFb", [128, 4]), sb2("rstdFb", [128, 4]), xnF, junkF)
        hT2 = [sb2("hTF%d" % i, [128, 8, 256], BF16) for i in range(2)]
        h2T = sb2("h2TF", [128, 8, 256], BF16)
        gl2 = [sb2("glF%d" % i, [128, 4, 256], BF16) for i in range(2)]
        qT2 = [sb2("qTF%d" % i, [128, 4, 256], BF16) for i in range(2)]
        qf2 = [sb2("qfF%d" % i, [128, 4, 256], BF16) for i in range(2)]
        qb2 = [sb2("qbF%d" % i, [128, 4, 256], BF16) for i in range(2)]
        kT2 = [sb2("kTF%d" % i, [128, 4, 256], BF16) for i in range(2)]
        Ktok2 = [sb2("KtokF%d" % i, [128, 2, 512], BF16) for i in range(2)]
        Vtok2 = [sb2("VtokF%d" % i, [128, 2, 512], BF16) for i in range(2)]
        VFtok2 = [sb2("VFtokF%d" % i, [128, 2, 512], BF16) for i in range(2)]
        rg2 = [sb2("rgF%d" % i, [128, 2, 512], BF16) for i in range(2)]
        Sf = sb2("SfF", [128, 4, 128])
        Sfb2 = [sb2("SfbF%d" % i, [128, 512], BF16) for i in range(2)]
        Sbb2 = [sb2("SbbF%d" % i, [128, 512], BF16) for i in range(2)]
        PTs2 = [sb2("PTsF%d" % i, [128, 512], BF16) for i in range(2)]
        hst = sb2("hstF", [128, 8])
        hbw = sb2("hbwF", [128, 4, 256])
        yT2 = [sb2("yTF%d" % i, [128, 8, 256], BF16) for i in range(2)]
        ytok2 = [sb2("ytokF%d" % i, [128, 512], BF16) for i in range(2)]
        otmp2 = [sb2("otmpF", [128, 512])] * 2
        bst = sb2("bstF", [128, 2, 4, 6])
        bag = sb2("bagF", [128, 2, 4, 2])
        grs = sb2("grsF", [128, 2, 4])
        wtmp2 = [sb2("wtmpF", [128, 512])] * 2
        L = (sb2("lxwF", [128, 4, 259]), [sb2("xcF%d" % i, [128, 256]) for i in range(2)], [sb2("xcbF%d" % i, [128, 256], BF16) for i in range(2)],
             [sb2("rrF%d" % i, [128, 256]) for i in range(2)], [sb2("iiF%d" % i, [128, 256]) for i in range(2)],
             sb2("a4F", [128, 4, 256]), sb2("v4F", [128, 4, 256]), sb2("t4F", [128, 4, 256]), sb2("hhF", [128, 4, 256]))
        if KDBG:
            print("pass F sbuf remaining", nc.sbuf_bytes_remaining, flush=True)
        winFc = []
        for ci5 in range(5):
            v = Tl(winF.t[:, :, ci5 * 512:(ci5 + 1) * 512], "winF_%d" % ci5)
            winFc.append(v)
            S.dma("pool", I("dma_start", out=v[:, :, :], in_=w_in[:, 512 + ci5 * 512:1024 + ci5 * 512].rearrange("(kc p) n -> p kc n", p=128)), writes=[v])
        S.dma("pool", I("dma_start", out=wo[:], in_=w_out.rearrange("(kc p) n -> p kc n", p=128)), writes=[wo])
        for cc in range(4):
            S.op("pool", I("tensor_scalar", out=wo[:, 4 + cc, :], in0=wo[:, 4 + cc, :], scalar1=gnw[:, cc:cc + 1], scalar2=None, op0=ALU.mult),
                 reads=[wo, gnw], writes=[wo])
        work = []
        for sg in segs:
            T = sg["T"]
            for kc0 in (0, 4):
                S.dma("sp", I("dma_start", out=sg["h2"][kc0:kc0 + 4, :, 0:64].rearrange("k p t -> p k t"),
                              in_=zerob[:, 0:256].rearrange("p (k t) -> p k t", t=64)), reads=[zerob], writes=[sg["h2"]])
                S.dma("sp", I("dma_start", out=sg["h2"][kc0:kc0 + 4, :, T + 64:T + 128].rearrange("k p t -> p k t"),
                              in_=zerob[:, 0:256].rearrange("p (k t) -> p k t", t=64)), reads=[zerob], writes=[sg["h2"]])
            for bi in range(len(sg["blocksBF"])):
                work.append((sg, bi))

        def XL_F(wi):
            sg, bi = work[wi]
            nt = sg["ntb"]
            ntt = nt // 128
            t0 = sg["blocksBF"][bi][0]
            xtm = xtm2[wi % 2]
            S.dma("sp", I("dma_start", out=xtm[:, 0:ntt, :], in_=sg["xin"][t0:t0 + nt, :].rearrange("(tt p) d -> p tt d", p=128)), writes=[xtm])
            yield

        def FE_F(wi):
            sg, bi = work[wi]
            p = wi % 2
            nt, m = sg["ntb"], sg["m"]
            ntt = nt // 128
            t0 = sg["blocksBF"][bi][0]
            xtm, scr, hT = xtm2[p], scr2[p], hT2[p]
            gl, qT, qf, qb, kT, Ktok, Vtok, VFtok, rg = gl2[p], qT2[p], qf2[p], qb2[p], kT2[p], Ktok2[p], Vtok2[p], VFtok2[p], rg2[p]
            for kc0 in (0, 4):
                S.dma("sp", I("dma_start", out=hT[:, kc0:kc0 + 4, 0:nt], in_=sg["h1"][kc0:kc0 + 4, :, t0:t0 + nt].rearrange("k p t -> p k t")),
                      reads=[sg["h1"]], writes=[hT])
            yield

            def fm_proj(col0, cc):
                pp = PP()
                wv = winFc[col0 // 512]
                fns = [I("matmul", pp[:, 0:nt], lhsT=wv[:, kc, cc * 128:(cc + 1) * 128], rhs=hT[:, kc, 0:nt],
                         start=(kc == 0), stop=(kc == 7)) for kc in range(8)]
                S.op("pe", fns, reads=[wv, hT], writes=[pp])
                return pp

            def tm_proj(col0, tt):
                pp = PP()
                wv = winFc[col0 // 512]
                fns = [I("matmul", pp[:], lhsT=hT[:, kc, tt * 128:(tt + 1) * 128], rhs=wv[:, kc, 0:512],
                         start=(kc == 0), stop=(kc == 7)) for kc in range(8)]
                S.op("pe", fns, reads=[wv, hT], writes=[pp])
                return pp

            for cc in range(4):
                pp = fm_proj(0, cc)
                S.op("act", I("activation", out=gl[:, cc, 0:nt], in_=pp[:, 0:nt], func=GELU), reads=[pp], writes=[gl])
            for h in range(4):
                pp = fm_proj(512, h)
                S.op("act", I("activation", out=qT[:, h, 0:nt], in_=pp[:, 0:nt], func=AF.Copy), reads=[pp], writes=[qT])
                S.op("dve", I("tensor_tensor", out=qf[:, h, 0:nt].rearrange("p (c i) -> p c i", i=128),
                              in0=pp[:, 0:nt].rearrange("p (c i) -> p c i", i=128),
                              in1=HF[:, h:h + 1, :].to_broadcast([128, ntt, 128]), op=ALU.mult), reads=[pp, HF], writes=[qf])
                S.op("dve", I("tensor_tensor", out=qb[:, h, 0:nt].rearrange("p (c i) -> p c i", i=128),
                              in0=pp[:, 0:nt].rearrange("p (c i) -> p c i", i=128),
                              in1=HB[:, h:h + 1, :].to_broadcast([128, ntt, 128]), op=ALU.mult), reads=[pp, HB], writes=[qb])
                yield
            for h in range(4):
                pp = fm_proj(1024, h)
                S.op("act", I("activation", out=kT[:, h, 0:nt], in_=pp[:, 0:nt], func=AF.Copy, scale=128.0 ** -0.5), reads=[pp], writes=[kT])
                yield
            for tt in range(ntt):
                pp = tm_proj(1024, tt)
                S.op("act", I("activation", out=Ktok[:, tt, :], in_=pp[:], func=AF.Copy, scale=128.0 ** -0.5), reads=[pp], writes=[Ktok])
                yield
                pp = tm_proj(1536, tt)
                S.op("act", I("activation", out=Vtok[:, tt, :], in_=pp[:], func=AF.Copy), reads=[pp], writes=[Vtok])
                S.op("dve", I("tensor_tensor", out=VFtok[:, tt, :], in0=pp[:], in1=TFm[:].rearrange("p h e -> p (h e)"), op=ALU.mult),
                     reads=[pp, TFm], writes=[VFtok])
                yield
                pp = tm_proj(2048, tt)
                S.op("act", I("activation", out=rg[:, tt, :], in_=pp[:], func=AF.Tanh, scale=0.5), reads=[pp], writes=[rg])
                S.op("dve", I("scalar_tensor_tensor", out=rg[:, tt, :], in0=rg[:, tt, :], scalar=1.0, in1=pp[:], op0=ALU.add, op1=ALU.mult),
                     reads=[rg, pp], writes=[rg])
                yield

        def BE_lru(wi):
            sg, bi = work[wi]
            p = wi % 2
            nt, m = sg["ntb"], sg["m"]
            ntt = nt // 128
            blocks = sg["blocksBF"]
            t0 = blocks[bi][0]
            xtm = xtm2[p]
            yT = yT2[p]
            gl, qT, qf, qb, kT, Ktok, Vtok, VFtok, rg = gl2[p], qT2[p], qf2[p], qb2[p], kT2[p], Ktok2[p], Vtok2[p], VFtok2[p], rg2[p]
            if bi == 0:
                if sg["kind"] == "s":
                    S.op("pool", I("tensor_copy", out=hst[:], in_=h0[:]), reads=[h0], writes=[hst])
                else:
                    S.op("pool", I("memset", hst[:], 0.0), writes=[hst])
            ret = []
            yield from lru_dir(sg, t0, nt, 0, L, hst, ret)
            hh = ret[0]
            if sg["kind"] == "p" and t0 == 0:
                ob2 = obuf(nlf, "nlf")
                outs_toks.append(S.dma("sp", I("dma_start", out=nlf[sg["idx"]].rearrange("(n p) -> p n", p=128), in_=hh[:, :, 0]),
                                       reads=[hh], writes=[ob2]))
            S.dma("sp", I("dma_start", out=hbw[:, :, 0:nt], in_=sg["hb"][:, :, t0:t0 + nt].rearrange("c p t -> p c t")),
                  reads=[sg["hb"]], writes=[hbw])
            yield
            S.op("pool", I("tensor_tensor", out=hbw[:, :, 0:nt], in0=hbw[:, :, 0:nt], in1=hh[:, :, 0:nt], op=ALU.add), reads=[hbw, hh], writes=[hbw])
            yield
            S.op("dve", I("tensor_tensor", out=yT[:, 0:4, 0:nt], in0=hbw[:, :, 0:nt], in1=gl[:, :, 0:nt], op=ALU.mult), reads=[hbw, gl], writes=[yT])
            yield
        def BE_ret(wi):
            sg, bi = work[wi]
            p = wi % 2
            nt, m = sg["ntb"], sg["m"]
            ntt = nt // 128
            blocks = sg["blocksBF"]
            t0 = blocks[bi][0]
            xtm = xtm2[p]
            yT = yT2[p]
            gl, qT, qf, qb, kT, Ktok, Vtok, VFtok, rg = gl2[p], qT2[p], qf2[p], qb2[p], kT2[p], Ktok2[p], Vtok2[p], VFtok2[p], rg2[p]
            if bi == 0:
                if sg["kind"] == "s":
                    S.dma("sp", I("dma_start", out=Sf[:], in_=st_rf.rearrange("h d e -> d h e")), writes=[Sf])
                else:
                    S.op("pool", I("memset", Sf[:], 0.0), writes=[Sf])
            for tt in range(ntt):
                ci = t0 // 128 + tt
                q2 = ci % 2
                Sbb, Sfb, PTs, ytok, otmp = Sbb2[q2], Sfb2[q2], PTs2[q2], ytok2[q2], otmp2[q2]
                tk = slice(tt * 128, (tt + 1) * 128)
                S.dma("sp", I("dma_start", out=Sbb[:], in_=sg["Sb"][ci]), reads=[sg["Sb"]], writes=[Sbb])
                yield
                S.op("act", I("activation", out=Sfb[:], in_=Sf[:].rearrange("p h e -> p (h e)"), func=AF.Copy), reads=[Sf], writes=[Sfb])
                yield
                ps_ = PP()
                fns = [I("matmul", ps_[:, h * 128:(h + 1) * 128], lhsT=kT[:, h, tk], rhs=qT[:, h, tk], start=True, stop=True) for h in range(4)]
                S.op("pe", fns, reads=[kT, qT], writes=[ps_])
                S.op("dve", I("tensor_tensor", out=PTs[:], in0=ps_[:], in1=Dm[:].rearrange("p h i -> p (h i)"), op=ALU.mult), reads=[ps_, Dm], writes=[PTs])
                yield
                po_ = PP()
                fns = []
                for h in range(4):
                    hs = slice(h * 128, (h + 1) * 128)
                    fns.append(I("matmul", po_[:, hs], lhsT=PTs[:, hs], rhs=Vtok[:, tt, hs], start=True, stop=False))
                    fns.append(I("matmul", po_[:, hs], lhsT=qf[:, h, tk], rhs=Sfb[:, hs], start=False, stop=False))
                    fns.append(I("matmul", po_[:, hs], lhsT=qb[:, h, tk], rhs=Sbb[:, hs], start=False, stop=True))
                S.op("pe", fns, reads=[PTs, Vtok, qf, qb, Sfb, Sbb], writes=[po_])
                S.op("act", I("activation", out=otmp[:], in_=po_[:], func=AF.Copy), reads=[po_], writes=[otmp])
                yield
                pkv = PP()
                fns = [I("matmul", pkv[:, h * 128:(h + 1) * 128], lhsT=Ktok[:, tt, h * 128:(h + 1) * 128],
                         rhs=VFtok[:, tt, h * 128:(h + 1) * 128], start=True, stop=True) for h in range(4)]
                S.op("pe", fns, reads=[Ktok, VFtok], writes=[pkv])
                for h in range(4):
                    S.op("dve", I("scalar_tensor_tensor", out=Sf[:, h, :], in0=Sf[:, h, :], scalar=g128[:, h:h + 1],
                                  in1=pkv[:, h * 128:(h + 1) * 128], op0=ALU.mult, op1=ALU.add), reads=[Sf, g128, pkv], writes=[Sf])
                yield
                for h in range(4):
                    S.op("dve", I("bn_stats", out=bst[:, q2, h, :], in_=otmp[:, h * 128:(h + 1) * 128]), reads=[otmp], writes=[bst])
                    yield
                for h in range(4):
                    S.op("dve", I("bn_aggr", out=bag[:, q2, h, :], in_=bst[:, q2, h, :]), reads=[bst], writes=[bag])
                    yield
                S.op("pool", I("tensor_scalar", out=grs[:, q2, :], in0=bag[:, q2, :, 1], scalar1=EPS, scalar2=None, op0=ALU.add), reads=[bag], writes=[grs])
                S.op("pool", I("tensor_tensor", out=grs[:, q2, :], in0=grs[:, q2, :], in1=mhalf[:, 0:4], op=ALU.pow), reads=[grs, mhalf], writes=[grs])
                yield
                for h in range(4):
                    S.op("dve", I("tensor_scalar", out=otmp[:, h * 128:(h + 1) * 128], in0=otmp[:, h * 128:(h + 1) * 128],
                                  scalar1=bag[:, q2, h, 0:1], scalar2=grs[:, q2, h:h + 1], op0=ALU.subtract, op1=ALU.mult),
                         reads=[otmp, bag, grs], writes=[otmp])
                    yield
                S.op("pool", I("tensor_tensor", out=ytok[:], in0=otmp[:], in1=rg[:, tt, :], op=ALU.mult), reads=[otmp, rg], writes=[ytok])
                yield
                pt = PT[tt % 2]
                fns = [I("transpose", out=pt[:, h * 128:(h + 1) * 128], in_=ytok[:, h * 128:(h + 1) * 128], identity=ident[:]) for h in range(4)]
                S.op("pe", fns, reads=[ytok, ident], writes=[pt])
                S.op("act", I("activation", out=yT[:, 4:8, tk], in_=pt[:, 0:512].rearrange("p (h i) -> p h i", i=128), func=AF.Copy),
                     reads=[pt], writes=[yT])
                yield
            if bi == len(blocks) - 1 and sg["kind"] == "p":
                i = sg["idx"]
                ob = obuf(nrf, "nrf")
                outs_toks.append(S.dma("sp", I("dma_start", out=nrf[i].rearrange("h d e -> d h e"), in_=Sf[:]), reads=[Sf], writes=[ob]))

            yield

        def BE_out(wi):
            sg, bi = work[wi]
            p = wi % 2
            nt, m = sg["ntb"], sg["m"]
            ntt = nt // 128
            blocks = sg["blocksBF"]
            t0 = blocks[bi][0]
            xtm = xtm2[p]
            yT = yT2[p]
            gl, qT, qf, qb, kT, Ktok, Vtok, VFtok, rg = gl2[p], qT2[p], qf2[p], qb2[p], kT2[p], Ktok2[p], Vtok2[p], VFtok2[p], rg2[p]
            k2 = 0
            for tt in range(ntt):
                for nh in range(2):
                    wtmp = wtmp2[k2 % 2]
                    k2 += 1
                    pp = PP()
                    fns = [I("matmul", pp[:], lhsT=yT[:, kc, tt * 128:(tt + 1) * 128], rhs=wo[:, kc, nh * 512:(nh + 1) * 512],
                             start=(kc == 0), stop=(kc == 7)) for kc in range(8)]
                    S.op("pe", fns, reads=[yT, wo], writes=[pp])
                    S.op("dve", I("tensor_tensor", out=wtmp[:], in0=pp[:], in1=g1bc[m][:, nh * 512:(nh + 1) * 512], op=ALU.mult),
                         reads=[pp, g1bc[m]], writes=[wtmp])
                    yield
                    S.op("pool", I("tensor_tensor", out=xtm[:, tt, nh * 512:(nh + 1) * 512], in0=xtm[:, tt, nh * 512:(nh + 1) * 512],
                                   in1=wtmp[:], op=ALU.add), reads=[xtm, wtmp], writes=[xtm])
                    yield
            S.dma("sp", I("dma_start", out=sg["x1"][t0:t0 + nt, :].rearrange("(tt p) d -> p tt d", p=128), in_=xtm[:, 0:ntt, :]),
                  reads=[xtm], writes=[sg["x1"]])
            yield
            yield from norm_mod_T(xtm, ntt, m, 2, h2T, scrBE)
            for kc0 in (0, 4):
                S.dma("sp", I("dma_start", out=sg["h2"][kc0:kc0 + 4, :, 64 + t0:64 + t0 + nt].rearrange("k p t -> p k t"),
                              in_=h2T[:, kc0:kc0 + 4, 0:nt]), reads=[h2T], writes=[sg["h2"]])
                yield
            yield

        run_merged([XL_F(0), FE_F(0)])
        nW = len(work)
        for wi in range(nW + 1):
            gens, after = [], {}
            if wi >= 1:
                gens.append(BE_out(wi - 1))
            if wi < nW:
                gens.append(BE_lru(wi))
                gens.append(BE_ret(wi))
                if wi + 1 < nW:
                    gens.append(FE_F(wi + 1))
                    gens.append(XL_F(wi + 1))
                    if wi >= 1:
                        after[len(gens) - 1] = 0
            run_merged(gens, after=after)
        S.barrier()
        S.emit()
    if KSTOP == "F":
        esBF.close()
        es.close()
        return nc

    esBF.close()
    with ExitStack() as es2:
        def sb2(name, shape, dtype=F32):
            return Tl(es2.enter_context(nc.sbuf_tensor(name, list(shape), dtype)), name)
        g2bc = [sb2("g2bcG%d" % m, [128, D]) for m in range(2)]
        fnwbc = sb2("fnwbc", [128, D])
        S.dma("sp", I("dma_start", out=fnwbc[:], in_=fnw.partition_broadcast(128)), writes=[fnwbc])
        for m in range(2):
            S.dma("sp", I("dma_start", out=g2bc[m][:], in_=g2d[m].partition_broadcast(128)), reads=[g2d], writes=[g2bc[m]])
        wg = sb2("wg", [128, 8, FH], BF16)
        wu = sb2("wu", [128, 8, FH], BF16)
        wd = sb2("wd", [128, NCC, D], BF16)
        h2w = sb2("h2w", [128, 8, 640], BF16)
        gp = [sb2("gp%d" % i, [128, 660]) for i in range(2)]
        cv = [sb2("cv%d" % i, [128, 512]) for i in range(2)]
        actT = sb2("actT", [128, NCC, 512], BF16)
        x1t = [sb2("x1t%d" % i, [128, D]) for i in range(2)]
        wtmps = [sb2("wtmpG%d" % i, [128, 512]) for i in range(2)]
        junk = sb2("junkG", [128, D], BF16)
        ssqs = sb2("ssqG", [128, 4])
        rstds = sb2("rstdG", [128, 4])
        if KDBG:
            print("pass G sbuf remaining", nc.sbuf_bytes_remaining, flush=True)
        ccb = [0, 6, 12, 17, 22]
        wgc, wuc = [], []
        for k4 in range(4):
            for (lst, w_, src_, nm) in ((wgc, wg, w_gate, "wg"), (wuc, wu, w_up, "wu")):
                c_lo, c_hi = ccb[k4] * 128, ccb[k4 + 1] * 128
                v = Tl(w_.t[:, :, c_lo:c_hi], "%s_c%d" % (nm, k4))
                lst.append(v)
                S.dma("pool", I("dma_start", out=v[:, :, :], in_=src_[:, c_lo:c_hi].rearrange("(kc p) n -> p kc n", p=128)), writes=[v])

        def wsel(lst, cc):
            k4 = max(k for k in range(4) if ccb[k] <= cc)
            lo = (cc - ccb[k4]) * 128
            return lst[k4], slice(lo, lo + 128)

        for c0 in (0, 11):
            S.dma("pool", I("dma_start", out=wd[:, c0:c0 + 11, :], in_=w_down[c0 * 128:(c0 + 11) * 128, :].rearrange("(kc p) n -> p kc n", p=128)),
                  writes=[wd])
        for g_ in gp:
            S.op("pool", I("memset", g_[:], 0.0), writes=[g_])
        blk = 0
        gblocks = [(sg_["name"], t0_) for sg_ in segs for (t0_, _) in sg_["blocks"]]
        segby = {sg_["name"]: sg_ for sg_ in segs}

        def load_h2w(gi):
            sgx = segby[gblocks[gi][0]]
            t0x = gblocks[gi][1]
            sx = sgx["kind"] == "s"
            Wx = sgx["nt"] + 128 if sx else sgx["nt"]
            c0x = t0x if sx else 64
            for kc0 in (0, 4):
                S.dma("sp", I("dma_start", out=h2w[:, kc0:kc0 + 4, 0:Wx], in_=sgx["h2"][kc0:kc0 + 4, :, c0x:c0x + Wx].rearrange("k p t -> p k t")),
                      reads=[sgx["h2"]], writes=[h2w])

        for sg in segs:
            T, nt, m = sg["T"], sg["nt"], sg["m"]
            ntt = nt // 128
            samp = sg["kind"] == "s"
            if not samp:
                for g_ in gp:
                    S.op("pool", I("memset", g_[:], 0.0), writes=[g_])
            for (t0, _) in sg["blocks"]:
                gi_ = gblocks.index((sg["name"], t0))
                if gi_ == 0:
                    load_h2w(0)
                own = 64 if samp else 0

                def G_cc(cc, slot, nt=nt, samp=samp, own=own):
                    g_ = gp[slot]
                    cv_ = cv[slot]
                    cs = slice(cc * 128, (cc + 1) * 128)
                    if samp:
                        p1, p2 = PP(), PP()
                        wgv, wcs = wsel(wgc, cc)
                        fns = [I("matmul", p1[:], lhsT=wgv[:, kc, wcs], rhs=h2w[:, kc, 0:512], start=(kc == 0), stop=(kc == 7)) for kc in range(8)]
                        S.op("pe", fns, reads=[wgv, h2w], writes=[p1])
                        fns = [I("matmul", p2[:, 0:128], lhsT=wgv[:, kc, wcs], rhs=h2w[:, kc, 512:640], start=(kc == 0), stop=(kc == 7)) for kc in range(8)]
                        S.op("pe", fns, reads=[wgv, h2w], writes=[p2])
                        gv = g_[:, 0:660].rearrange("p (r c) -> p r c", c=66)
                        S.op("act", I("activation", out=gv[:, 0:8, 1:65], in_=p1[:].rearrange("p (r c) -> p r c", c=64), func=AF.Copy),
                             reads=[p1], writes=[g_])
                        S.op("act", I("activation", out=gv[:, 8:10, 1:65], in_=p2[:, 0:128].rearrange("p (r c) -> p r c", c=64), func=AF.Copy),
                             reads=[p2], writes=[g_])
                        taps = [(dy, dx) for dy in (-1, 0, 1) for dx in (-1, 0, 1)]
                        cvv = cv_[:, 0:512].rearrange("p (r c) -> p r c", c=64)

                        def gsl(dy, dx):
                            return gv[:, 1 + dy:9 + dy, 1 + dx:65 + dx]
                    else:
                        p1 = PP()
                        wgv, wcs = wsel(wgc, cc)
                        fns = [I("matmul", p1[:, 0:nt], lhsT=wgv[:, kc, wcs], rhs=h2w[:, kc, 0:nt], start=(kc == 0), stop=(kc == 7)) for kc in range(8)]
                        S.op("pe", fns, reads=[wgv, h2w], writes=[p1])
                        gv = g_[:, 0:nt + 2]
                        S.op("act", I("activation", out=gv[:, 1:nt + 1], in_=p1[:, 0:nt], func=AF.Copy), reads=[p1], writes=[g_])
                        taps = [(0, dx) for dx in (-1, 0, 1)]
                        cvv = cv_[:, 0:nt]

                        def gsl(dy, dx):
                            return gv[:, 1 + dx:1 + dx + nt]
                    yield
                    pu = PP()
                    wuv, wcs2 = wsel(wuc, cc)
                    fns = [I("matmul", pu[:, 0:nt], lhsT=wuv[:, kc, wcs2], rhs=h2w[:, kc, own:own + nt], start=(kc == 0), stop=(kc == 7)) for kc in range(8)]
                    S.op("pe", fns, reads=[wuv, h2w], writes=[pu])
                    for ti, (dy, dx) in enumerate(taps):
                        tcol = ((dy + 1) * 3 + (dx + 1)) * NCC + cc
                        if ti == 0:
                            S.op("act", I("activation", out=cvv, in_=gsl(dy, dx), func=AF.Identity, scale=fcw[:, tcol:tcol + 1], bias=fcb[:, cc:cc + 1]),
                                 reads=[g_, fcw, fcb], writes=[cv_])
                        elif cc in POOL_CCS:
                            tmpw = wtmps[slot]
                            tmpv = tmpw[:, 0:512].rearrange("p (r c) -> p r c", c=64) if samp else tmpw[:, 0:nt]
                            S.op("pool", I("tensor_scalar", out=tmpv, in0=gsl(dy, dx), scalar1=fcw[:, tcol:tcol + 1], scalar2=None, op0=ALU.mult),
                                 reads=[g_, fcw], writes=[tmpw])
                            S.op("pool", I("tensor_tensor", out=cvv, in0=cvv, in1=tmpv, op=ALU.add), reads=[cv_, tmpw], writes=[cv_])
                        else:
                            S.op("dve", I("scalar_tensor_tensor",
                                out=cvv, in0=gsl(dy, dx), scalar=fcw[:, tcol:tcol + 1], in1=cvv, op0=ALU.mult, op1=ALU.add),
                                reads=[g_, fcw, cv_], writes=[cv_])
                        yield
                    S.op("act", I("activation", out=cv_[:, 0:nt], in_=cv_[:, 0:nt], func=GELU), reads=[cv_], writes=[cv_])
                    yield
                    S.op("dve", I("tensor_tensor", out=actT[:, cc, 0:nt], in0=pu[:, 0:nt], in1=cv_[:, 0:nt], op=ALU.mult),
                         reads=[pu, cv_], writes=[actT])
                    yield

                for cc in range(0, NCC, 2):
                    run_merged([G_cc(cc, 0), G_cc(cc + 1, 1)])
                if gi_ + 1 < len(gblocks):
                    load_h2w(gi_ + 1)
                def G_down(tt, slot, sg=sg, t0=t0, m=m):
                    xt = x1t[slot]
                    wtmp = wtmps[slot]
                    S.dma("sp", I("dma_start", out=xt[:], in_=sg["x1"][t0 + tt * 128:t0 + (tt + 1) * 128, :]), reads=[sg["x1"]], writes=[xt])
                    yield
                    for nh in range(2):
                        pp = PP()
                        fns = [I("matmul", pp[:], lhsT=actT[:, cc, tt * 128:(tt + 1) * 128], rhs=wd[:, cc, nh * 512:(nh + 1) * 512],
                                 start=(cc == 0), stop=(cc == NCC - 1)) for cc in range(NCC)]
                        S.op("pe", fns, reads=[actT, wd], writes=[pp])
                        S.op("dve", I("tensor_tensor", out=wtmp[:], in0=pp[:], in1=g2bc[m][:, nh * 512:(nh + 1) * 512], op=ALU.mult),
                             reads=[pp, g2bc[m]], writes=[wtmp])
                        yield
                        S.op("pool", I("tensor_tensor", out=xt[:, nh * 512:(nh + 1) * 512], in0=xt[:, nh * 512:(nh + 1) * 512], in1=wtmp[:], op=ALU.add),
                             reads=[xt, wtmp], writes=[xt])
                        yield
                    S.op("act", I("activation", out=junk[:], in_=xt[:], func=AF.Square, accum_out=ssqs[:, slot:slot + 1]), reads=[xt], writes=[junk, ssqs])
                    yield
                    S.op("dve", I("tensor_scalar", out=rstds[:, slot:slot + 1], in0=ssqs[:, slot:slot + 1], scalar1=1.0 / D, scalar2=EPS, op0=ALU.mult, op1=ALU.add),
                         reads=[ssqs], writes=[rstds])
                    S.op("pool", I("tensor_tensor", out=rstds[:, slot:slot + 1], in0=rstds[:, slot:slot + 1], in1=mhalf[:, 0:1], op=ALU.pow),
                         reads=[rstds, mhalf], writes=[rstds])
                    yield
                    S.op("dve", I("scalar_tensor_tensor", out=xt[:], in0=xt[:], scalar=rstds[:, slot:slot + 1], in1=fnwbc[:], op0=ALU.mult, op1=ALU.mult),
                         reads=[xt, rstds, fnwbc], writes=[xt])
                    yb = sg["yb"][slot % 2]
                    S.dma("sp", I("dma_start", out=sg["yout"][t0 + tt * 128:t0 + (tt + 1) * 128, :], in_=xt[:]), reads=[xt], writes=[yb])
                    yield

                for tt0 in range(0, ntt, 2):
                    run_merged([G_down(tt0 + k, k) for k in range(min(2, ntt - tt0))])
        for sg in segs:
            for yb in sg["yb"]:
                outs_toks.append(yb.b.wtok)
        waits = []
        for t in outs_toks:
            S._need("sp", t, waits)
        S.ops["sp"].append(("op", waits, [], None))
        S.barrier()
        S.emit()

    es.close()
    return nc


_NC_CACHE = {}


def kernel(**inputs):
    x_prompt = np.ascontiguousarray(inputs["x_prompt"], dtype=np.float32)
    x_sample = np.ascontiguousarray(inputs["x_sample"], dtype=np.float32)
    BP, TP, _ = x_prompt.shape
    BS, TS, _ = x_sample.shape
    ncores = 8
    NP = BP // ncores
    assert BS == ncores
    key = (TS, NP, TP)
    if key not in _NC_CACHE:
        _NC_CACHE[key] = build_nc(TS, NP, TP)
    nc = _NC_CACHE[key]

    def f(n):
        return np.ascontiguousarray(inputs[n], dtype=np.float32)

    shared = {
        "c_ctx": f("c_ctx"), "norm1_w": f("norm1_w")[0], "w_mod": f("w_mod")[0], "b_mod": f("b_mod")[0], "w_in": f("w_in")[0],
        "lru_conv_w": f("lru_conv_w")[0], "lru_conv_b": f("lru_conv_b")[0],
        "lru_wa_fw": f("lru_wa_fw")[0], "lru_wx_fw": f("lru_wx_fw")[0], "lru_wa_bw": f("lru_wa_bw")[0], "lru_wx_bw": f("lru_wx_bw")[0],
        "lru_ba_fw": f("lru_ba_fw")[0], "lru_bx_fw": f("lru_bx_fw")[0], "lru_ba_bw": f("lru_ba_bw")[0], "lru_bx_bw": f("lru_bx_bw")[0],
        "lru_lambda_fw": f("lru_lambda_fw")[0], "lru_lambda_bw": f("lru_lambda_bw")[0],
        "ret_decay_fw": f("ret_decay_fw")[0], "ret_decay_bw": f("ret_decay_bw")[0], "ret_gn_w": f("ret_gn_w")[0],
        "w_out": f("w_out")[0], "norm2_w": f("norm2_w")[0], "ffn_w_gate": f("ffn_w_gate")[0], "ffn_w_up": f("ffn_w_up")[0],
        "ffn_conv_w": f("ffn_conv_w")[0].reshape(9, FH), "ffn_conv_b": f("ffn_conv_b")[0], "ffn_w_down": f("ffn_w_down")[0],
        "final_norm_w": f("final_norm_w"),
    }
    shared = {k: np.ascontiguousarray(v) for k, v in shared.items()}
    in_maps = []
    for ci in range(ncores):
        mp = dict(shared)
        mp["xs"] = x_sample[ci]
        mp["xp"] = np.ascontiguousarray(x_prompt[ci * NP:(ci + 1) * NP].reshape(NP * TP, D))
        mp["st_lf"] = np.ascontiguousarray(f("state_lru_fw")[ci, 0])
        mp["st_lb"] = np.ascontiguousarray(f("state_lru_bw")[ci, 0])
        mp["st_rf"] = np.ascontiguousarray(f("state_ret_fw")[ci, 0])
        mp["st_rb"] = np.ascontiguousarray(f("state_ret_bw")[ci, 0])
        mp["c"] = np.ascontiguousarray(f("c")[ci])
        in_maps.append(mp)
    res = run_bass_kernel_spmd(nc, in_maps, core_ids=list(range(ncores)))
    R = res.results
    y_prompt = np.concatenate([r["yp"].reshape(NP, TP, D) for r in R], axis=0)
    y_sample = np.stack([r["ys"] for r in R], axis=0)
    new_lf = np.concatenate([r["nlf"].reshape(NP, 1, LW) for r in R], axis=0)
    new_lb = np.concatenate([r["nlb"].reshape(NP, 1, LW) for r in R], axis=0)
    new_rf = np.concatenate([r["nrf"].reshape(NP, 1, 4, 128, 128) for r in R], axis=0)
    new_rb = np.concatenate([r["nrb"].reshape(NP, 1, 4, 128, 128) for r in R], axis=0)
    return (y_prompt.astype(np.float32), y_sample.astype(np.float32), new_lf.astype(np.float32), new_lb.astype(np.float32),
            new_rf.astype(np.float32), new_rb.astype(np.float32))
```
